# Optimizing a Trainium2 kernel written in Bass

```python
import math
import jax, jax.numpy as jnp
from jax import lax
import numpy as np

D_MODEL = 1024
BATCH = 16
SEQ = 2048
DEPTH = 4

N_A_LAYERS = DEPTH // 2
N_B_LAYERS = DEPTH - N_A_LAYERS
PLE_DIM = 256
SSM_DIM = D_MODEL
SSM_GROUP_DIM = 16
SSM_GROUPS = SSM_DIM // SSM_GROUP_DIM
SSM_STATE = 64
SSM_CHUNK = 128
N_HEADS = 8
HEAD_DIM = 64
Q_BLOCK = 128
REL_BUCKETS = 32
REL_MAX_EXACT = REL_BUCKETS // 2
REL_MAX_DIST = 128
D_FF = 2816
CONV_WIDTH = 3
NEG_INF = -1e30
EPS = 1e-6

kernel_name = "yoco_s5_diffattn_convffn_hybrid"


def rms_norm(x, g):
    xf = x.astype(jnp.float32)
    y = xf * lax.rsqrt(jnp.mean(xf * xf, axis=-1, keepdims=True) + EPS)
    return (y * g.astype(jnp.float32)).astype(x.dtype)


def _ssm_combine(e_i, e_j):
    ar_i, ai_i, br_i, bi_i = e_i
    ar_j, ai_j, br_j, bi_j = e_j
    return (ar_j * ar_i - ai_j * ai_i,
            ar_j * ai_i + ai_j * ar_i,
            ar_j * br_i - ai_j * bi_i + br_j,
            ar_j * bi_i + ai_j * br_i + bi_j)


def s5_mixer(h, w_in, log_dt, lam_re, lam_im, b_re, b_im, c_re, c_im, d_skip, w_glu):
    f32 = jnp.float32
    bsz, seqlen, _ = h.shape
    u = h @ w_in
    dt = jnp.exp(log_dt.astype(f32))[:, None]
    lr = lam_re.astype(f32)
    li = lam_im.astype(f32)
    mag = jnp.exp(lr * dt)
    ab_r = mag * jnp.cos(li * dt)
    ab_i = mag * jnp.sin(li * dt)
    den = lr * lr + li * li
    f_r = ((ab_r - 1.0) * lr + ab_i * li) / den
    f_i = (ab_i * lr - (ab_r - 1.0) * li) / den
    br = b_re.astype(f32)
    bi = b_im.astype(f32)
    bb_r = f_r[..., None] * br - f_i[..., None] * bi
    bb_i = f_r[..., None] * bi + f_i[..., None] * br
    cr = c_re.astype(f32)
    ci = c_im.astype(f32)
    n_chunks = seqlen // SSM_CHUNK
    uc = u.astype(f32).reshape(bsz, n_chunks, SSM_CHUNK, SSM_GROUPS, SSM_GROUP_DIM)
    uc = uc.transpose(1, 2, 0, 3, 4)
    a_shape = (SSM_CHUNK, bsz, SSM_GROUPS, SSM_STATE)
    a_r = jnp.broadcast_to(ab_r, a_shape)
    a_i = jnp.broadcast_to(ab_i, a_shape)

    def chunk_step(carry, u_chunk):
        x_r0, x_i0 = carry
        bu_r = jnp.einsum("tbgc,gpc->tbgp", u_chunk, bb_r)
        bu_i = jnp.einsum("tbgc,gpc->tbgp", u_chunk, bb_i)
        pw_r, pw_i, s_r, s_i = lax.associative_scan(
            _ssm_combine, (a_r, a_i, bu_r, bu_i), axis=0)
        x_r = s_r + pw_r * x_r0 - pw_i * x_i0
        x_i = s_i + pw_r * x_i0 + pw_i * x_r0
        y = (jnp.einsum("tbgp,gcp->tbgc", x_r, cr)
             - jnp.einsum("tbgp,gcp->tbgc", x_i, ci))
        return (x_r[-1], x_i[-1]), y

    zeros = jnp.zeros((bsz, SSM_GROUPS, SSM_STATE), f32)
    _, ys = lax.scan(chunk_step, (zeros, zeros), uc)
    y = ys.transpose(2, 0, 1, 3, 4).reshape(bsz, seqlen, SSM_DIM)
    y = y + d_skip.astype(f32) * u.astype(f32)
    g = jax.nn.gelu(y).astype(h.dtype)
    val, gate = jnp.split(g @ w_glu, 2, axis=-1)
    return val * jax.nn.sigmoid(gate)


def t5_causal_bucket(rel):
    n = jnp.maximum(-rel, 0)
    nf = jnp.maximum(n, REL_MAX_EXACT).astype(jnp.float32)
    large = REL_MAX_EXACT + (jnp.log(nf / REL_MAX_EXACT)
                             / math.log(REL_MAX_DIST / REL_MAX_EXACT)
                             * (REL_BUCKETS - REL_MAX_EXACT)).astype(jnp.int32)
    large = jnp.minimum(large, REL_BUCKETS - 1)
    return jnp.where(n < REL_MAX_EXACT, n, large)


def shared_kv(x, kv_norm, kv_w, k_norm):
    bsz, seqlen, _ = x.shape
    kv = rms_norm(x, kv_norm) @ kv_w
    nk = 2 * N_HEADS * HEAD_DIM
    k = rms_norm(kv[..., :nk].reshape(bsz, seqlen, N_HEADS, 2, HEAD_DIM), k_norm)
    v = kv[..., nk:].reshape(bsz, seqlen, N_HEADS, 2 * HEAD_DIM)
    return k, v


def diff_attention(h, k, v, rel_bias, w_q, q_norm, lq1, lk1, lq2, lk2, subln, w_o, lam_init):
    f32 = jnp.float32
    bsz, seqlen, _ = h.shape
    q = rms_norm((h @ w_q).reshape(bsz, seqlen, N_HEADS, 2, HEAD_DIM), q_norm)
    lam = (jnp.exp(jnp.sum(lq1.astype(f32) * lk1.astype(f32)))
           - jnp.exp(jnp.sum(lq2.astype(f32) * lk2.astype(f32))) + lam_init)
    scale = HEAD_DIM ** -0.5
    n_blocks = seqlen // Q_BLOCK
    qb = q.reshape(bsz, n_blocks, Q_BLOCK, N_HEADS, 2, HEAD_DIM).transpose(1, 0, 2, 3, 4, 5)
    k_pos = jnp.arange(seqlen)

    def block(args):
        idx, qc = args
        q_pos = idx * Q_BLOCK + jnp.arange(Q_BLOCK)
        rel = k_pos[None, :] - q_pos[:, None]
        bias = rel_bias[t5_causal_bucket(rel)].astype(f32).transpose(2, 0, 1)
        s = jnp.einsum("bqhcd,bkhcd->bhcqk", qc, k).astype(f32) * scale
        s = s + bias[None, :, None]
        s = jnp.where(rel <= 0, s, NEG_INF)
        pm = jax.nn.softmax(s, axis=-1)
        attn = pm[:, :, 0] - lam * pm[:, :, 1]
        return jnp.einsum("bhqk,bkhe->bqhe", attn.astype(v.dtype), v)

    o = lax.map(block, (jnp.arange(n_blocks), qb))
    o = o.transpose(1, 0, 2, 3, 4).reshape(bsz, seqlen, N_HEADS, 2 * HEAD_DIM)
    o = rms_norm(o, subln) * (1.0 - lam_init)
    return o.reshape(bsz, seqlen, N_HEADS * 2 * HEAD_DIM) @ w_o


def conv_ffn(h, w_up, conv_w, conv_b, w_down):
    up = h @ w_up
    seqlen = up.shape[1]
    padded = jnp.pad(up, ((0, 0), (CONV_WIDTH - 1, 0), (0, 0)))
    c = conv_b
    for j in range(CONV_WIDTH):
        c = c + conv_w[j] * padded[:, j:j + seqlen]
    g, val = jnp.split(c, 2, axis=-1)
    return (jax.nn.gelu(g) * val) @ w_down


def setup_inputs(seed: int = 0) -> dict:
    key = jax.random.key(seed)
    ks = iter(jax.random.split(key, 40))
    f32 = jnp.float32

    def nrm(shape, scale):
        return jax.random.normal(next(ks), shape, f32) * scale

    def gain(shape):
        return 1.0 + nrm(shape, 0.05)

    na, nb = N_A_LAYERS, N_B_LAYERS
    lam_im = jnp.pi * jnp.arange(SSM_STATE, dtype=f32)
    return {
        "x": nrm((BATCH, SEQ, D_MODEL), 1.0),
        "p": nrm((DEPTH, BATCH, SEQ, PLE_DIM), 1.0),
        "ssm_norm": gain((na, D_MODEL)),
        "ssm_w_in": nrm((na, D_MODEL, SSM_DIM), D_MODEL ** -0.5),
        "ssm_log_dt": jax.random.uniform(next(ks), (na, SSM_GROUPS), f32,
                                         math.log(1e-3), math.log(1e-1)),
        "ssm_lambda_re": -0.5 + nrm((na, SSM_GROUPS, SSM_STATE), 0.01),
        "ssm_lambda_im": lam_im + nrm((na, SSM_GROUPS, SSM_STATE), 0.01),
        "ssm_b_re": nrm((na, SSM_GROUPS, SSM_STATE, SSM_GROUP_DIM), SSM_GROUP_DIM ** -0.5),
        "ssm_b_im": nrm((na, SSM_GROUPS, SSM_STATE, SSM_GROUP_DIM), SSM_GROUP_DIM ** -0.5),
        "ssm_c_re": nrm((na, SSM_GROUPS, SSM_GROUP_DIM, SSM_STATE), SSM_STATE ** -0.5),
        "ssm_c_im": nrm((na, SSM_GROUPS, SSM_GROUP_DIM, SSM_STATE), SSM_STATE ** -0.5),
        "ssm_d": nrm((na, SSM_DIM), 0.5),
        "ssm_w_glu": nrm((na, SSM_DIM, 2 * D_MODEL), SSM_DIM ** -0.5),
        "kv_norm": gain((D_MODEL,)),
        "kv_w": nrm((D_MODEL, 2 * N_HEADS * HEAD_DIM + N_HEADS * 2 * HEAD_DIM), D_MODEL ** -0.5),
        "k_norm": gain((HEAD_DIM,)),
        "attn_norm": gain((nb, D_MODEL)),
        "attn_w_q": nrm((nb, D_MODEL, 2 * N_HEADS * HEAD_DIM), D_MODEL ** -0.5),
        "q_norm": gain((nb, HEAD_DIM)),
        "lambda_q1": nrm((nb, HEAD_DIM), 0.1),
        "lambda_k1": nrm((nb, HEAD_DIM), 0.1),
        "lambda_q2": nrm((nb, HEAD_DIM), 0.1),
        "lambda_k2": nrm((nb, HEAD_DIM), 0.1),
        "subln": gain((nb, 2 * HEAD_DIM)),
        "attn_w_o": nrm((nb, N_HEADS * 2 * HEAD_DIM, D_MODEL), (N_HEADS * 2 * HEAD_DIM) ** -0.5),
        "rel_bias": nrm((REL_BUCKETS, N_HEADS), 0.5),
        "ffn_norm": gain((DEPTH, D_MODEL)),
        "ffn_w_up": nrm((DEPTH, D_MODEL, 2 * D_FF), D_MODEL ** -0.5),
        "ffn_conv_w": nrm((DEPTH, CONV_WIDTH, 2 * D_FF), CONV_WIDTH ** -0.5),
        "ffn_conv_b": nrm((DEPTH, 2 * D_FF), 0.02),
        "ffn_w_down": nrm((DEPTH, D_FF, D_MODEL), D_FF ** -0.5),
        "ple_norm": gain((DEPTH, D_MODEL)),
        "ple_w_gate": nrm((DEPTH, D_MODEL, D_MODEL), D_MODEL ** -0.5),
        "ple_w_proj": nrm((DEPTH, PLE_DIM, D_MODEL), PLE_DIM ** -0.5),
    }


def reference(x, p, ssm_norm, ssm_w_in, ssm_log_dt, ssm_lambda_re, ssm_lambda_im,
              ssm_b_re, ssm_b_im, ssm_c_re, ssm_c_im, ssm_d, ssm_w_glu,
              kv_norm, kv_w, k_norm, attn_norm, attn_w_q, q_norm,
              lambda_q1, lambda_k1, lambda_q2, lambda_k2, subln, attn_w_o, rel_bias,
              ffn_norm, ffn_w_up, ffn_conv_w, ffn_conv_b, ffn_w_down,
              ple_norm, ple_w_gate, ple_w_proj):
    k = v = None
    for i in range(DEPTH):
        if i < N_A_LAYERS:
            h = rms_norm(x, ssm_norm[i])
            x = x + s5_mixer(h, ssm_w_in[i], ssm_log_dt[i], ssm_lambda_re[i], ssm_lambda_im[i],
                             ssm_b_re[i], ssm_b_im[i], ssm_c_re[i], ssm_c_im[i],
                             ssm_d[i], ssm_w_glu[i])
        else:
            if i == N_A_LAYERS:
                k, v = shared_kv(x, kv_norm, kv_w, k_norm)
            j = i - N_A_LAYERS
            lam_init = 0.8 - 0.6 * math.exp(-0.3 * i)
            h = rms_norm(x, attn_norm[j])
            x = x + diff_attention(h, k, v, rel_bias, attn_w_q[j], q_norm[j],
                                   lambda_q1[j], lambda_k1[j], lambda_q2[j], lambda_k2[j],
                                   subln[j], attn_w_o[j], lam_init)
        h = rms_norm(x, ffn_norm[i])
        x = x + conv_ffn(h, ffn_w_up[i], ffn_conv_w[i], ffn_conv_b[i], ffn_w_down[i])
        gate = jax.nn.sigmoid(rms_norm(x, ple_norm[i]) @ ple_w_gate[i])
        x = x + gate * (p[i] @ ple_w_proj[i])
    return x
```

```python
import math
from concourse.bass_utils import run_bass_kernel_spmd
import numpy as np
import concourse.bass as bass
import concourse.mybir as mybir
from contextlib import ExitStack

F32 = mybir.dt.float32
BF16 = mybir.dt.bfloat16
AF = mybir.ActivationFunctionType
ALU = mybir.AluOpType
AX = mybir.AxisListType

ENGS = ("pe", "act", "dve", "pool", "sp")
SAME_ENGINE_SYNC = ("act", "dve", "pool")
NDMA_SEMS = 12


class T:
    __slots__ = ("name", "w", "r")

    def __init__(self, name):
        self.name = name
        self.w = None
        self.r = []


class Op:
    __slots__ = ("eng", "fn", "deps", "is_dma", "signal", "sigval", "sem", "idx", "noinst")

    def __init__(self, eng, fn, is_dma):
        self.eng = eng
        self.fn = fn
        self.deps = []
        self.is_dma = is_dma
        self.signal = False
        self.sigval = 0
        self.sem = None
        self.idx = 0
        self.noinst = False


class Prog:
    def __init__(self, nc):
        self.nc = nc
        self.ops = {e: [] for e in ENGS}
        self.dma_slots = {e: [None] * NDMA_SEMS for e in ENGS}
        self.dma_count = {e: 0 for e in ENGS}
        self.dma_slot_vals = {e: [0] * NDMA_SEMS for e in ENGS}
        self.all_ops = []
        self.last_barrier = None

    def _track(self, op, reads, writes):
        deps = op.deps
        for t in reads:
            if t.w is not None:
                deps.append(t.w)
        for t in writes:
            if t.w is not None:
                deps.append(t.w)
            deps.extend(t.r)
        for t in reads:
            t.r.append(op)
        for t in writes:
            t.w = op
            t.r = []

    def add(self, eng, fn, reads=(), writes=()):
        op = Op(eng, fn, False)
        self._track(op, reads, writes)
        op.idx = len(self.all_ops)
        self.all_ops.append(op)
        self.ops[eng].append(op)
        return op

    def wait_only(self, eng, reads=()):
        op = Op(eng, lambda e: None, False)
        op.noinst = True
        for t in reads:
            if t.w is not None:
                op.deps.append(t.w)
        op.idx = len(self.all_ops)
        self.all_ops.append(op)
        self.ops[eng].append(op)
        return op

    def dma(self, eng, out, in_, reads=(), writes=(), **kw):
        def fn(e):
            return e.dma_start(out=out, in_=in_, **kw)
        op = Op(eng, fn, True)
        self._track(op, reads, writes)
        n = self.dma_count[eng]
        slot = n % NDMA_SEMS
        self.dma_count[eng] = n + 1
        prev = self.dma_slots[eng][slot]
        if prev is not None:
            op.deps.append(prev)
        self.dma_slots[eng][slot] = op
        self.dma_slot_vals[eng][slot] += 16
        op.sem = (eng, slot)
        op.sigval = self.dma_slot_vals[eng][slot]
        op.idx = len(self.all_ops)
        self.all_ops.append(op)
        self.ops[eng].append(op)
        return op

    def barrier(self):
        lasts = []
        for e in ENGS:
            for o in reversed(self.ops[e]):
                if o.noinst or o.is_dma:
                    continue
                lasts.append(o)
                break
        outstanding = [op for e in ENGS for op in self.dma_slots[e] if op is not None]
        for e in ENGS:
            op = Op(e, lambda eng: None, False)
            op.noinst = True
            op.deps = list(lasts) + list(outstanding)
            op.idx = len(self.all_ops)
            self.all_ops.append(op)
            self.ops[e].append(op)

    def emit(self, final_waits=()):
        nc = self.nc
        for op in self.all_ops:
            for d in op.deps:
                if d.is_dma:
                    continue
                if d.eng == op.eng and d.eng not in SAME_ENGINE_SYNC:
                    continue
                d.signal = True
        for e in ENGS:
            c = 0
            for op in self.ops[e]:
                if not op.is_dma:
                    if op.signal:
                        c += 1
                    op.sigval = c
        self.stats = {e: len(self.ops[e]) for e in ENGS}
        with ExitStack() as st:
            esem = {e: st.enter_context(nc.semaphore("s_" + e)) for e in ENGS}
            dsem = {}
            for e in ENGS:
                if self.dma_count[e] > 0:
                    for s in range(min(NDMA_SEMS, self.dma_count[e])):
                        dsem[(e, s)] = st.enter_context(nc.semaphore("d_%s_%d" % (e, s)))
            block = st.enter_context(nc.Block())
            ops = self.ops

            def run(ename, eng):
                waited = {}
                nwait = 0
                for op in ops[ename]:
                    need = {}
                    for d in op.deps:
                        if d.is_dma:
                            key = ("d",) + d.sem
                            v = d.sigval
                        else:
                            if d.eng == ename and ename not in SAME_ENGINE_SYNC:
                                continue
                            key = ("e", d.eng)
                            v = d.sigval
                        if waited.get(key, 0) >= v:
                            continue
                        if need.get(key, 0) < v:
                            need[key] = v
                    for key, v in need.items():
                        sem = dsem[key[1:]] if key[0] == "d" else esem[key[1]]
                        eng.wait_ge(sem, v)
                        waited[key] = v
                        nwait += 1
                    ins = op.fn(eng)
                    if ins is None:
                        continue
                    if op.is_dma:
                        ins.then_inc(dsem[op.sem], 16)
                    elif op.signal:
                        ins.then_inc(esem[ename], 1)
                self.stats["wait_" + ename] = nwait

            @block.tensor
            def _(eng):
                run("pe", eng)

            @block.scalar
            def _(eng):
                run("act", eng)

            @block.vector
            def _(eng):
                run("dve", eng)

            @block.gpsimd
            def _(eng):
                run("pool", eng)

            @block.sync
            def _(eng):
                run("sp", eng)


NT = 4096
SEQ = 2048
TL = 512
NTILE = NT // TL
D = 1024
DFF = 2816
NFC = 22
EPS = 1e-6
TWO_PI = 2.0 * math.pi
MAGIC = 12582912.0
NEG = -30000.0
N_A = 2
DEPTH = 4


def colv(v):
    v = np.asarray(v, np.float32)
    return np.ascontiguousarray(v.reshape(-1, 128).T)


class VecPack:
    def __init__(self):
        self.cols = {}
        self.parts = []
        self.n = 0

    def add(self, name, arr):
        arr = np.ascontiguousarray(arr, dtype=np.float32)
        assert arr.shape[0] == 128
        self.cols[name] = (self.n, arr.shape[1])
        self.parts.append(arr)
        self.n += arr.shape[1]

    def build(self):
        return np.ascontiguousarray(np.concatenate(self.parts, axis=1))


def t5_bucket_np(n):
    n = np.maximum(n, 0)
    nf = np.maximum(n, 16).astype(np.float32)
    large = 16 + (np.log(nf / np.float32(16)) / np.float32(math.log(128 / 16)) * np.float32(16)).astype(np.int32)
    large = np.minimum(large, 31)
    return np.where(n < 16, n, large)


def vec_layout(inp):
    vp = VecPack()
    for l in range(N_A):
        vp.add("ssm_norm%d" % l, colv(inp["ssm_norm"][l]))
        vp.add("ssm_d%d" % l, colv(inp["ssm_d"][l]))
    vp.add("kv_norm", colv(inp["kv_norm"]))
    for j in range(2):
        vp.add("attn_norm%d" % j, colv(inp["attn_norm"][j]))
    for l in range(DEPTH):
        vp.add("ffn_norm%d" % l, colv(inp["ffn_norm"][l]))
        vp.add("ple_norm%d" % l, colv(inp["ple_norm"][l]))
        cw = np.concatenate([colv(inp["ffn_conv_w"][l][j]) for j in range(3)], axis=1)
        vp.add("conv_w%d" % l, cw)
        vp.add("conv_b%d" % l, colv(inp["ffn_conv_b"][l]))
    vp.add("k_norm", np.tile(np.asarray(inp["k_norm"], np.float32), 2)[:, None])
    for j in range(2):
        vp.add("q_norm%d" % j, np.tile(np.asarray(inp["q_norm"][j], np.float32), 2)[:, None])
        vp.add("subln%d" % j, np.asarray(inp["subln"][j], np.float32)[:, None])
        for nm in ("lambda_q1", "lambda_k1", "lambda_q2", "lambda_k2"):
            vp.add("%s%d" % (nm, j), np.broadcast_to(np.asarray(inp[nm][j], np.float32)[None, :], (128, 64)))
    vp.add("rbT", np.broadcast_to(np.asarray(inp["rel_bias"], np.float32).T.reshape(1, 256), (128, 256)))
    sgn = np.concatenate([-np.ones(64, np.float32), np.ones(64, np.float32)])[:, None]
    vp.add("sgn", sgn)
    vp.add("nsgn", -sgn)
    mask8 = (np.arange(128)[:, None] // 16 == np.arange(8)[None, :]).astype(np.float32)
    vp.add("mask8", mask8)
    return vp


def ssm_layouts(inp, l):
    lre = np.asarray(inp["ssm_lambda_re"][l], np.float32)
    lim = np.asarray(inp["ssm_lambda_im"][l], np.float32)
    ldt = np.asarray(inp["ssm_log_dt"][l], np.float32)
    bre = np.asarray(inp["ssm_b_re"][l], np.float32)
    bim = np.asarray(inp["ssm_b_im"][l], np.float32)
    cre = np.asarray(inp["ssm_c_re"][l], np.float32)
    cim = np.asarray(inp["ssm_c_im"][l], np.float32)
    def F_gm(a):
        a4 = a.reshape(8, 8, 64)
        a4 = np.repeat(a4[:, :, None, :], 16, axis=2)
        return a4.transpose(1, 2, 0, 3).reshape(128, 8, 64)
    def F_b(b):
        b4 = b.reshape(8, 8, 64, 16)
        return b4.transpose(1, 3, 0, 2).reshape(128, 8, 64)
    ldtF = np.broadcast_to(ldt[:, None], (64, 64))
    ssmF = np.stack([F_gm(lre), F_gm(lim), F_b(bre), F_b(bim), F_gm(ldtF)], axis=1)
    ssmF = np.ascontiguousarray(ssmF.reshape(128, 5 * 512), dtype=np.float32)
    def S_gm(a):
        return np.concatenate([a.T, a.T], axis=0)
    ccS = np.concatenate([cre.transpose(2, 0, 1), cim.transpose(2, 0, 1)], axis=0)
    ssmS = np.concatenate([S_gm(lre), S_gm(lim), S_gm(ldtF), ccS.reshape(128, 1024)], axis=1)
    return ssmF, np.ascontiguousarray(ssmS, dtype=np.float32)


def t5_consts():
    k = np.arange(128)[:, None]
    q = np.arange(128)[None, :]
    oh = np.zeros((2, 128, 128, 32), np.float32)
    for dd in range(2):
        n = q - k + 128 * dd
        b = t5_bucket_np(n)
        oh[dd] = (b[:, :, None] == np.arange(32)[None, None, :]).astype(np.float32)
    mask0 = np.where(q >= k, 0.0, NEG).astype(np.float32)
    return np.ascontiguousarray(oh.reshape(2, 128, 4096)), mask0


class Arena:
    def __init__(self, ap, n):
        self.ap = ap
        self.n = n
        self.off = 0

    def f32(self, n):
        a = self.ap[:, self.off:self.off + n]
        self.off += n
        assert self.off <= self.n, ("arena overflow", self.off, self.n)
        return a

    def bf16(self, n):
        m = (n + 1) // 2
        a = self.ap[:, self.off:self.off + m].bitcast(BF16)
        self.off += m
        assert self.off <= self.n, ("arena overflow", self.off, self.n)
        return a


class Builder:
    def __init__(self, vp, stages=None, dump=None):
        self.vp = vp
        self.nc = bass.Bass("TRN2", target_bir_lowering=False)
        self.stages = stages
        self.bank_rr = 0

    def mm(self, out, lhsT, rhs, start, stop, reads, writes):
        self.P.add("pe", lambda e: e.matmul(out, lhsT, rhs, start=start, stop=stop), reads, writes)

    def act(self, out, in_, func, reads, writes, scale=1.0, bias=0.0):
        self.P.add("act", lambda e: e.activation(out=out, in_=in_, func=func, scale=scale, bias=bias), reads, writes)

    def stt(self, eng, out, in0, scalar, in1, op0, op1, reads, writes):
        self.P.add(eng, lambda e: e.scalar_tensor_tensor(out=out, in0=in0, scalar=scalar, in1=in1, op0=op0, op1=op1), reads, writes)

    def tt(self, eng, out, in0, in1, op, reads, writes):
        self.P.add(eng, lambda e: e.tensor_tensor(out=out, in0=in0, in1=in1, op=op), reads, writes)

    def ts(self, eng, out, in0, s1, s2, op0, op1, reads, writes):
        if s2 is None:
            self.P.add(eng, lambda e: e.tensor_scalar(out=out, in0=in0, scalar1=s1, scalar2=None, op0=op0), reads, writes)
        else:
            self.P.add(eng, lambda e: e.tensor_scalar(out=out, in0=in0, scalar1=s1, scalar2=s2, op0=op0, op1=op1), reads, writes)

    def cp(self, eng, out, in_, reads, writes):
        self.P.add(eng, lambda e: e.tensor_copy(out=out, in_=in_), reads, writes)

    def memset(self, eng, ap, val, writes):
        self.P.add(eng, lambda e: e.memset(ap, val), (), writes)

    def reduce(self, out, in_, reads, writes):
        self.P.add("dve", lambda e: e.tensor_reduce(out=out, in_=in_, axis=AX.X, op=ALU.add), reads, writes)

    def recip(self, out, in_, reads, writes):
        self.P.add("dve", lambda e: e.reciprocal(out=out, in_=in_), reads, writes)

    def vcol(self, name, i=0, n=1):
        o, w = self.vp.cols[name]
        return self.vecs[:, o + i:o + i + n]

    def bank(self):
        b = self.bank_rr % 8
        self.bank_rr += 1
        return self.ps[b], self.Tps[b]

    def load_w(self, dst3, src2, ncols, T_, c0=0, step=512):
        sv = src2.rearrange("(kc p) n -> p kc n", p=128)
        for a in range(0, ncols, step):
            b = min(ncols, a + step)
            self.P.dma("pool", dst3[:, :, a:b], sv[:, :, c0 + a:c0 + b], writes=[T_])

    def rmsnorm(self, xt, Txt, gname, h, Th, sq, Tsq, rstd, Trstd, N):
        Tsql = Tsq if isinstance(Tsq, list) else [Tsq]
        self.act(sq.rearrange("p c n -> p (c n)"), xt.rearrange("p c n -> p (c n)"), AF.Square, [Txt], Tsql)
        pb, Tpb = self.bank()
        for c in range(8):
            self.mm(pb[:, :N], self.ones_bf[:, :], sq[:, c, :], c == 0, c == 7, Tsql + [self.Tconst], [Tpb])
        self.act(rstd, pb[:, :N], AF.Ln, [Tpb], [Trstd], scale=1.0 / D, bias=EPS)
        self.act(rstd, rstd, AF.Exp, [Trstd], [Trstd], scale=-0.5)
        for c in range(8):
            self.stt("dve", h[:, c, :], xt[:, c, :], self.vcol(gname, c), rstd, ALU.mult, ALU.mult, [Txt, Trstd], [Th])

    def headnorm(self, pb, Tpb, gcol, out, Tout, sq, Tsq, rstd, Trstd, N, dim, ones_mat):
        self.act(sq, pb[:, :N], AF.Square, [Tpb], [Tsq])
        p2, Tp2 = self.bank()
        self.mm(p2[:, :N], ones_mat, sq, True, True, [Tsq, self.Tconst], [Tp2])
        self.act(rstd, p2[:, :N], AF.Ln, [Tp2], [Trstd], scale=1.0 / dim, bias=EPS)
        self.act(rstd, rstd, AF.Exp, [Trstd], [Trstd], scale=-0.5)
        self.stt("dve", out, pb[:, :N], gcol, rstd, ALU.mult, ALU.mult, [Tpb, Trstd], [Tout])

    def build(self, upto=None):
        nc = self.nc
        st = ExitStack()
        with st:
            def din(name, shape, dt=F32):
                return nc.dram_tensor(name, list(shape), dt, kind="ExternalInput").ap()
            self.xT = din("xT", [D, NT])
            self.pT = din("pT", [DEPTH, 256, NT])
            self.vecs_d = din("vecs", [128, self.vp.n])
            self.ssmF_d = [din("ssmF%d" % l, [128, 2560]) for l in range(N_A)]
            self.ssmS_d = [din("ssmS%d" % l, [128, 1216]) for l in range(N_A)]
            self.oh_d = din("t5oh", [2, 128, 4096])
            self.mask0_d = din("t5mask0", [128, 128])
            self.w = {}
            for nm, shp in (("ssm_w_in", [2, D, D]), ("ssm_w_glu", [2, D, 2 * D]), ("kv_w", [D, 2 * D]),
                            ("attn_w_q", [2, D, D]), ("attn_w_o", [2, D, D]), ("ffn_w_up", [4, D, 2 * DFF]),
                            ("ffn_w_down", [4, DFF, D]), ("ple_w_gate", [4, D, D]), ("ple_w_proj", [4, 256, D])):
                self.w[nm] = din(nm, shp)
            self.yT = nc.dram_tensor("yT", [D, NT], F32, kind="ExternalOutput").ap()
            self.XT = nc.dram_tensor("XTs", [D, NT], F32).ap()
            self.KTd = nc.dram_tensor("KTs", [8, 128, NT], BF16).ap()
            self.QTd = nc.dram_tensor("QTs", [8, 128, NT], BF16).ap()
            self.Vd = nc.dram_tensor("Vs", [NT, D], BF16).ap()

            NARENA = 52800
            arena_t = st.enter_context(nc.sbuf_tensor("arena", [128, NARENA], F32))
            self.A = Arena(arena_t[:, :], NARENA)
            self.ps = [st.enter_context(nc.psum_tensor("ps%d" % i, [128, 512], F32)) for i in range(8)]
            self.Tps = [T("ps%d" % i) for i in range(8)]
            self.P = Prog(nc)
            P = self.P
            A = self.A
            self.Tconst = T("const")
            self.vecs = A.f32(self.vp.n)
            P.dma("sp", self.vecs, self.vecs_d, writes=[self.Tconst])
            self.ones_bf = A.bf16(128)
            self.memset("pool", self.ones_bf, 1.0, [self.Tconst])
            self.blk_bf = A.bf16(128)
            self.memset("pool", self.blk_bf, 0.0, [self.Tconst])
            self.memset("pool", self.blk_bf[0:64, 0:64], 1.0, [self.Tconst])
            self.memset("pool", self.blk_bf[64:128, 64:128], 1.0, [self.Tconst])
            self.t5tab = A.f32(8 * 2 * 128)
            self.Tt5 = T("t5tab")
            self.lamcols = A.f32(8)
            self.Tlam = T("lam")
            self.persist_off = A.off

            stages = []
            for l in range(DEPTH):
                if l < N_A:
                    stages.append(("A0_%d" % l, lambda xi, xo, l=l: self.stage_A(l, xi, xo)))
                else:
                    if l == N_A:
                        stages.append(("T5", lambda xi, xo: self.stage_T5()))
                        stages.append(("KV", lambda xi, xo: self.stage_KV(xi)))
                    stages.append(("B_%d" % l, lambda xi, xo, l=l: self.stage_B(l, xi, xo)))
                stages.append(("FFN_%d" % l, lambda xi, xo, l=l: self.stage_FFN(l, xi, xo)))
                stages.append(("PLE_%d" % l, lambda xi, xo, l=l: self.stage_PLE(l, xi, xo)))
            if upto is not None:
                stages = stages[:upto]
            mod = [i for i, (n, f) in enumerate(stages) if not (n.startswith("T5") or n.startswith("KV"))]
            first, last = mod[0], mod[-1]
            for i, (n, f) in enumerate(stages):
                xi = self.xT if i <= first else self.XT
                xo = self.yT if i == last else self.XT
                if i > last:
                    xi = self.yT
                P.barrier()
                A.off = self.persist_off
                f(xi, xo)
            P.barrier()
            P.emit()
            print("ops", P.stats)
        return nc

    def xtile_view(self, xd, t):
        return xd.rearrange("(c p) n -> p c n", p=128)[:, :, t * TL:(t + 1) * TL]

    def stage_A(self, l, xin, xout):
        P, A = self.P, self.A
        Ttab = T("ssmtab")
        BtX = A.bf16(64 * 128).rearrange("p (g m) -> p g m", m=128)
        BtXs = A.bf16(64 * 128).rearrange("p (g m) -> p g m", m=128)
        Ct = A.bf16(64 * 128).rearrange("p (g m) -> p g m", m=128)
        coef = A.f32(64 * 11 * 3).rearrange("p (g k j) -> p g k j", k=11, j=3)
        U = A.bf16(8 * NT).rearrange("p (c n) -> p c n", n=NT)
        mark = A.off
        sF = A.f32(2560)
        TF = T("sF")
        P.dma("sp", sF, self.ssmF_d[l], writes=[TF])
        lre, lim, bre, bim, ldt = [sF[:, i * 512:(i + 1) * 512] for i in range(5)]
        tmpF = [A.f32(512) for _ in range(10)]
        Tt = [T("tF%d" % i) for i in range(10)]

        def trig(theta, Tth, lrdt, Tlr, out_r, out_i, To, tA, TA, tB, TB, n):
            self.ts("dve", tA, theta, 1.0 / TWO_PI, MAGIC, ALU.mult, ALU.add, [Tth], [TA])
            self.ts("dve", tA, tA, -MAGIC, None, ALU.add, None, [TA], [TA])
            self.stt("dve", tA, tA, -TWO_PI, theta, ALU.mult, ALU.add, [TA, Tth], [TA])
            self.ts("dve", tA, tA, 3.1415925, -3.1415925, ALU.min, ALU.max, [TA], [TA])
            self.act(out_i, tA, AF.Sin, [TA], [To])
            self.act(tB, tA, AF.Abs, [TA], [TB])
            self.act(out_r, tB, AF.Sin, [TB], [To], scale=-1.0, bias=math.pi / 2)
            self.act(tB, lrdt, AF.Exp, [Tlr], [TB])
            self.tt("dve", out_r, out_r, tB, ALU.mult, [To, TB], [To])
            self.tt("dve", out_i, out_i, tB, ALU.mult, [To, TB], [To])

        dt, lrdt, th, abr, abi, t5, t6, t7, t8, t9 = tmpF
        Tdt, Tlrdt, Tth, Tab, _, T5_, T6_, T7_, T8_, T9_ = Tt
        self.act(dt, ldt, AF.Exp, [TF], [Tdt])
        self.tt("dve", lrdt, lre, dt, ALU.mult, [TF, Tdt], [Tlrdt])
        self.tt("dve", th, lim, dt, ALU.mult, [TF, Tdt], [Tth])
        trig(th, Tth, lrdt, Tlrdt, abr, abi, Tab, t5, T5_, t6, T6_, 512)
        self.tt("dve", t5, lre, lre, ALU.mult, [TF], [T5_])
        self.tt("dve", t6, lim, lim, ALU.mult, [TF], [T6_])
        self.tt("dve", t5, t5, t6, ALU.add, [T5_, T6_], [T5_])
        self.recip(t5, t5, [T5_], [T5_])
        self.ts("dve", t6, abr, -1.0, None, ALU.add, None, [Tab], [T6_])
        self.tt("dve", t7, t6, lre, ALU.mult, [T6_, TF], [T7_])
        self.tt("dve", t8, abi, lim, ALU.mult, [Tab, TF], [T8_])
        self.tt("dve", t7, t7, t8, ALU.add, [T7_, T8_], [T7_])
        self.tt("dve", t7, t7, t5, ALU.mult, [T7_, T5_], [T7_])
        self.tt("dve", t8, abi, lre, ALU.mult, [Tab, TF], [T8_])
        self.tt("dve", t9, t6, lim, ALU.mult, [T6_, TF], [T9_])
        self.tt("dve", t8, t8, t9, ALU.subtract, [T8_, T9_], [T8_])
        self.tt("dve", t8, t8, t5, ALU.mult, [T8_, T5_], [T8_])
        bfull = A.f32(8 * 128).rearrange("p (c m) -> p c m", m=128)
        bfulls = A.f32(8 * 128).rearrange("p (c m) -> p c m", m=128)
        Tbf = T("bfull")
        v3 = lambda a: a.rearrange("p (c m) -> p c m", m=64)
        self.tt("dve", t5, t7, bre, ALU.mult, [T7_, TF], [T5_])
        self.tt("dve", t6, t8, bim, ALU.mult, [T8_, TF], [T6_])
        self.tt("dve", bfull[:, :, 0:64], v3(t5), v3(t6), ALU.subtract, [T5_, T6_], [Tbf])
        self.tt("dve", t5, t7, bim, ALU.mult, [T7_, TF], [T5_])
        self.tt("dve", t6, t8, bre, ALU.mult, [T8_, TF], [T6_])
        self.tt("dve", bfull[:, :, 64:128], v3(t5), v3(t6), ALU.add, [T5_, T6_], [Tbf])
        self.cp("dve", bfulls[:, :, 0:64], bfull[:, :, 64:128], [Tbf], [Tbf])
        self.cp("dve", bfulls[:, :, 64:128], bfull[:, :, 0:64], [Tbf], [Tbf])
        for (bt, bf_) in ((BtX, bfull), (BtXs, bfulls)):
            for ch in range(8):
                for g8 in range(8):
                    self.ts("dve", bt[:, ch * 8 + g8, :], bf_[:, ch, :], self.vcol("mask8", g8), None, ALU.mult, None,
                            [Tbf, self.Tconst], [Ttab])
        sS = A.f32(1216)
        TS = T("sS")
        P.dma("sp", sS, self.ssmS_d[l], writes=[TS])
        lreS, limS, ldtS = sS[:, 0:64], sS[:, 64:128], sS[:, 128:192]
        ccS = sS[:, 192:1216].rearrange("p (g c) -> p g c", c=16)
        s_ = [A.f32(64) for _ in range(8)]
        Ts_ = [T("tS%d" % i) for i in range(8)]
        dtS, lrdtS, thS, ar, ai, u5, u6, u7 = s_
        TdtS, TlrS, TthS, Ta, _, Tu5, Tu6, Tu7 = Ts_
        self.act(dtS, ldtS, AF.Exp, [TS], [TdtS])
        self.tt("dve", lrdtS, lreS, dtS, ALU.mult, [TS, TdtS], [TlrS])
        self.tt("dve", thS, limS, dtS, ALU.mult, [TS, TdtS], [TthS])
        trig(thS, TthS, lrdtS, TlrS, ar, ai, Ta, u5, Tu5, u6, Tu6, 64)
        Tcoef = T("coef")
        for k in range(11):
            self.cp("dve", coef[:, :, k, 0], ar, [Ta], [Tcoef])
            self.ts("dve", coef[:, :, k, 1], ai, self.vcol("sgn"), None, ALU.mult, None, [Ta, self.Tconst], [Tcoef])
            self.ts("dve", coef[:, :, k, 2], ai, self.vcol("nsgn"), None, ALU.mult, None, [Ta, self.Tconst], [Tcoef])
            if k < 10:
                self.tt("dve", u5, ar, ar, ALU.mult, [Ta], [Tu5])
                self.tt("dve", u6, ai, ai, ALU.mult, [Ta], [Tu6])
                self.tt("dve", u7, ar, ai, ALU.mult, [Ta], [Tu7])
                self.tt("dve", ar, u5, u6, ALU.subtract, [Tu5, Tu6], [Ta])
                self.ts("dve", ai, u7, 2.0, None, ALU.mult, None, [Tu7], [Ta])
        self.ts("dve", ccS[64:128, :, :], ccS[64:128, :, :], -1.0, None, ALU.mult, None, [TS], [TS])
        self.memset("pool", Ct.rearrange("p g m -> p (g m)"), 0.0, [Ttab])
        Ctd = Ct.rearrange("p (ch g8) (h c) -> p ch g8 h c", g8=8, c=16)
        ccS4 = ccS.rearrange("p (ch g8) c -> p ch g8 c", g8=8)
        for g8 in range(8):
            self.cp("dve", Ctd[:, :, g8, g8, :], ccS4[:, :, g8, :], [TS], [Ttab])

        if getattr(self, "substop", 9) <= 0:
            return
        P.barrier()
        A.off = mark
        win = A.bf16(8 * D).rearrange("p (k n) -> p k n", n=D)
        Tw = T("win")
        self.load_w(win, self.w["ssm_w_in"][l], D, Tw)
        xb = [A.f32(8 * TL).rearrange("p (c n) -> p c n", n=TL) for _ in range(2)]
        Txb = [T("xb0"), T("xb1")]
        sq = A.bf16(8 * TL).rearrange("p (c n) -> p c n", n=TL)
        Tsq = T("sq")
        h = A.bf16(8 * TL).rearrange("p (c n) -> p c n", n=TL)
        Th = T("h")
        rstd = A.f32(TL)
        Trs = T("rstd")
        TU = [[T("U%d_%d" % (c, t)) for t in range(NTILE)] for c in range(8)]
        for t in range(NTILE):
            xt, Txt = xb[t % 2], Txb[t % 2]
            P.dma("sp", xt, self.xtile_view(xin, t), writes=[Txt])
            self.rmsnorm(xt, Txt, "ssm_norm%d" % l, h, Th, sq, Tsq, rstd, Trs, TL)
            for oc in range(8):
                pb, Tpb = self.bank()
                for kc in range(8):
                    self.mm(pb[:, :], win[:, kc, oc * 128:(oc + 1) * 128], h[:, kc, :], kc == 0, kc == 7, [Tw, Th], [Tpb])
                self.act(U[:, oc, t * TL:(t + 1) * TL], pb[:, :], AF.Copy, [Tpb], [TU[oc][t]])

        if getattr(self, "substop", 9) <= 1:
            return
        P.barrier()
        A.off = mark
        X = A.f32(NT)
        Xs = A.f32(NT)
        TX, TXs = T("X"), T("Xs")
        Xbf = [A.bf16(NT) for _ in range(2)]
        TXbf = [T("Xbf0"), T("Xbf1")]
        Ysb = A.f32(NT)
        TY = T("Ysb")
        for ch in range(getattr(self, "nch", 8)):
            for pair in range(4):
                for gi in range(2):
                    g8 = pair * 2 + gi
                    g = ch * 8 + g8
                    for t in range(NTILE):
                        for (bt, dst, Td) in ((BtX, X, TX), (BtXs, Xs, TXs)):
                            pb, Tpb = self.bank()
                            self.mm(pb[:, :], bt[:, g, :], U[:, ch, t * TL:(t + 1) * TL], True, True, [Ttab, TU[ch][t]], [Tpb])
                            self.act(dst[:, t * TL:(t + 1) * TL], pb[:, :], AF.Copy, [Tpb], [Td])
                    def lvl(tgtX, srcX, tgtXs, srcXs, k):
                        p1 = coef[:, g, k, 0:1]
                        p2 = coef[:, g, k, 1:2]
                        p2s = coef[:, g, k, 2:3]
                        self.stt("dve", tgtX, srcX, p1, tgtX, ALU.mult, ALU.add, [TX, Tcoef], [TX])
                        self.stt("dve", tgtX, srcXs, p2, tgtX, ALU.mult, ALU.add, [TX, TXs, Tcoef], [TX])
                        self.stt("dve", tgtXs, srcXs, p1, tgtXs, ALU.mult, ALU.add, [TXs, Tcoef], [TXs])
                        self.stt("dve", tgtXs, srcX, p2s, tgtXs, ALU.mult, ALU.add, [TX, TXs, Tcoef], [TXs])
                    for k in range(11):
                        blk = 2 << k
                        d = 1 << k
                        Xv = X.rearrange("p (s m q) -> p s m q", s=2, q=blk)
                        Xsv = Xs.rearrange("p (s m q) -> p s m q", s=2, q=blk)
                        lvl(Xv[:, :, :, blk - 1], Xv[:, :, :, d - 1], Xsv[:, :, :, blk - 1], Xsv[:, :, :, d - 1], k)
                    for k in range(9, -1, -1):
                        blk = 2 << k
                        d = 1 << k
                        M = SEQ // blk
                        Xv = X.rearrange("p (s m q) -> p s m q", s=2, q=blk)
                        Xsv = Xs.rearrange("p (s m q) -> p s m q", s=2, q=blk)
                        lvl(Xv[:, :, 1:, d - 1], Xv[:, :, :M - 1, blk - 1], Xsv[:, :, 1:, d - 1], Xsv[:, :, :M - 1, blk - 1], k)
                    self.act(Xbf[gi], X, AF.Copy, [TX], [TXbf[gi]])
                for t in range(NTILE):
                    pb, Tpb = self.bank()
                    for gi in range(2):
                        g = ch * 8 + pair * 2 + gi
                        self.mm(pb[:, :], Ct[:, g, :], Xbf[gi][:, t * TL:(t + 1) * TL], gi == 0, gi == 1, [Ttab, TXbf[gi]], [Tpb])
                    lo = pair * 32
                    self.act(Ysb[lo:lo + 32, t * TL:(t + 1) * TL], pb[lo:lo + 32, :], AF.Copy, [Tpb], [TY])
            Tuc = TU[ch]
            self.stt("dve", Ysb, U[:, ch, :], self.vcol("ssm_d%d" % l, ch), Ysb, ALU.mult, ALU.add, [TY] + Tuc, [TY])
            self.act(U[:, ch, :], Ysb, AF.Gelu, [TY], Tuc)

        if getattr(self, "substop", 9) <= 2:
            return
        P.barrier()
        A.off = mark
        wg = A.bf16(8 * 2 * D).rearrange("p (k n) -> p k n", n=2 * D)
        Twg = T("wglu")
        self.load_w(wg, self.w["ssm_w_glu"][l], 2 * D, Twg)
        xb = [A.f32(8 * TL).rearrange("p (c n) -> p c n", n=TL) for _ in range(2)]
        Txb = [T("xb0"), T("xb1")]
        sg = [A.f32(TL) for _ in range(2)]
        Tsg = [T("sg0"), T("sg1")]
        for t in range(NTILE):
            xt, Txt = xb[t % 2], Txb[t % 2]
            P.dma("sp", xt, self.xtile_view(xin, t), writes=[Txt])
            for oc in range(8):
                pv, Tpv = self.bank()
                pg, Tpg = self.bank()
                for kc in range(8):
                    self.mm(pv[:, :], wg[:, kc, oc * 128:(oc + 1) * 128], U[:, kc, t * TL:(t + 1) * TL], kc == 0, kc == 7, [Twg, TU[kc][t]], [Tpv])
                for kc in range(8):
                    self.mm(pg[:, :], wg[:, kc, D + oc * 128:D + (oc + 1) * 128], U[:, kc, t * TL:(t + 1) * TL], kc == 0, kc == 7, [Twg, TU[kc][t]], [Tpg])
                s_, Ts2 = sg[oc % 2], Tsg[oc % 2]
                self.act(s_, pg[:, :], AF.Sigmoid, [Tpg], [Ts2])
                self.tt("dve", s_, pv[:, :], s_, ALU.mult, [Tpv, Ts2], [Ts2])
                self.tt("dve", xt[:, oc, :], xt[:, oc, :], s_, ALU.add, [Txt, Ts2], [Txt])
            P.dma("sp", self.xtile_view(xout, t), xt, reads=[Txt])

    def stage_FFN(self, l, xin, xout):
        P, A = self.P, self.A
        wup = A.bf16(8 * 2 * DFF).rearrange("p (k n) -> p k n", n=2 * DFF)
        Twu = [T("wup%d" % j) for j in range(11)]
        wsrc = self.w["ffn_w_up"][l].rearrange("(kc p) n -> p kc n", p=128)
        for j in range(11):
            P.dma("pool", wup[:, :, j * 256:(j + 1) * 256], wsrc[:, :, j * 256:(j + 1) * 256], writes=[Twu[j]])
            P.dma("pool", wup[:, :, DFF + j * 256:DFF + (j + 1) * 256], wsrc[:, :, DFF + j * 256:DFF + (j + 1) * 256], writes=[Twu[j]])
        dsrc = self.w["ffn_w_down"][l].rearrange("(kc p) n -> p kc n", p=128)
        NR = 3
        wdr = [A.bf16(NFC * 128).rearrange("p (k n) -> p k n", n=128) for _ in range(NR)]
        Twd = [T("wdr%d" % i) for i in range(NR)]
        xb = [A.f32(8 * TL).rearrange("p (c n) -> p c n", n=TL) for _ in range(2)]
        Txb = [T("xb0"), T("xb1")]
        h = A.bf16(8 * TL).rearrange("p (c n) -> p c n", n=TL)
        Th = T("h")
        rstd = A.f32(TL)
        Trs = T("rstd")
        abuf = A.bf16(NFC * TL).rearrange("p (k n) -> p k n", n=TL)
        Ta = [T("a%d" % k) for k in range(NFC)]
        ub = [A.f32(TL + 2) for _ in range(4)]
        Tub = [T("ub%d" % i) for i in range(4)]
        tball = A.f32(4 * TL)
        tb = [tball[:, i * TL:(i + 1) * TL] for i in range(4)]
        Ttb = [T("tb%d" % i) for i in range(4)]
        sqh = tball.bitcast(BF16).rearrange("p (c n) -> p c n", n=TL)
        halo = A.f32(44 * 2).rearrange("p (v c) -> p v c", c=2)
        Thalo = [T("halo%d" % v) for v in range(44)]
        cw = lambda tap, vc: self.vcol("conv_w%d" % l, tap * 44 + vc)
        cb = lambda vc: self.vcol("conv_b%d" % l, vc)
        cnt = 0
        dcnt = 0
        for t in range(NTILE):
            xt, Txt = xb[t % 2], Txb[t % 2]
            P.dma("sp", xt, self.xtile_view(xin, t), writes=[Txt])
            self.rmsnorm(xt, Txt, "ffn_norm%d" % l, h, Th, sqh, list(Ttb), rstd, Trs, TL)
            for k in range(NFC):
                res = []
                for part in range(2):
                    vc = part * NFC + k
                    col0 = part * DFF + k * 128
                    u, Tu = ub[cnt % 4], Tub[cnt % 4]
                    tt_, Ttt = tb[cnt % 4], Ttb[cnt % 4]
                    cnt += 1
                    pb, Tpb = self.bank()
                    for kc in range(8):
                        self.mm(pb[:, :], wup[:, kc, col0:col0 + 128], h[:, kc, :], kc == 0, kc == 7, [Twu[k // 2], Th], [Tpb])
                    if t % 4 == 0:
                        self.memset("dve", u[:, 0:2], 0.0, [Tu])
                    else:
                        self.act(u[:, 0:2], halo[:, vc, :], AF.Copy, [Thalo[vc]], [Tu])
                    self.act(u[:, 2:TL + 2], pb[:, :], AF.Copy, [Tpb], [Tu])
                    self.act(halo[:, vc, :], u[:, TL:TL + 2], AF.Copy, [Tu], [Thalo[vc]])
                    self.act(tt_, u[:, 2:TL + 2], AF.Identity, [Tu, self.Tconst], [Ttt], scale=cw(2, vc), bias=cb(vc))
                    self.stt("dve", tt_, u[:, 1:TL + 1], cw(1, vc), tt_, ALU.mult, ALU.add, [Tu, Ttt], [Ttt])
                    self.stt("dve", tt_, u[:, 0:TL], cw(0, vc), tt_, ALU.mult, ALU.add, [Tu, Ttt], [Ttt])
                    res.append((tt_, Ttt))
                (tg, Ttg), (tv, Ttv) = res
                self.act(tg, tg, AF.Gelu, [Ttg], [Ttg])
                self.tt("dve", abuf[:, k, :], tg, tv, ALU.mult, [Ttg, Ttv], [Ta[k]])
            for oc in range(8):
                wd, Tw_ = wdr[dcnt % NR], Twd[dcnt % NR]
                dcnt += 1
                P.dma("pool", wd, dsrc[:, :, oc * 128:(oc + 1) * 128], writes=[Tw_])
                pb, Tpb = self.bank()
                for k in range(NFC):
                    self.mm(pb[:, :], wd[:, k, :], abuf[:, k, :], k == 0, k == NFC - 1, [Tw_, Ta[k]], [Tpb])
                self.tt("dve", xt[:, oc, :], xt[:, oc, :], pb[:, :], ALU.add, [Txt, Tpb], [Txt])
            P.dma("sp", self.xtile_view(xout, t), xt, reads=[Txt])

    def stage_PLE(self, l, xin, xout):
        P, A = self.P, self.A
        wgt = A.bf16(8 * D).rearrange("p (k n) -> p k n", n=D)
        wpj = A.bf16(2 * D).rearrange("p (k n) -> p k n", n=D)
        Tw = T("wple")
        self.load_w(wgt, self.w["ple_w_gate"][l], D, Tw)
        self.load_w(wpj, self.w["ple_w_proj"][l], D, Tw)
        xb = [A.f32(8 * TL).rearrange("p (c n) -> p c n", n=TL) for _ in range(2)]
        Txb = [T("xb0"), T("xb1")]
        pb_ = [A.bf16(2 * TL).rearrange("p (c n) -> p c n", n=TL) for _ in range(2)]
        Tpp = [T("pp0"), T("pp1")]
        sq = A.bf16(8 * TL).rearrange("p (c n) -> p c n", n=TL)
        Tsq = T("sq")
        h = A.bf16(8 * TL).rearrange("p (c n) -> p c n", n=TL)
        Th = T("h")
        rstd = A.f32(TL)
        Trs = T("rstd")
        sg = [A.f32(TL) for _ in range(2)]
        Tsg = [T("sg0"), T("sg1")]
        psrc = self.pT[l].rearrange("(c p) n -> p c n", p=128)
        for t in range(NTILE):
            xt, Txt = xb[t % 2], Txb[t % 2]
            pp, Tp = pb_[t % 2], Tpp[t % 2]
            P.dma("sp", xt, self.xtile_view(xin, t), writes=[Txt])
            P.dma("pool", pp, psrc[:, :, t * TL:(t + 1) * TL], writes=[Tp])
            self.rmsnorm(xt, Txt, "ple_norm%d" % l, h, Th, sq, Tsq, rstd, Trs, TL)
            for oc in range(8):
                pg, Tpg = self.bank()
                pv, Tpv = self.bank()
                for kc in range(8):
                    self.mm(pg[:, :], wgt[:, kc, oc * 128:(oc + 1) * 128], h[:, kc, :], kc == 0, kc == 7, [Tw, Th], [Tpg])
                for kc in range(2):
                    self.mm(pv[:, :], wpj[:, kc, oc * 128:(oc + 1) * 128], pp[:, kc, :], kc == 0, kc == 1, [Tw, Tp], [Tpv])
                s_, Ts2 = sg[oc % 2], Tsg[oc % 2]
                self.act(s_, pg[:, :], AF.Sigmoid, [Tpg], [Ts2])
                self.tt("dve", s_, pv[:, :], s_, ALU.mult, [Tpv, Ts2], [Ts2])
                self.tt("dve", xt[:, oc, :], xt[:, oc, :], s_, ALU.add, [Txt, Ts2], [Txt])
            P.dma("sp", self.xtile_view(xout, t), xt, reads=[Txt])

    def stage_T5(self):
        P, A = self.P, self.A
        oh = [A.f32(4096) for _ in range(2)]
        Toh = T("oh")
        for dd in range(2):
            P.dma("sp", oh[dd], self.oh_d[dd], writes=[Toh])
        m0 = A.f32(128)
        P.dma("sp", m0, self.mask0_d, writes=[Toh])
        tmp = A.f32(4096)
        Ttmp = T("t5tmp")
        tab = self.t5tab.rearrange("p (h d q) -> p h d q", d=2, q=128)
        o, _ = self.vp.cols["rbT"]
        for hh in range(8):
            rb = self.vecs[:, o + hh * 32:o + (hh + 1) * 32]
            rbb = rb.unsqueeze(1).broadcast_to([128, 128, 32])
            for dd in range(2):
                self.tt("dve", tmp.rearrange("p (q b) -> p q b", b=32), oh[dd].rearrange("p (q b) -> p q b", b=32), rbb, ALU.mult,
                        [Toh, self.Tconst], [Ttmp])
                self.reduce(tab[:, hh, dd, :], tmp.rearrange("p (q b) -> p q b", b=32), [Ttmp], [self.Tt5])
                self.ts("dve", tab[:, hh, dd, :], tab[:, hh, dd, :], self.vecs[:, o + hh * 32 + 31:o + hh * 32 + 32], None, ALU.subtract, None,
                        [self.Tt5, self.Tconst], [self.Tt5])
            self.tt("dve", tab[:, hh, 0, :], tab[:, hh, 0, :], m0, ALU.add, [self.Tt5, Toh], [self.Tt5])

    def stage_KV(self, xin):
        P, A = self.P, self.A
        wkv = A.bf16(8 * 2 * D).rearrange("p (k n) -> p k n", n=2 * D)
        Tw = T("wkv")
        self.load_w(wkv, self.w["kv_w"], 2 * D, Tw)
        xb = [A.f32(8 * TL).rearrange("p (c n) -> p c n", n=TL) for _ in range(2)]
        Txb = [T("xb0"), T("xb1")]
        sq = A.bf16(8 * TL).rearrange("p (c n) -> p c n", n=TL)
        Tsq = T("sq")
        h = A.bf16(8 * TL).rearrange("p (c n) -> p c n", n=TL)
        Th = T("h")
        rstd = A.f32(TL)
        Trs = T("rstd")
        sq2 = [A.bf16(TL) for _ in range(2)]
        Tsq2 = [T("sq2a"), T("sq2b")]
        rs2 = [A.f32(TL) for _ in range(2)]
        Trs2 = [T("rs2a"), T("rs2b")]
        kb_ = [A.bf16(8 * TL).rearrange("p (c n) -> p c n", n=TL) for _ in range(2)]
        Tkb = [T("kb0"), T("kb1")]
        vb_ = [A.bf16(4 * D).rearrange("p (b n) -> p b n", n=D) for _ in range(2)]
        Tvb = [T("vb0"), T("vb1")]
        for t in range(NTILE):
            xt, Txt = xb[t % 2], Txb[t % 2]
            kb, Tk = kb_[t % 2], Tkb[t % 2]
            vb, Tv = vb_[t % 2], Tvb[t % 2]
            P.dma("sp", xt, self.xtile_view(xin, t), writes=[Txt])
            self.rmsnorm(xt, Txt, "kv_norm", h, Th, sq, Tsq, rstd, Trs, TL)
            for hh in range(8):
                pb, Tpb = self.bank()
                for kc in range(8):
                    self.mm(pb[:, :], wkv[:, kc, hh * 128:(hh + 1) * 128], h[:, kc, :], kc == 0, kc == 7, [Tw, Th], [Tpb])
                self.headnorm(pb, Tpb, self.vcol("k_norm"), kb[:, hh, :], Tk, sq2[hh % 2], Tsq2[hh % 2], rs2[hh % 2], Trs2[hh % 2], TL, 64, self.blk_bf[:, :])
            P.dma("sp", self.KTd.rearrange("h p n -> p h n")[:, :, t * TL:(t + 1) * TL], kb, reads=[Tk])
            for tb in range(4):
                for half in range(2):
                    pb, Tpb = self.bank()
                    for kc in range(8):
                        self.mm(pb[:, :], h[:, kc, tb * 128:(tb + 1) * 128], wkv[:, kc, D + half * 512:D + (half + 1) * 512], kc == 0, kc == 7, [Tw, Th], [Tpb])
                    self.act(vb[:, tb, half * 512:(half + 1) * 512], pb[:, :], AF.Copy, [Tpb], [Tv])
            P.dma("sp", self.Vd[t * TL:(t + 1) * TL, :].rearrange("(b p) n -> p b n", p=128), vb, reads=[Tv])

    def stage_B(self, l, xin, xout):
        P, A = self.P, self.A
        j = l - N_A
        lam_init = 0.8 - 0.6 * math.exp(-0.3 * l)
        mark = A.off
        lt = A.f32(64)
        Tlt = T("lt")
        la = A.f32(4)
        Tla = T("la")
        for i, (a, b) in enumerate((("lambda_q1", "lambda_k1"), ("lambda_q2", "lambda_k2"))):
            oa, _ = self.vp.cols["%s%d" % (a, j)]
            ob, _ = self.vp.cols["%s%d" % (b, j)]
            self.tt("dve", lt, self.vecs[:, oa:oa + 64], self.vecs[:, ob:ob + 64], ALU.mult, [self.Tconst], [Tlt])
            self.reduce(la[:, i:i + 1], lt, [Tlt], [Tla])
        self.act(la[:, 0:2], la[:, 0:2], AF.Exp, [Tla], [Tla])
        neglam = self.lamcols[:, 0:1]
        gq = self.lamcols[:, 1:2]
        gsub = self.lamcols[:, 2:3]
        self.tt("dve", la[:, 2:3], la[:, 1:2], la[:, 0:1], ALU.subtract, [Tla], [Tla])
        self.ts("dve", neglam, la[:, 2:3], -lam_init, None, ALU.add, None, [Tla], [self.Tlam])
        self.ts("dve", gq, self.vcol("q_norm%d" % j), 0.125, None, ALU.mult, None, [self.Tconst], [self.Tlam])
        self.ts("dve", gsub, self.vcol("subln%d" % j), 1.0 - lam_init, None, ALU.mult, None, [self.Tconst], [self.Tlam])

        wq = A.bf16(8 * D).rearrange("p (k n) -> p k n", n=D)
        Tw = T("wq")
        self.load_w(wq, self.w["attn_w_q"][j], D, Tw)
        xb = [A.f32(8 * TL).rearrange("p (c n) -> p c n", n=TL) for _ in range(2)]
        Txb = [T("xb0"), T("xb1")]
        sq = A.bf16(8 * TL).rearrange("p (c n) -> p c n", n=TL)
        Tsq = T("sq")
        h = A.bf16(8 * TL).rearrange("p (c n) -> p c n", n=TL)
        Th = T("h")
        rstd = A.f32(TL)
        Trs = T("rstd")
        sq2 = [A.bf16(TL) for _ in range(2)]
        Tsq2 = [T("sq2a"), T("sq2b")]
        rs2 = [A.f32(TL) for _ in range(2)]
        Trs2 = [T("rs2a"), T("rs2b")]
        qb_ = [A.bf16(8 * TL).rearrange("p (c n) -> p c n", n=TL) for _ in range(2)]
        Tqb = [T("qb0"), T("qb1")]
        for t in range(NTILE):
            xt, Txt = xb[t % 2], Txb[t % 2]
            qb, Tq = qb_[t % 2], Tqb[t % 2]
            P.dma("sp", xt, self.xtile_view(xin, t), writes=[Txt])
            self.rmsnorm(xt, Txt, "attn_norm%d" % j, h, Th, sq, Tsq, rstd, Trs, TL)
            for hh in range(8):
                pb, Tpb = self.bank()
                for kc in range(8):
                    self.mm(pb[:, :], wq[:, kc, hh * 128:(hh + 1) * 128], h[:, kc, :], kc == 0, kc == 7, [Tw, Th], [Tpb])
                self.headnorm(pb, Tpb, gq, qb[:, hh, :], Tq, sq2[hh % 2], Tsq2[hh % 2], rs2[hh % 2], Trs2[hh % 2], TL, 64, self.blk_bf[:, :])
            P.dma("sp", self.QTd.rearrange("h p n -> p h n")[:, :, t * TL:(t + 1) * TL], qb, reads=[Tq, self.Tlam])

        P.barrier()
        A.off = mark
        wo = A.bf16(8 * D).rearrange("p (k n) -> p k n", n=D)
        Two = T("wo")
        self.load_w(wo, self.w["attn_w_o"][j], D, Two)
        ON = A.bf16(8 * SEQ).rearrange("p (h n) -> p h n", n=SEQ)
        TON = [[T("on%d_%d" % (hh, qt)) for qt in range(4)] for hh in range(8)]
        kq = [[A.bf16(SEQ), A.bf16(SEQ), A.bf16(16 * 128).rearrange("p (b e) -> p b e", e=128)] for _ in range(2)]
        Tkq = [T("kq0"), T("kq1")]
        NE = 4
        Eb = [A.bf16(TL) for _ in range(NE)]
        TE = [T("E%d" % i) for i in range(NE)]
        fin = [[A.f32(TL) for _ in range(4)] for _ in range(2)]
        Tfin = [T("fin0"), T("fin1")]
        sqs = A.bf16(TL)
        Tsqs = T("sqs")
        rss = A.f32(TL)
        Trss = T("rss")
        xt = A.f32(8 * TL).rearrange("p (c n) -> p c n", n=TL)
        Txt = T("xt")
        tab = self.t5tab.rearrange("p (h d q) -> p h (d q)", d=2, q=128)
        orb, _ = self.vp.cols["rbT"]
        accb = [0, 1, 2, 3]
        scb = [4, 5, 6, 7]
        it = 0
        hcount = 0
        for s in range(2):
            c0 = s * SEQ
            for hh in range(8):
                Kt, Qt, Vt = kq[hcount % 2]
                Tk = Tkq[hcount % 2]
                hcount += 1
                P.dma("sp", Kt, self.KTd[hh, :, c0:c0 + SEQ], writes=[Tk])
                P.dma("sp", Qt, self.QTd[hh, :, c0:c0 + SEQ], writes=[Tk])
                P.dma("sp", Vt, self.Vd[c0:c0 + SEQ, hh * 128:(hh + 1) * 128].rearrange("(b p) e -> p b e", p=128), writes=[Tk])
                b31 = self.vecs[:, orb + hh * 32 + 31:orb + hh * 32 + 32]
                for qt in range(4):
                    nkb = 4 * qt + 4
                    items = [(kb, c) for kb in range(nkb) for c in range(2)]
                    pend = []

                    def do_pv(item):
                        kb, c, ei, n_lo = item
                        E, TE_ = Eb[ei], TE[ei]
                        N = TL - n_lo
                        ob, sb = accb[c], accb[2 + c]
                        self.mm(self.ps[ob][:, n_lo:TL], Vt[:, kb, :], E[:, :N], kb == 0, kb == nkb - 1, [Tk, TE_], [self.Tps[ob]])
                        self.mm(self.ps[sb][:, n_lo:TL], self.ones_bf[:, :], E[:, :N], kb == 0, kb == nkb - 1, [self.Tconst, TE_], [self.Tps[sb]])

                    for (kb, c) in items:
                        n_lo = max(0, kb * 128 - qt * TL)
                        N = TL - n_lo
                        sbk = scb[it % 4]
                        ei = it % NE
                        it += 1
                        pb, Tpb = self.ps[sbk], self.Tps[sbk]
                        q0 = qt * TL + n_lo
                        self.mm(pb[:, :N], Kt[64 * c:64 * c + 64, kb * 128:(kb + 1) * 128], Qt[64 * c:64 * c + 64, q0:q0 + N], True, True, [Tk], [Tpb])
                        dblk0 = (q0 // 128) - kb
                        if dblk0 == 0:
                            w_ = min(N, 256)
                            self.tt("dve", pb[:, 0:w_], pb[:, 0:w_], tab[:, hh, 0:w_], ALU.add, [Tpb, self.Tt5], [Tpb])
                        elif dblk0 == 1:
                            self.tt("dve", pb[:, 0:128], pb[:, 0:128], tab[:, hh, 128:256], ALU.add, [Tpb, self.Tt5], [Tpb])
                        self.act(Eb[ei][:, :N], pb[:, :N], AF.Exp, [Tpb, self.Tconst], [TE[ei]], bias=b31)
                        pend.append((kb, c, ei, n_lo))
                        if len(pend) > 2:
                            do_pv(pend.pop(0))
                    while pend:
                        do_pv(pend.pop(0))
                    f = fin[qt % 2]
                    Tf = Tfin[qt % 2]
                    r1, r2, o1, o2 = f
                    self.recip(r1, self.ps[accb[2]][:, :], [self.Tps[accb[2]]], [Tf])
                    self.recip(r2, self.ps[accb[3]][:, :], [self.Tps[accb[3]]], [Tf])
                    self.tt("dve", o1, self.ps[accb[0]][:, :], r1, ALU.mult, [self.Tps[accb[0]], Tf], [Tf])
                    self.tt("dve", o2, self.ps[accb[1]][:, :], r2, ALU.mult, [self.Tps[accb[1]], Tf], [Tf])
                    self.stt("dve", o1, o2, neglam, o1, ALU.mult, ALU.add, [Tf, self.Tlam], [Tf])
                    self.act(sqs, o1, AF.Square, [Tf], [Tsqs])
                    sbk = scb[it % 4]
                    it += 1
                    pb, Tpb = self.ps[sbk], self.Tps[sbk]
                    self.mm(pb[:, :], self.ones_bf[:, :], sqs, True, True, [Tsqs, self.Tconst], [Tpb])
                    self.act(rss, pb[:, :], AF.Ln, [Tpb], [Trss], scale=1.0 / 128, bias=EPS)
                    self.act(rss, rss, AF.Exp, [Trss], [Trss], scale=-0.5)
                    self.stt("dve", ON[:, hh, qt * TL:(qt + 1) * TL], o1, gsub, rss, ALU.mult, ALU.mult, [Tf, Trss, self.Tlam], [TON[hh][qt]])
            for qt in range(4):
                t = s * 4 + qt
                P.dma("sp", xt, self.xtile_view(xin, t), writes=[Txt])
                for oc in range(8):
                    pb, Tpb = self.bank()
                    for hh in range(8):
                        self.mm(pb[:, :], wo[:, hh, oc * 128:(oc + 1) * 128], ON[:, hh, qt * TL:(qt + 1) * TL], hh == 0, hh == 7, [Two, TON[hh][qt]], [Tpb])
                    self.tt("dve", xt[:, oc, :], xt[:, oc, :], pb[:, :], ALU.add, [Txt, Tpb], [Txt])
                P.dma("sp", self.xtile_view(xout, t), xt, reads=[Txt])


def make_inputs(inp, core, vp, vecs, ssm, t5):
    x = np.asarray(inp["x"], np.float32)[2 * core:2 * core + 2].reshape(NT, D)
    p = np.asarray(inp["p"], np.float32)[:, 2 * core:2 * core + 2].reshape(DEPTH, NT, 256)
    m = {"xT": np.ascontiguousarray(x.T), "pT": np.ascontiguousarray(p.transpose(0, 2, 1)), "vecs": vecs,
         "t5oh": t5[0], "t5mask0": t5[1]}
    for l in range(N_A):
        m["ssmF%d" % l] = ssm[l][0]
        m["ssmS%d" % l] = ssm[l][1]
    for nm in ("ssm_w_in", "ssm_w_glu", "kv_w", "attn_w_q", "attn_w_o", "ffn_w_up", "ffn_w_down", "ple_w_gate", "ple_w_proj"):
        m[nm] = np.ascontiguousarray(inp[nm], dtype=np.float32)
    return m


_CACHE = {}


def kernel(**inputs):
    inp = {k: np.asarray(v) for k, v in inputs.items()}
    vp = vec_layout(inp)
    vecs = vp.build()
    ssm = [ssm_layouts(inp, l) for l in range(N_A)]
    t5 = t5_consts()
    nc = Builder(vp).build()
    ncores = 8
    in_maps = [make_inputs(inp, c, vp, vecs, ssm, t5) for c in range(ncores)]
    res = run_bass_kernel_spmd(nc, in_maps, core_ids=list(range(ncores)))
    outs = []
    for c in range(ncores):
        yT = np.asarray(res.results[c]["yT"])
        outs.append(yT.T.reshape(2, SEQ, D))
    return np.ascontiguousarray(np.concatenate(outs, axis=0).astype(np.float32))
```

```python
import math
from concourse.bass_utils import run_bass_kernel_spmd
import numpy as np
import concourse.bass as bass
import concourse.mybir as mybir
from contextlib import ExitStack

F32 = mybir.dt.float32
BF16 = mybir.dt.bfloat16
AF = mybir.ActivationFunctionType
ALU = mybir.AluOpType
AX = mybir.AxisListType

ENGS = ("pe", "act", "dve", "pool", "sp")
SAME_ENGINE_SYNC = ("act", "dve", "pool")
NDMA_SEMS = 12


class T:
    __slots__ = ("name", "w", "r")

    def __init__(self, name):
        self.name = name
        self.w = None
        self.r = []


class Op:
    __slots__ = ("eng", "fn", "deps", "is_dma", "signal", "sigval", "sem", "idx", "noinst")

    def __init__(self, eng, fn, is_dma):
        self.eng = eng
        self.fn = fn
        self.deps = []
        self.is_dma = is_dma
        self.signal = False
        self.sigval = 0
        self.sem = None
        self.idx = 0
        self.noinst = False


class Prog:
    def __init__(self, nc):
        self.nc = nc
        self.ops = {e: [] for e in ENGS}
        self.dma_slots = {e: [None] * NDMA_SEMS for e in ENGS}
        self.dma_count = {e: 0 for e in ENGS}
        self.dma_slot_vals = {e: [0] * NDMA_SEMS for e in ENGS}
        self.all_ops = []
        self.last_barrier = None

    def _track(self, op, reads, writes):
        deps = op.deps
        for t in reads:
            if t.w is not None:
                deps.append(t.w)
        for t in writes:
            if t.w is not None:
                deps.append(t.w)
            deps.extend(t.r)
        for t in reads:
            t.r.append(op)
        for t in writes:
            t.w = op
            t.r = []

    def add(self, eng, fn, reads=(), writes=()):
        op = Op(eng, fn, False)
        self._track(op, reads, writes)
        op.idx = len(self.all_ops)
        self.all_ops.append(op)
        self.ops[eng].append(op)
        return op

    def wait_only(self, eng, reads=()):
        op = Op(eng, lambda e: None, False)
        op.noinst = True
        for t in reads:
            if t.w is not None:
                op.deps.append(t.w)
        op.idx = len(self.all_ops)
        self.all_ops.append(op)
        self.ops[eng].append(op)
        return op

    def dma(self, eng, out, in_, reads=(), writes=(), **kw):
        def fn(e):
            return e.dma_start(out=out, in_=in_, **kw)
        op = Op(eng, fn, True)
        self._track(op, reads, writes)
        n = self.dma_count[eng]
        slot = n % NDMA_SEMS
        self.dma_count[eng] = n + 1
        prev = self.dma_slots[eng][slot]
        if prev is not None:
            op.deps.append(prev)
        self.dma_slots[eng][slot] = op
        self.dma_slot_vals[eng][slot] += 16
        op.sem = (eng, slot)
        op.sigval = self.dma_slot_vals[eng][slot]
        op.idx = len(self.all_ops)
        self.all_ops.append(op)
        self.ops[eng].append(op)
        return op

    def barrier(self):
        lasts = []
        for e in ENGS:
            for o in reversed(self.ops[e]):
                if o.noinst or o.is_dma:
                    continue
                lasts.append(o)
                break
        outstanding = [op for e in ENGS for op in self.dma_slots[e] if op is not None]
        for e in ENGS:
            op = Op(e, lambda eng: None, False)
            op.noinst = True
            op.deps = list(lasts) + list(outstanding)
            op.idx = len(self.all_ops)
            self.all_ops.append(op)
            self.ops[e].append(op)

    def emit(self, final_waits=()):
        nc = self.nc
        for op in self.all_ops:
            for d in op.deps:
                if d.is_dma:
                    continue
                if d.eng == op.eng and d.eng not in SAME_ENGINE_SYNC:
                    continue
                d.signal = True
        for e in ENGS:
            c = 0
            for op in self.ops[e]:
                if not op.is_dma:
                    if op.signal:
                        c += 1
                    op.sigval = c
        self.stats = {e: len(self.ops[e]) for e in ENGS}
        with ExitStack() as st:
            esem = {e: st.enter_context(nc.semaphore("s_" + e)) for e in ENGS}
            dsem = {}
            for e in ENGS:
                if self.dma_count[e] > 0:
                    for s in range(min(NDMA_SEMS, self.dma_count[e])):
                        dsem[(e, s)] = st.enter_context(nc.semaphore("d_%s_%d" % (e, s)))
            block = st.enter_context(nc.Block())
            ops = self.ops

            def run(ename, eng):
                waited = {}
                nwait = 0
                for op in ops[ename]:
                    need = {}
                    for d in op.deps:
                        if d.is_dma:
                            key = ("d",) + d.sem
                            v = d.sigval
                        else:
                            if d.eng == ename and ename not in SAME_ENGINE_SYNC:
                                continue
                            key = ("e", d.eng)
                            v = d.sigval
                        if waited.get(key, 0) >= v:
                            continue
                        if need.get(key, 0) < v:
                            need[key] = v
                    for key, v in need.items():
                        sem = dsem[key[1:]] if key[0] == "d" else esem[key[1]]
                        eng.wait_ge(sem, v)
                        waited[key] = v
                        nwait += 1
                    ins = op.fn(eng)
                    if ins is None:
                        continue
                    if op.is_dma:
                        ins.then_inc(dsem[op.sem], 16)
                    elif op.signal:
                        ins.then_inc(esem[ename], 1)
                self.stats["wait_" + ename] = nwait

            @block.tensor
            def _(eng):
                run("pe", eng)

            @block.scalar
            def _(eng):
                run("act", eng)

            @block.vector
            def _(eng):
                run("dve", eng)

            @block.gpsimd
            def _(eng):
                run("pool", eng)

            @block.sync
            def _(eng):
                run("sp", eng)


NT = 4096
SEQ = 2048
TL = 512
NTILE = NT // TL
D = 1024
DFF = 2816
NFC = 22
EPS = 1e-6
TWO_PI = 2.0 * math.pi
MAGIC = 12582912.0
NEG = -30000.0
N_A = 2
DEPTH = 4


def colv(v):
    v = np.asarray(v, np.float32)
    return np.ascontiguousarray(v.reshape(-1, 128).T)


class VecPack:
    def __init__(self):
        self.cols = {}
        self.parts = []
        self.n = 0

    def add(self, name, arr):
        arr = np.ascontiguousarray(arr, dtype=np.float32)
        assert arr.shape[0] == 128
        self.cols[name] = (self.n, arr.shape[1])
        self.parts.append(arr)
        self.n += arr.shape[1]

    def build(self):
        return np.ascontiguousarray(np.concatenate(self.parts, axis=1))


def t5_bucket_np(n):
    n = np.maximum(n, 0)
    nf = np.maximum(n, 16).astype(np.float32)
    large = 16 + (np.log(nf / np.float32(16)) / np.float32(math.log(128 / 16)) * np.float32(16)).astype(np.int32)
    large = np.minimum(large, 31)
    return np.where(n < 16, n, large)


def vec_layout(inp):
    vp = VecPack()
    for l in range(N_A):
        vp.add("ssm_norm%d" % l, colv(inp["ssm_norm"][l]))
        vp.add("ssm_d%d" % l, colv(inp["ssm_d"][l]))
    vp.add("kv_norm", colv(inp["kv_norm"]))
    for j in range(2):
        vp.add("attn_norm%d" % j, colv(inp["attn_norm"][j]))
    for l in range(DEPTH):
        vp.add("ffn_norm%d" % l, colv(inp["ffn_norm"][l]))
        vp.add("ple_norm%d" % l, colv(inp["ple_norm"][l]))
        cw = np.concatenate([colv(inp["ffn_conv_w"][l][j]) for j in range(3)], axis=1)
        vp.add("conv_w%d" % l, cw)
        vp.add("conv_b%d" % l, colv(inp["ffn_conv_b"][l]))
    vp.add("k_norm", np.tile(np.asarray(inp["k_norm"], np.float32), 2)[:, None])
    for j in range(2):
        vp.add("q_norm%d" % j, np.tile(np.asarray(inp["q_norm"][j], np.float32), 2)[:, None])
        vp.add("subln%d" % j, np.asarray(inp["subln"][j], np.float32)[:, None])
        for nm in ("lambda_q1", "lambda_k1", "lambda_q2", "lambda_k2"):
            vp.add("%s%d" % (nm, j), np.broadcast_to(np.asarray(inp[nm][j], np.float32)[None, :], (128, 64)))
    vp.add("rbT", np.broadcast_to(np.asarray(inp["rel_bias"], np.float32).T.reshape(1, 256), (128, 256)))
    sgn = np.concatenate([-np.ones(64, np.float32), np.ones(64, np.float32)])[:, None]
    vp.add("sgn", sgn)
    vp.add("nsgn", -sgn)
    mask8 = (np.arange(128)[:, None] // 16 == np.arange(8)[None, :]).astype(np.float32)
    vp.add("mask8", mask8)
    return vp


def ssm_layouts(inp, l):
    lre = np.asarray(inp["ssm_lambda_re"][l], np.float32)
    lim = np.asarray(inp["ssm_lambda_im"][l], np.float32)
    ldt = np.asarray(inp["ssm_log_dt"][l], np.float32)
    bre = np.asarray(inp["ssm_b_re"][l], np.float32)
    bim = np.asarray(inp["ssm_b_im"][l], np.float32)
    cre = np.asarray(inp["ssm_c_re"][l], np.float32)
    cim = np.asarray(inp["ssm_c_im"][l], np.float32)
    def F_gm(a):
        a4 = a.reshape(8, 8, 64)
        a4 = np.repeat(a4[:, :, None, :], 16, axis=2)
        return a4.transpose(1, 2, 0, 3).reshape(128, 8, 64)
    def F_b(b):
        b4 = b.reshape(8, 8, 64, 16)
        return b4.transpose(1, 3, 0, 2).reshape(128, 8, 64)
    ldtF = np.broadcast_to(ldt[:, None], (64, 64))
    ssmF = np.stack([F_gm(lre), F_gm(lim), F_b(bre), F_b(bim), F_gm(ldtF)], axis=1)
    ssmF = np.ascontiguousarray(ssmF.reshape(128, 5 * 512), dtype=np.float32)
    def S_gm(a):
        return np.concatenate([a.T, a.T], axis=0)
    ccS = np.concatenate([cre.transpose(2, 0, 1), cim.transpose(2, 0, 1)], axis=0)
    ccS2 = np.concatenate([cim.transpose(2, 0, 1), cre.transpose(2, 0, 1)], axis=0)
    bSre = np.concatenate([bre.transpose(1, 0, 2), bre.transpose(1, 0, 2)], axis=0)
    bSim = np.concatenate([bim.transpose(1, 0, 2), bim.transpose(1, 0, 2)], axis=0)
    ssmS = np.concatenate([S_gm(lre), S_gm(lim), S_gm(ldtF), ccS.reshape(128, 1024), ccS2.reshape(128, 1024),
                           bSre.reshape(128, 1024), bSim.reshape(128, 1024)], axis=1)
    return ssmF, np.ascontiguousarray(ssmS, dtype=np.float32)


def t5_consts():
    k = np.arange(128)[:, None]
    q = np.arange(128)[None, :]
    oh = np.zeros((2, 128, 128, 32), np.float32)
    for dd in range(2):
        n = q - k + 128 * dd
        b = t5_bucket_np(n)
        oh[dd] = (b[:, :, None] == np.arange(32)[None, None, :]).astype(np.float32)
    mask0 = np.where(q >= k, 0.0, NEG).astype(np.float32)
    return np.ascontiguousarray(oh.reshape(2, 128, 4096)), mask0


class Arena:
    def __init__(self, ap, n):
        self.ap = ap
        self.n = n
        self.off = 0

    def f32(self, n):
        a = self.ap[:, self.off:self.off + n]
        self.off += n
        assert self.off <= self.n, ("arena overflow", self.off, self.n)
        return a

    def bf16(self, n):
        m = (n + 1) // 2
        a = self.ap[:, self.off:self.off + m].bitcast(BF16)
        self.off += m
        assert self.off <= self.n, ("arena overflow", self.off, self.n)
        return a


class Builder:
    def __init__(self, vp, stages=None, dump=None):
        self.vp = vp
        self.nc = bass.Bass("TRN2", target_bir_lowering=False)
        self.stages = stages
        self.bank_rr = 0

    def mm(self, out, lhsT, rhs, start, stop, reads, writes):
        self.P.add("pe", lambda e: e.matmul(out, lhsT, rhs, start=start, stop=stop), reads, writes)

    def act(self, out, in_, func, reads, writes, scale=1.0, bias=0.0):
        self.P.add("act", lambda e: e.activation(out=out, in_=in_, func=func, scale=scale, bias=bias), reads, writes)

    def stt(self, eng, out, in0, scalar, in1, op0, op1, reads, writes):
        self.P.add(eng, lambda e: e.scalar_tensor_tensor(out=out, in0=in0, scalar=scalar, in1=in1, op0=op0, op1=op1), reads, writes)

    def tt(self, eng, out, in0, in1, op, reads, writes):
        self.P.add(eng, lambda e: e.tensor_tensor(out=out, in0=in0, in1=in1, op=op), reads, writes)

    def ts(self, eng, out, in0, s1, s2, op0, op1, reads, writes):
        if s2 is None:
            self.P.add(eng, lambda e: e.tensor_scalar(out=out, in0=in0, scalar1=s1, scalar2=None, op0=op0), reads, writes)
        else:
            self.P.add(eng, lambda e: e.tensor_scalar(out=out, in0=in0, scalar1=s1, scalar2=s2, op0=op0, op1=op1), reads, writes)

    def cp(self, eng, out, in_, reads, writes):
        self.P.add(eng, lambda e: e.tensor_copy(out=out, in_=in_), reads, writes)

    def memset(self, eng, ap, val, writes):
        self.P.add(eng, lambda e: e.memset(ap, val), (), writes)

    def reduce(self, out, in_, reads, writes):
        self.P.add("dve", lambda e: e.tensor_reduce(out=out, in_=in_, axis=AX.X, op=ALU.add), reads, writes)

    def recip(self, out, in_, reads, writes):
        self.P.add("dve", lambda e: e.reciprocal(out=out, in_=in_), reads, writes)

    def vcol(self, name, i=0, n=1):
        o, w = self.vp.cols[name]
        return self.vecs[:, o + i:o + i + n]

    def bank(self):
        b = self.bank_rr % 8
        self.bank_rr += 1
        return self.ps[b], self.Tps[b]

    def load_w(self, dst3, src2, ncols, T_, c0=0, step=512):
        sv = src2.rearrange("(kc p) n -> p kc n", p=128)
        for a in range(0, ncols, step):
            b = min(ncols, a + step)
            self.P.dma("pool", dst3[:, :, a:b], sv[:, :, c0 + a:c0 + b], writes=[T_])

    def rmsnorm(self, xt, Txt, gname, h, Th, sq, Tsq, rstd, Trstd, N):
        Tsql = Tsq if isinstance(Tsq, list) else [Tsq]
        self.act(sq.rearrange("p c n -> p (c n)"), xt.rearrange("p c n -> p (c n)"), AF.Square, [Txt], Tsql)
        pb, Tpb = self.bank()
        for c in range(8):
            self.mm(pb[:, :N], self.ones_bf[:, :], sq[:, c, :], c == 0, c == 7, Tsql + [self.Tconst], [Tpb])
        self.act(rstd, pb[:, :N], AF.Ln, [Tpb], [Trstd], scale=1.0 / D, bias=EPS)
        self.act(rstd, rstd, AF.Exp, [Trstd], [Trstd], scale=-0.5)
        for c in range(8):
            self.stt("dve", h[:, c, :], xt[:, c, :], self.vcol(gname, c), rstd, ALU.mult, ALU.mult, [Txt, Trstd], [Th])

    def headnorm(self, pb, Tpb, gcol, out, Tout, sq, Tsq, rstd, Trstd, N, dim, ones_mat):
        self.act(sq, pb[:, :N], AF.Square, [Tpb], [Tsq])
        p2, Tp2 = self.bank()
        self.mm(p2[:, :N], ones_mat, sq, True, True, [Tsq, self.Tconst], [Tp2])
        self.act(rstd, p2[:, :N], AF.Ln, [Tp2], [Trstd], scale=1.0 / dim, bias=EPS)
        self.act(rstd, rstd, AF.Exp, [Trstd], [Trstd], scale=-0.5)
        self.stt("dve", out, pb[:, :N], gcol, rstd, ALU.mult, ALU.mult, [Tpb, Trstd], [Tout])

    def build(self, upto=None):
        nc = self.nc
        st = ExitStack()
        with st:
            def din(name, shape, dt=F32):
                return nc.dram_tensor(name, list(shape), dt, kind="ExternalInput").ap()
            self.xT = din("xT", [D, NT])
            self.pT = din("pT", [DEPTH, 256, NT])
            self.vecs_d = din("vecs", [128, self.vp.n])
            self.ssmF_d = [din("ssmF%d" % l, [128, 2560]) for l in range(N_A)]
            self.ssmS_d = [din("ssmS%d" % l, [128, 4288]) for l in range(N_A)]
            self.oh_d = din("t5oh", [2, 128, 4096])
            self.mask0_d = din("t5mask0", [128, 128])
            self.w = {}
            for nm, shp in (("ssm_w_in", [2, D, D]), ("ssm_w_glu", [2, D, 2 * D]), ("kv_w", [D, 2 * D]),
                            ("attn_w_q", [2, D, D]), ("attn_w_o", [2, D, D]), ("ffn_w_up", [4, D, 2 * DFF]),
                            ("ffn_w_down", [4, DFF, D]), ("ple_w_gate", [4, D, D]), ("ple_w_proj", [4, 256, D])):
                self.w[nm] = din(nm, shp)
            self.yT = nc.dram_tensor("yT", [D, NT], F32, kind="ExternalOutput").ap()
            self.XT = nc.dram_tensor("XTs", [D, NT], F32).ap()
            self.KTd = nc.dram_tensor("KTs", [8, 128, NT], BF16).ap()
            self.QTd = nc.dram_tensor("QTs", [8, 128, NT], BF16).ap()
            self.Vd = nc.dram_tensor("Vs", [NT, D], BF16).ap()

            NARENA = 52800
            arena_t = st.enter_context(nc.sbuf_tensor("arena", [128, NARENA], F32))
            self.A = Arena(arena_t[:, :], NARENA)
            self.ps = [st.enter_context(nc.psum_tensor("ps%d" % i, [128, 512], F32)) for i in range(8)]
            self.Tps = [T("ps%d" % i) for i in range(8)]
            self.P = Prog(nc)
            P = self.P
            A = self.A
            self.Tconst = T("const")
            self.vecs = A.f32(self.vp.n)
            P.dma("sp", self.vecs, self.vecs_d, writes=[self.Tconst])
            self.ones_bf = A.bf16(128)
            self.memset("pool", self.ones_bf, 1.0, [self.Tconst])
            self.blk_bf = A.bf16(128)
            self.memset("pool", self.blk_bf, 0.0, [self.Tconst])
            self.memset("pool", self.blk_bf[0:64, 0:64], 1.0, [self.Tconst])
            self.memset("pool", self.blk_bf[64:128, 64:128], 1.0, [self.Tconst])
            self.t5tab = A.f32(8 * 2 * 128)
            self.Tt5 = T("t5tab")
            self.lamcols = A.f32(8)
            self.Tlam = T("lam")
            self.persist_off = A.off

            stages = []
            for l in range(DEPTH):
                if l < N_A:
                    stages.append(("A0_%d" % l, lambda xi, xo, l=l: self.stage_A(l, xi, xo)))
                else:
                    if l == N_A:
                        stages.append(("T5", lambda xi, xo: self.stage_T5()))
                        stages.append(("KV", lambda xi, xo: self.stage_KV(xi)))
                    stages.append(("B_%d" % l, lambda xi, xo, l=l: self.stage_B(l, xi, xo)))
                stages.append(("FFN_%d" % l, lambda xi, xo, l=l: self.stage_FFN(l, xi, xo)))
                stages.append(("PLE_%d" % l, lambda xi, xo, l=l: self.stage_PLE(l, xi, xo)))
            if upto is not None:
                stages = stages[:upto]
            mod = [i for i, (n, f) in enumerate(stages) if not (n.startswith("T5") or n.startswith("KV"))]
            first, last = mod[0], mod[-1]
            for i, (n, f) in enumerate(stages):
                xi = self.xT if i <= first else self.XT
                xo = self.yT if i == last else self.XT
                if i > last:
                    xi = self.yT
                P.barrier()
                A.off = self.persist_off
                f(xi, xo)
            P.barrier()
            P.emit()
            print("ops", P.stats)
        return nc

    def xtile_view(self, xd, t):
        return xd.rearrange("(c p) n -> p c n", p=128)[:, :, t * TL:(t + 1) * TL]

    def stage_A(self, l, xin, xout):
        P, A = self.P, self.A
        coef = A.f32(64 * 11 * 3).rearrange("p (g k j) -> p g k j", k=11, j=3)
        U = A.bf16(8 * NT).rearrange("p (c n) -> p c n", n=NT)
        abF = A.f32(1024)
        bbF = A.f32(1024)
        pwS = A.f32(2 * 9 * 64).rearrange("p (t k g) -> p t k g", t=2, k=9)
        ccS = A.f32(1024).rearrange("p (g c) -> p g c", c=16)
        ccS2 = A.f32(1024).rearrange("p (g c) -> p g c", c=16)
        bbS = A.f32(1024).rearrange("p (g c) -> p g c", c=16)
        Ttab = T("ssmtab")
        Tcoef = T("coef")
        mark = A.off
        sF = A.f32(2560)
        TF = T("sF")
        P.dma("sp", sF, self.ssmF_d[l], writes=[TF])
        lre, lim, bre, bim, ldt = [sF[:, i * 512:(i + 1) * 512] for i in range(5)]
        tmpF = [A.f32(512) for _ in range(8)]
        Tt = [T("tF%d" % i) for i in range(8)]

        def trig(theta, Tth, lrdt, Tlr, out_r, out_i, To, tA, TA, tB, TB):
            self.ts("dve", tA, theta, 1.0 / TWO_PI, MAGIC, ALU.mult, ALU.add, [Tth], [TA])
            self.ts("dve", tA, tA, -MAGIC, None, ALU.add, None, [TA], [TA])
            self.stt("dve", tA, tA, -TWO_PI, theta, ALU.mult, ALU.add, [TA, Tth], [TA])
            self.ts("dve", tA, tA, 3.1415925, -3.1415925, ALU.min, ALU.max, [TA], [TA])
            self.act(out_i, tA, AF.Sin, [TA], [To])
            self.act(tB, tA, AF.Abs, [TA], [TB])
            self.act(out_r, tB, AF.Sin, [TB], [To], scale=-1.0, bias=math.pi / 2)
            self.act(tB, lrdt, AF.Exp, [Tlr], [TB])
            self.tt("dve", out_r, out_r, tB, ALU.mult, [To, TB], [To])
            self.tt("dve", out_i, out_i, tB, ALU.mult, [To, TB], [To])

        def fcoef(lr_, li_, Tl, a_r, a_i, Ta_, t5, T5_, t6, T6_, t7, T7_, t8, T8_, t9, T9_):
            self.tt("dve", t5, lr_, lr_, ALU.mult, [Tl], [T5_])
            self.tt("dve", t6, li_, li_, ALU.mult, [Tl], [T6_])
            self.tt("dve", t5, t5, t6, ALU.add, [T5_, T6_], [T5_])
            self.recip(t5, t5, [T5_], [T5_])
            self.ts("dve", t6, a_r, -1.0, None, ALU.add, None, [Ta_], [T6_])
            self.tt("dve", t7, t6, lr_, ALU.mult, [T6_, Tl], [T7_])
            self.tt("dve", t8, a_i, li_, ALU.mult, [Ta_, Tl], [T8_])
            self.tt("dve", t7, t7, t8, ALU.add, [T7_, T8_], [T7_])
            self.tt("dve", t7, t7, t5, ALU.mult, [T7_, T5_], [T7_])
            self.tt("dve", t8, a_i, lr_, ALU.mult, [Ta_, Tl], [T8_])
            self.tt("dve", t9, t6, li_, ALU.mult, [T6_, Tl], [T9_])
            self.tt("dve", t8, t8, t9, ALU.subtract, [T8_, T9_], [T8_])
            self.tt("dve", t8, t8, t5, ALU.mult, [T8_, T5_], [T8_])

        dt, lrdt, th, t5, t6, t7, t8, t9 = tmpF
        Tdt, Tlrdt, Tth, T5_, T6_, T7_, T8_, T9_ = Tt
        abr, abi = abF[:, 0:512], abF[:, 512:1024]
        bbr, bbi = bbF[:, 0:512], bbF[:, 512:1024]
        self.act(dt, ldt, AF.Exp, [TF], [Tdt])
        self.tt("dve", lrdt, lre, dt, ALU.mult, [TF, Tdt], [Tlrdt])
        self.tt("dve", th, lim, dt, ALU.mult, [TF, Tdt], [Tth])
        trig(th, Tth, lrdt, Tlrdt, abr, abi, Ttab, t5, T5_, t6, T6_)
        fcoef(lre, lim, TF, abr, abi, Ttab, t5, T5_, t6, T6_, t7, T7_, t8, T8_, t9, T9_)
        self.tt("dve", t5, t7, bre, ALU.mult, [T7_, TF], [T5_])
        self.tt("dve", t6, t8, bim, ALU.mult, [T8_, TF], [T6_])
        self.tt("dve", bbr, t5, t6, ALU.subtract, [T5_, T6_], [Ttab])
        self.tt("dve", t5, t7, bim, ALU.mult, [T7_, TF], [T5_])
        self.tt("dve", t6, t8, bre, ALU.mult, [T8_, TF], [T6_])
        self.tt("dve", bbi, t5, t6, ALU.add, [T5_, T6_], [Ttab])
        sS = A.f32(4288)
        TS = T("sS")
        P.dma("sp", sS, self.ssmS_d[l], writes=[TS])
        lreS, limS, ldtS = sS[:, 0:64], sS[:, 64:128], sS[:, 128:192]
        cc_in = sS[:, 192:1216].rearrange("p (g c) -> p g c", c=16)
        cc2_in = sS[:, 1216:2240].rearrange("p (g c) -> p g c", c=16)
        bSre = sS[:, 2240:3264].rearrange("p (g c) -> p g c", c=16)
        bSim = sS[:, 3264:4288].rearrange("p (g c) -> p g c", c=16)
        self.cp("dve", ccS, cc_in, [TS], [Ttab])
        self.cp("dve", ccS2, cc2_in, [TS], [Ttab])
        s_ = [A.f32(64) for _ in range(10)]
        Ts_ = [T("tS%d" % i) for i in range(10)]
        dtS, lrdtS, thS, u5, u6, u7, u8, u9, ar, ai = s_
        TdtS, TlrS, TthS, Tu5, Tu6, Tu7, Tu8, Tu9, Ta, _ = Ts_
        self.act(dtS, ldtS, AF.Exp, [TS], [TdtS])
        self.tt("dve", lrdtS, lreS, dtS, ALU.mult, [TS, TdtS], [TlrS])
        self.tt("dve", thS, limS, dtS, ALU.mult, [TS, TdtS], [TthS])
        trig(thS, TthS, lrdtS, TlrS, ar, ai, Ta, u5, Tu5, u6, Tu6)
        self.memset("dve", pwS[:, 0, 0, :], 1.0, [Ttab])
        self.memset("dve", pwS[:, 1, 0, :], 0.0, [Ttab])
        self.cp("dve", pwS[:, 0, 1, :], ar, [Ta], [Ttab])
        self.cp("dve", pwS[:, 1, 1, :], ai, [Ta], [Ttab])
        for k in range(1, 8):
            pr, pi = pwS[:, 0, k, :], pwS[:, 1, k, :]
            nr, ni = pwS[:, 0, k + 1, :], pwS[:, 1, k + 1, :]
            self.tt("dve", u5, pr, ar, ALU.mult, [Ttab, Ta], [Tu5])
            self.tt("dve", u6, pi, ai, ALU.mult, [Ttab, Ta], [Tu6])
            self.tt("dve", nr, u5, u6, ALU.subtract, [Tu5, Tu6], [Ttab])
            self.tt("dve", u5, pr, ai, ALU.mult, [Ttab, Ta], [Tu5])
            self.tt("dve", u6, pi, ar, ALU.mult, [Ttab, Ta], [Tu6])
            self.tt("dve", ni, u5, u6, ALU.add, [Tu5, Tu6], [Ttab])
        fcoef(lreS, limS, TS, ar, ai, Ta, u5, Tu5, u6, Tu6, u7, Tu7, u8, Tu8, u9, Tu9)
        fr_b = u7.unsqueeze(2).broadcast_to([128, 64, 16])
        fi_b = u8.unsqueeze(2).broadcast_to([128, 64, 16])
        w1 = A.f32(1024).rearrange("p (g c) -> p g c", c=16)
        w2 = A.f32(1024).rearrange("p (g c) -> p g c", c=16)
        w3 = A.f32(1024).rearrange("p (g c) -> p g c", c=16)
        Tw1, Tw2, Tw3 = T("w1"), T("w2"), T("w3")
        self.tt("dve", w1, bSre, fr_b, ALU.mult, [TS, Tu7], [Tw1])
        self.tt("dve", w2, bSim, fi_b, ALU.mult, [TS, Tu8], [Tw2])
        self.tt("dve", w3, w1, w2, ALU.subtract, [Tw1, Tw2], [Tw3])
        self.cp("dve", bbS[0:64], w3[0:64], [Tw3], [Ttab])
        self.tt("dve", w1, bSim, fr_b, ALU.mult, [TS, Tu7], [Tw1])
        self.tt("dve", w2, bSre, fi_b, ALU.mult, [TS, Tu8], [Tw2])
        self.tt("dve", w3, w1, w2, ALU.add, [Tw1, Tw2], [Tw3])
        self.cp("dve", bbS[64:128], w3[64:128], [Tw3], [Ttab])
        for k in range(11):
            self.cp("dve", coef[:, :, k, 0], ar, [Ta], [Tcoef])
            self.ts("dve", coef[:, :, k, 1], ai, self.vcol("sgn"), None, ALU.mult, None, [Ta, self.Tconst], [Tcoef])
            self.ts("dve", coef[:, :, k, 2], ai, self.vcol("nsgn"), None, ALU.mult, None, [Ta, self.Tconst], [Tcoef])
            if k < 10:
                self.tt("dve", u5, ar, ar, ALU.mult, [Ta], [Tu5])
                self.tt("dve", u6, ai, ai, ALU.mult, [Ta], [Tu6])
                self.tt("dve", u9, ar, ai, ALU.mult, [Ta], [Tu9])
                self.tt("dve", ar, u5, u6, ALU.subtract, [Tu5, Tu6], [Ta])
                self.ts("dve", ai, u9, 2.0, None, ALU.mult, None, [Tu9], [Ta])

        if getattr(self, "substop", 9) <= 0:
            return
        P.barrier()
        A.off = mark
        win = A.bf16(8 * D).rearrange("p (k n) -> p k n", n=D)
        Tw = T("win")
        self.load_w(win, self.w["ssm_w_in"][l], D, Tw)
        xb = [A.f32(8 * TL).rearrange("p (c n) -> p c n", n=TL) for _ in range(2)]
        Txb = [T("xb0"), T("xb1")]
        sq = A.bf16(8 * TL).rearrange("p (c n) -> p c n", n=TL)
        Tsq = T("sq")
        h = A.bf16(8 * TL).rearrange("p (c n) -> p c n", n=TL)
        Th = T("h")
        rstd = A.f32(TL)
        Trs = T("rstd")
        TU = [[T("U%d_%d" % (c, t)) for t in range(NTILE)] for c in range(8)]
        for t in range(NTILE):
            xt, Txt = xb[t % 2], Txb[t % 2]
            P.dma("sp", xt, self.xtile_view(xin, t), writes=[Txt])
            self.rmsnorm(xt, Txt, "ssm_norm%d" % l, h, Th, sq, Tsq, rstd, Trs, TL)
            for oc in range(8):
                pb, Tpb = self.bank()
                for kc in range(8):
                    self.mm(pb[:, :], win[:, kc, oc * 128:(oc + 1) * 128], h[:, kc, :], kc == 0, kc == 7, [Tw, Th], [Tpb])
                self.act(U[:, oc, t * TL:(t + 1) * TL], pb[:, :], AF.Copy, [Tpb], [TU[oc][t]])

        if getattr(self, "substop", 9) <= 1:
            return
        P.barrier()
        A.off = mark
        NG = 4
        BtK = A.bf16(8 * NG * 128).rearrange("p (s g m) -> p s g m", s=8, g=NG)
        BtKs = A.bf16(8 * NG * 128).rearrange("p (s g m) -> p s g m", s=8, g=NG)
        TBt = T("BtK")
        CAp = A.bf16(9 * 1152).rearrange("p (k x) -> p k x", x=1152)
        TCA = T("CAp")
        bbSp = A.bf16(1152)
        TbSp = T("bbSp")
        Klag = A.bf16(8 * 128).rearrange("p (k x) -> p k x", x=128)
        TK = T("Klag")
        Xall = A.f32(NG * 512).rearrange("p (g n) -> p g n", n=512)
        Xsall = A.f32(NG * 512).rearrange("p (g n) -> p g n", n=512)
        TXg = [T("X%d" % i) for i in range(NG)]
        TXsg = [T("Xs%d" % i) for i in range(NG)]
        Xp = A.bf16(8 * 512).rearrange("p (g n) -> p g n", n=512)
        TXp = [T("Xp%d" % i) for i in range(8)]
        Ysb = A.f32(NT)
        TY = T("Ysb")
        bk = [A.f32(128) for _ in range(2)]
        bks = A.f32(128)
        Tbk = [T("bk0"), T("bk1")]
        Tbks = T("bks")
        q5, q6 = A.f32(64), A.f32(64)
        Tq5, Tq6 = T("q5"), T("q6")
        tA = A.f32(128).rearrange("p (g c) -> p g c", c=16)
        tB = A.f32(128).rearrange("p (g c) -> p g c", c=16)
        TtA, TtB = T("tA"), T("tB")
        self.memset("pool", CAp.rearrange("p k x -> p (k x)"), 0.0, [TCA])
        self.memset("pool", bbSp, 0.0, [TbSp])
        self.memset("pool", Xp.rearrange("p g n -> p (g n)"), 0.0, TXp)
        abr3 = abr.rearrange("p (c m) -> p c m", m=64)
        abi3 = abi.rearrange("p (c m) -> p c m", m=64)
        bbr3 = bbr.rearrange("p (c m) -> p c m", m=64)
        bbi3 = bbi.rearrange("p (c m) -> p c m", m=64)
        for ch in range(getattr(self, "nch", 8)):
            Uc = U[:, ch, :].rearrange("p (q s) -> p s q", s=8)
            for lag in range(9):
                prb = pwS[:, 0, lag, ch * 8:(ch + 1) * 8].unsqueeze(2).broadcast_to([128, 8, 16])
                pib = pwS[:, 1, lag, ch * 8:(ch + 1) * 8].unsqueeze(2).broadcast_to([128, 8, 16])
                self.tt("dve", tA, ccS[:, ch * 8:(ch + 1) * 8, :], prb, ALU.mult, [Ttab], [TtA])
                self.tt("dve", tB, ccS2[:, ch * 8:(ch + 1) * 8, :], pib, ALU.mult, [Ttab], [TtB])
                dv = CAp[:, lag, :].rearrange("p (g x) -> p g x", x=144)[:, :, 0:16]
                self.stt("dve", dv, tA, self.vcol("nsgn"), tB, ALU.mult, ALU.subtract, [TtA, TtB, self.Tconst], [TCA])
            self.cp("dve", bbSp.rearrange("p (g x) -> p g x", x=144)[:, :, 0:16], bbS[:, ch * 8:(ch + 1) * 8, :], [Ttab], [TbSp])
            for lag in range(8):
                pb, Tpb = self.bank()
                for g8 in range(8):
                    self.mm(pb[:, 0:128], bbSp[:, g8 * 128:(g8 + 1) * 128], CAp[:, lag, g8 * 128:(g8 + 1) * 128], g8 == 0, g8 == 7, [TbSp, TCA], [Tpb])
                self.act(Klag[:, lag, :], pb[:, 0:128], AF.Copy, [Tpb], [TK])
            for half in range(2):
                cur = 0
                self.cp("dve", bk[0][:, 0:64], bbr3[:, ch, :], [Ttab], [Tbk[0]])
                self.cp("dve", bk[0][:, 64:128], bbi3[:, ch, :], [Ttab], [Tbk[0]])
                for k in range(8):
                    s_i = 7 - k
                    b_, Tb_ = bk[cur], Tbk[cur]
                    self.cp("dve", bks[:, 0:64], b_[:, 64:128], [Tb_], [Tbks])
                    self.cp("dve", bks[:, 64:128], b_[:, 0:64], [Tb_], [Tbks])
                    for gi in range(NG):
                        g8 = half * NG + gi
                        self.ts("pool", BtK[:, s_i, gi, :], b_, self.vcol("mask8", g8), None, ALU.mult, None, [Tb_, self.Tconst], [TBt])
                        self.ts("pool", BtKs[:, s_i, gi, :], bks, self.vcol("mask8", g8), None, ALU.mult, None, [Tbks, self.Tconst], [TBt])
                    if k < 7:
                        n_, Tn_ = bk[1 - cur], Tbk[1 - cur]
                        self.tt("dve", q5, b_[:, 0:64], abr3[:, ch, :], ALU.mult, [Tb_, Ttab], [Tq5])
                        self.tt("dve", q6, b_[:, 64:128], abi3[:, ch, :], ALU.mult, [Tb_, Ttab], [Tq6])
                        self.tt("dve", n_[:, 0:64], q5, q6, ALU.subtract, [Tq5, Tq6], [Tn_])
                        self.tt("dve", q5, b_[:, 0:64], abi3[:, ch, :], ALU.mult, [Tb_, Ttab], [Tq5])
                        self.tt("dve", q6, b_[:, 64:128], abr3[:, ch, :], ALU.mult, [Tb_, Ttab], [Tq6])
                        self.tt("dve", n_[:, 64:128], q5, q6, ALU.add, [Tq5, Tq6], [Tn_])
                        cur = 1 - cur
                for gi in range(NG):
                    for (bt, dst, Td) in ((BtK, Xall, TXg[gi]), (BtKs, Xsall, TXsg[gi])):
                        pb, Tpb = self.bank()
                        for s_i in range(8):
                            self.mm(pb[:, :], bt[:, s_i, gi, :], Uc[:, s_i, :], s_i == 0, s_i == 7, [TBt] + TU[ch], [Tpb])
                        self.act(dst[:, gi, :], pb[:, :], AF.Copy, [Tpb], [Td])
                def lvl(gi, tsl, ssl, q, k):
                    g = ch * 8 + half * NG + gi
                    p1 = coef[:, g, k + 3, 0:1]
                    p2 = coef[:, g, k + 3, 1:2]
                    p2s = coef[:, g, k + 3, 2:3]
                    Xv = Xall[:, gi, :].rearrange("p (s m q) -> p s m q", s=2, q=q)
                    Xsv = Xsall[:, gi, :].rearrange("p (s m q) -> p s m q", s=2, q=q)
                    tX, sX = Xv[:, :, tsl[0], tsl[1]], Xv[:, :, ssl[0], ssl[1]]
                    tXs, sXs = Xsv[:, :, tsl[0], tsl[1]], Xsv[:, :, ssl[0], ssl[1]]
                    TX, TXs = TXg[gi], TXsg[gi]
                    self.stt("dve", tX, sX, p1, tX, ALU.mult, ALU.add, [TX, Tcoef], [TX])
                    self.stt("dve", tX, sXs, p2, tX, ALU.mult, ALU.add, [TX, TXs, Tcoef], [TX])
                    self.stt("dve", tXs, sXs, p1, tXs, ALU.mult, ALU.add, [TXs, Tcoef], [TXs])
                    self.stt("dve", tXs, sX, p2s, tXs, ALU.mult, ALU.add, [TX, TXs, Tcoef], [TXs])
                for k in range(8):
                    blk, d = 2 << k, 1 << k
                    for gi in range(NG):
                        lvl(gi, (slice(None), blk - 1), (slice(None), d - 1), blk, k)
                for k in range(6, -1, -1):
                    blk, d = 2 << k, 1 << k
                    M = 256 // blk
                    for gi in range(NG):
                        lvl(gi, (slice(1, M), d - 1), (slice(0, M - 1), blk - 1), blk, k)
                for gi in range(NG):
                    g8 = half * NG + gi
                    self.act(Xp[:, g8, :].rearrange("p (s j) -> p s j", s=2)[:, :, 1:256],
                             Xall[:, gi, :].rearrange("p (s j) -> p s j", s=2)[:, :, 0:255], AF.Copy, [TXg[gi]], [TXp[g8]])
            Yv = Ysb.rearrange("p (q s) -> p s q", s=8)
            for r in range(8):
                pb, Tpb = self.bank()
                for g8 in range(8):
                    self.mm(pb[:, :], CAp[:, r + 1, g8 * 128:(g8 + 1) * 128], Xp[:, g8, :], g8 == 0, False, [TCA, TXp[g8]], [Tpb])
                for s_i in range(r + 1):
                    self.mm(pb[:, :], Klag[:, r - s_i, :], Uc[:, s_i, :], False, s_i == r, [TK] + TU[ch], [Tpb])
                self.act(Yv[:, r, :], pb[:, :], AF.Copy, [Tpb], [TY])
            Tuc = TU[ch]
            self.stt("dve", Ysb, U[:, ch, :], self.vcol("ssm_d%d" % l, ch), Ysb, ALU.mult, ALU.add, [TY] + Tuc, [TY])
            self.act(U[:, ch, :], Ysb, AF.Gelu, [TY], Tuc)

        if getattr(self, "substop", 9) <= 2:
            return
        P.barrier()
        A.off = mark
        wg = A.bf16(8 * 2 * D).rearrange("p (k n) -> p k n", n=2 * D)
        Twg = T("wglu")
        self.load_w(wg, self.w["ssm_w_glu"][l], 2 * D, Twg)
        xb = [A.f32(8 * TL).rearrange("p (c n) -> p c n", n=TL) for _ in range(2)]
        Txb = [T("xb0"), T("xb1")]
        sg = [A.f32(TL) for _ in range(2)]
        Tsg = [T("sg0"), T("sg1")]
        for t in range(NTILE):
            xt, Txt = xb[t % 2], Txb[t % 2]
            P.dma("sp", xt, self.xtile_view(xin, t), writes=[Txt])
            for oc in range(8):
                pv, Tpv = self.bank()
                pg, Tpg = self.bank()
                for kc in range(8):
                    self.mm(pv[:, :], wg[:, kc, oc * 128:(oc + 1) * 128], U[:, kc, t * TL:(t + 1) * TL], kc == 0, kc == 7, [Twg, TU[kc][t]], [Tpv])
                for kc in range(8):
                    self.mm(pg[:, :], wg[:, kc, D + oc * 128:D + (oc + 1) * 128], U[:, kc, t * TL:(t + 1) * TL], kc == 0, kc == 7, [Twg, TU[kc][t]], [Tpg])
                s_, Ts2 = sg[oc % 2], Tsg[oc % 2]
                self.act(s_, pg[:, :], AF.Sigmoid, [Tpg], [Ts2])
                self.tt("dve", s_, pv[:, :], s_, ALU.mult, [Tpv, Ts2], [Ts2])
                self.tt("dve", xt[:, oc, :], xt[:, oc, :], s_, ALU.add, [Txt, Ts2], [Txt])
            P.dma("sp", self.xtile_view(xout, t), xt, reads=[Txt])

    def stage_FFN(self, l, xin, xout):
        P, A = self.P, self.A
        wup = A.bf16(8 * 2 * DFF).rearrange("p (k n) -> p k n", n=2 * DFF)
        Twu = [T("wup%d" % j) for j in range(11)]
        wsrc = self.w["ffn_w_up"][l].rearrange("(kc p) n -> p kc n", p=128)
        for j in range(11):
            P.dma("pool", wup[:, :, j * 256:(j + 1) * 256], wsrc[:, :, j * 256:(j + 1) * 256], writes=[Twu[j]])
            P.dma("pool", wup[:, :, DFF + j * 256:DFF + (j + 1) * 256], wsrc[:, :, DFF + j * 256:DFF + (j + 1) * 256], writes=[Twu[j]])
        dsrc = self.w["ffn_w_down"][l].rearrange("(kc p) n -> p kc n", p=128)
        NR = 3
        wdr = [A.bf16(NFC * 128).rearrange("p (k n) -> p k n", n=128) for _ in range(NR)]
        Twd = [T("wdr%d" % i) for i in range(NR)]
        xb = [A.f32(8 * TL).rearrange("p (c n) -> p c n", n=TL) for _ in range(2)]
        Txb = [T("xb0"), T("xb1")]
        h = A.bf16(8 * TL).rearrange("p (c n) -> p c n", n=TL)
        Th = T("h")
        rstd = A.f32(TL)
        Trs = T("rstd")
        abuf = A.bf16(NFC * TL).rearrange("p (k n) -> p k n", n=TL)
        Ta = [T("a%d" % k) for k in range(NFC)]
        ub = [A.f32(TL + 2) for _ in range(4)]
        Tub = [T("ub%d" % i) for i in range(4)]
        tball = A.f32(4 * TL)
        tb = [tball[:, i * TL:(i + 1) * TL] for i in range(4)]
        Ttb = [T("tb%d" % i) for i in range(4)]
        sqh = tball.bitcast(BF16).rearrange("p (c n) -> p c n", n=TL)
        halo = A.f32(44 * 2).rearrange("p (v c) -> p v c", c=2)
        Thalo = [T("halo%d" % v) for v in range(44)]
        cw = lambda tap, vc: self.vcol("conv_w%d" % l, tap * 44 + vc)
        cb = lambda vc: self.vcol("conv_b%d" % l, vc)
        cnt = 0
        dcnt = 0
        for t in range(NTILE):
            xt, Txt = xb[t % 2], Txb[t % 2]
            P.dma("sp", xt, self.xtile_view(xin, t), writes=[Txt])
            self.rmsnorm(xt, Txt, "ffn_norm%d" % l, h, Th, sqh, list(Ttb), rstd, Trs, TL)
            for k in range(NFC):
                res = []
                for part in range(2):
                    vc = part * NFC + k
                    col0 = part * DFF + k * 128
                    u, Tu = ub[cnt % 4], Tub[cnt % 4]
                    tt_, Ttt = tb[cnt % 4], Ttb[cnt % 4]
                    cnt += 1
                    pb, Tpb = self.bank()
                    for kc in range(8):
                        self.mm(pb[:, :], wup[:, kc, col0:col0 + 128], h[:, kc, :], kc == 0, kc == 7, [Twu[k // 2], Th], [Tpb])
                    if t % 4 == 0:
                        self.memset("dve", u[:, 0:2], 0.0, [Tu])
                    else:
                        self.act(u[:, 0:2], halo[:, vc, :], AF.Copy, [Thalo[vc]], [Tu])
                    self.act(u[:, 2:TL + 2], pb[:, :], AF.Copy, [Tpb], [Tu])
                    self.act(halo[:, vc, :], u[:, TL:TL + 2], AF.Copy, [Tu], [Thalo[vc]])
                    self.act(tt_, u[:, 2:TL + 2], AF.Identity, [Tu, self.Tconst], [Ttt], scale=cw(2, vc), bias=cb(vc))
                    self.stt("dve", tt_, u[:, 1:TL + 1], cw(1, vc), tt_, ALU.mult, ALU.add, [Tu, Ttt], [Ttt])
                    self.stt("dve", tt_, u[:, 0:TL], cw(0, vc), tt_, ALU.mult, ALU.add, [Tu, Ttt], [Ttt])
                    res.append((tt_, Ttt))
                (tg, Ttg), (tv, Ttv) = res
                self.act(tg, tg, AF.Gelu, [Ttg], [Ttg])
                self.tt("dve", abuf[:, k, :], tg, tv, ALU.mult, [Ttg, Ttv], [Ta[k]])
            for oc in range(8):
                wd, Tw_ = wdr[dcnt % NR], Twd[dcnt % NR]
                dcnt += 1
                P.dma("pool", wd, dsrc[:, :, oc * 128:(oc + 1) * 128], writes=[Tw_])
                pb, Tpb = self.bank()
                for k in range(NFC):
                    self.mm(pb[:, :], wd[:, k, :], abuf[:, k, :], k == 0, k == NFC - 1, [Tw_, Ta[k]], [Tpb])
                self.tt("dve", xt[:, oc, :], xt[:, oc, :], pb[:, :], ALU.add, [Txt, Tpb], [Txt])
            P.dma("sp", self.xtile_view(xout, t), xt, reads=[Txt])

    def stage_PLE(self, l, xin, xout):
        P, A = self.P, self.A
        wgt = A.bf16(8 * D).rearrange("p (k n) -> p k n", n=D)
        wpj = A.bf16(2 * D).rearrange("p (k n) -> p k n", n=D)
        Tw = T("wple")
        self.load_w(wgt, self.w["ple_w_gate"][l], D, Tw)
        self.load_w(wpj, self.w["ple_w_proj"][l], D, Tw)
        xb = [A.f32(8 * TL).rearrange("p (c n) -> p c n", n=TL) for _ in range(2)]
        Txb = [T("xb0"), T("xb1")]
        pb_ = [A.bf16(2 * TL).rearrange("p (c n) -> p c n", n=TL) for _ in range(2)]
        Tpp = [T("pp0"), T("pp1")]
        sq = A.bf16(8 * TL).rearrange("p (c n) -> p c n", n=TL)
        Tsq = T("sq")
        h = A.bf16(8 * TL).rearrange("p (c n) -> p c n", n=TL)
        Th = T("h")
        rstd = A.f32(TL)
        Trs = T("rstd")
        sg = [A.f32(TL) for _ in range(2)]
        Tsg = [T("sg0"), T("sg1")]
        psrc = self.pT[l].rearrange("(c p) n -> p c n", p=128)
        for t in range(NTILE):
            xt, Txt = xb[t % 2], Txb[t % 2]
            pp, Tp = pb_[t % 2], Tpp[t % 2]
            P.dma("sp", xt, self.xtile_view(xin, t), writes=[Txt])
            P.dma("pool", pp, psrc[:, :, t * TL:(t + 1) * TL], writes=[Tp])
            self.rmsnorm(xt, Txt, "ple_norm%d" % l, h, Th, sq, Tsq, rstd, Trs, TL)
            for oc in range(8):
                pg, Tpg = self.bank()
                pv, Tpv = self.bank()
                for kc in range(8):
                    self.mm(pg[:, :], wgt[:, kc, oc * 128:(oc + 1) * 128], h[:, kc, :], kc == 0, kc == 7, [Tw, Th], [Tpg])
                for kc in range(2):
                    self.mm(pv[:, :], wpj[:, kc, oc * 128:(oc + 1) * 128], pp[:, kc, :], kc == 0, kc == 1, [Tw, Tp], [Tpv])
                s_, Ts2 = sg[oc % 2], Tsg[oc % 2]
                self.act(s_, pg[:, :], AF.Sigmoid, [Tpg], [Ts2])
                self.tt("dve", s_, pv[:, :], s_, ALU.mult, [Tpv, Ts2], [Ts2])
                self.tt("dve", xt[:, oc, :], xt[:, oc, :], s_, ALU.add, [Txt, Ts2], [Txt])
            P.dma("sp", self.xtile_view(xout, t), xt, reads=[Txt])

    def stage_T5(self):
        P, A = self.P, self.A
        oh = [A.f32(4096) for _ in range(2)]
        Toh = T("oh")
        for dd in range(2):
            P.dma("sp", oh[dd], self.oh_d[dd], writes=[Toh])
        m0 = A.f32(128)
        P.dma("sp", m0, self.mask0_d, writes=[Toh])
        tmp = A.f32(4096)
        Ttmp = T("t5tmp")
        tab = self.t5tab.rearrange("p (h d q) -> p h d q", d=2, q=128)
        o, _ = self.vp.cols["rbT"]
        for hh in range(8):
            rb = self.vecs[:, o + hh * 32:o + (hh + 1) * 32]
            rbb = rb.unsqueeze(1).broadcast_to([128, 128, 32])
            for dd in range(2):
                self.tt("dve", tmp.rearrange("p (q b) -> p q b", b=32), oh[dd].rearrange("p (q b) -> p q b", b=32), rbb, ALU.mult,
                        [Toh, self.Tconst], [Ttmp])
                self.reduce(tab[:, hh, dd, :], tmp.rearrange("p (q b) -> p q b", b=32), [Ttmp], [self.Tt5])
                self.ts("dve", tab[:, hh, dd, :], tab[:, hh, dd, :], self.vecs[:, o + hh * 32 + 31:o + hh * 32 + 32], None, ALU.subtract, None,
                        [self.Tt5, self.Tconst], [self.Tt5])
            self.tt("dve", tab[:, hh, 0, :], tab[:, hh, 0, :], m0, ALU.add, [self.Tt5, Toh], [self.Tt5])

    def stage_KV(self, xin):
        P, A = self.P, self.A
        wkv = A.bf16(8 * 2 * D).rearrange("p (k n) -> p k n", n=2 * D)
        Tw = T("wkv")
        self.load_w(wkv, self.w["kv_w"], 2 * D, Tw)
        xb = [A.f32(8 * TL).rearrange("p (c n) -> p c n", n=TL) for _ in range(2)]
        Txb = [T("xb0"), T("xb1")]
        sq = A.bf16(8 * TL).rearrange("p (c n) -> p c n", n=TL)
        Tsq = T("sq")
        h = A.bf16(8 * TL).rearrange("p (c n) -> p c n", n=TL)
        Th = T("h")
        rstd = A.f32(TL)
        Trs = T("rstd")
        sq2 = [A.bf16(TL) for _ in range(2)]
        Tsq2 = [T("sq2a"), T("sq2b")]
        rs2 = [A.f32(TL) for _ in range(2)]
        Trs2 = [T("rs2a"), T("rs2b")]
        kb_ = [A.bf16(8 * TL).rearrange("p (c n) -> p c n", n=TL) for _ in range(2)]
        Tkb = [T("kb0"), T("kb1")]
        vb_ = [A.bf16(4 * D).rearrange("p (b n) -> p b n", n=D) for _ in range(2)]
        Tvb = [T("vb0"), T("vb1")]
        for t in range(NTILE):
            xt, Txt = xb[t % 2], Txb[t % 2]
            kb, Tk = kb_[t % 2], Tkb[t % 2]
            vb, Tv = vb_[t % 2], Tvb[t % 2]
            P.dma("sp", xt, self.xtile_view(xin, t), writes=[Txt])
            self.rmsnorm(xt, Txt, "kv_norm", h, Th, sq, Tsq, rstd, Trs, TL)
            for hh in range(8):
                pb, Tpb = self.bank()
                for kc in range(8):
                    self.mm(pb[:, :], wkv[:, kc, hh * 128:(hh + 1) * 128], h[:, kc, :], kc == 0, kc == 7, [Tw, Th], [Tpb])
                self.headnorm(pb, Tpb, self.vcol("k_norm"), kb[:, hh, :], Tk, sq2[hh % 2], Tsq2[hh % 2], rs2[hh % 2], Trs2[hh % 2], TL, 64, self.blk_bf[:, :])
            P.dma("sp", self.KTd.rearrange("h p n -> p h n")[:, :, t * TL:(t + 1) * TL], kb, reads=[Tk])
            for tb in range(4):
                for half in range(2):
                    pb, Tpb = self.bank()
                    for kc in range(8):
                        self.mm(pb[:, :], h[:, kc, tb * 128:(tb + 1) * 128], wkv[:, kc, D + half * 512:D + (half + 1) * 512], kc == 0, kc == 7, [Tw, Th], [Tpb])
                    self.act(vb[:, tb, half * 512:(half + 1) * 512], pb[:, :], AF.Copy, [Tpb], [Tv])
            P.dma("sp", self.Vd[t * TL:(t + 1) * TL, :].rearrange("(b p) n -> p b n", p=128), vb, reads=[Tv])

    def stage_B(self, l, xin, xout):
        P, A = self.P, self.A
        j = l - N_A
        lam_init = 0.8 - 0.6 * math.exp(-0.3 * l)
        mark = A.off
        lt = A.f32(64)
        Tlt = T("lt")
        la = A.f32(4)
        Tla = T("la")
        for i, (a, b) in enumerate((("lambda_q1", "lambda_k1"), ("lambda_q2", "lambda_k2"))):
            oa, _ = self.vp.cols["%s%d" % (a, j)]
            ob, _ = self.vp.cols["%s%d" % (b, j)]
            self.tt("dve", lt, self.vecs[:, oa:oa + 64], self.vecs[:, ob:ob + 64], ALU.mult, [self.Tconst], [Tlt])
            self.reduce(la[:, i:i + 1], lt, [Tlt], [Tla])
        self.act(la[:, 0:2], la[:, 0:2], AF.Exp, [Tla], [Tla])
        neglam = self.lamcols[:, 0:1]
        gq = self.lamcols[:, 1:2]
        gsub = self.lamcols[:, 2:3]
        self.tt("dve", la[:, 2:3], la[:, 1:2], la[:, 0:1], ALU.subtract, [Tla], [Tla])
        self.ts("dve", neglam, la[:, 2:3], -lam_init, None, ALU.add, None, [Tla], [self.Tlam])
        self.ts("dve", gq, self.vcol("q_norm%d" % j), 0.125, None, ALU.mult, None, [self.Tconst], [self.Tlam])
        self.ts("dve", gsub, self.vcol("subln%d" % j), 1.0 - lam_init, None, ALU.mult, None, [self.Tconst], [self.Tlam])

        wq = A.bf16(8 * D).rearrange("p (k n) -> p k n", n=D)
        Tw = T("wq")
        self.load_w(wq, self.w["attn_w_q"][j], D, Tw)
        xb = [A.f32(8 * TL).rearrange("p (c n) -> p c n", n=TL) for _ in range(2)]
        Txb = [T("xb0"), T("xb1")]
        sq = A.bf16(8 * TL).rearrange("p (c n) -> p c n", n=TL)
        Tsq = T("sq")
        h = A.bf16(8 * TL).rearrange("p (c n) -> p c n", n=TL)
        Th = T("h")
        rstd = A.f32(TL)
        Trs = T("rstd")
        sq2 = [A.bf16(TL) for _ in range(2)]
        Tsq2 = [T("sq2a"), T("sq2b")]
        rs2 = [A.f32(TL) for _ in range(2)]
        Trs2 = [T("rs2a"), T("rs2b")]
        qb_ = [A.bf16(8 * TL).rearrange("p (c n) -> p c n", n=TL) for _ in range(2)]
        Tqb = [T("qb0"), T("qb1")]
        for t in range(NTILE):
            xt, Txt = xb[t % 2], Txb[t % 2]
            qb, Tq = qb_[t % 2], Tqb[t % 2]
            P.dma("sp", xt, self.xtile_view(xin, t), writes=[Txt])
            self.rmsnorm(xt, Txt, "attn_norm%d" % j, h, Th, sq, Tsq, rstd, Trs, TL)
            for hh in range(8):
                pb, Tpb = self.bank()
                for kc in range(8):
                    self.mm(pb[:, :], wq[:, kc, hh * 128:(hh + 1) * 128], h[:, kc, :], kc == 0, kc == 7, [Tw, Th], [Tpb])
                self.headnorm(pb, Tpb, gq, qb[:, hh, :], Tq, sq2[hh % 2], Tsq2[hh % 2], rs2[hh % 2], Trs2[hh % 2], TL, 64, self.blk_bf[:, :])
            P.dma("sp", self.QTd.rearrange("h p n -> p h n")[:, :, t * TL:(t + 1) * TL], qb, reads=[Tq, self.Tlam])

        P.barrier()
        A.off = mark
        wo = A.bf16(8 * D).rearrange("p (k n) -> p k n", n=D)
        Two = T("wo")
        self.load_w(wo, self.w["attn_w_o"][j], D, Two)
        ON = A.bf16(8 * SEQ).rearrange("p (h n) -> p h n", n=SEQ)
        TON = [[T("on%d_%d" % (hh, qt)) for qt in range(4)] for hh in range(8)]
        kq = [[A.bf16(SEQ), A.bf16(SEQ), A.bf16(16 * 128).rearrange("p (b e) -> p b e", e=128)] for _ in range(2)]
        Tkq = [T("kq0"), T("kq1")]
        NE = 4
        Eb = [A.bf16(TL) for _ in range(NE)]
        TE = [T("E%d" % i) for i in range(NE)]
        fin = [[A.f32(TL) for _ in range(4)] for _ in range(2)]
        Tfin = [T("fin0"), T("fin1")]
        sqs = A.bf16(TL)
        Tsqs = T("sqs")
        rss = A.f32(TL)
        Trss = T("rss")
        xt = A.f32(8 * TL).rearrange("p (c n) -> p c n", n=TL)
        Txt = T("xt")
        tab = self.t5tab.rearrange("p (h d q) -> p h (d q)", d=2, q=128)
        orb, _ = self.vp.cols["rbT"]
        accb = [0, 1, 2, 3]
        scb = [4, 5, 6, 7]
        it = 0
        hcount = 0
        for s in range(2):
            c0 = s * SEQ
            for hh in range(8):
                Kt, Qt, Vt = kq[hcount % 2]
                Tk = Tkq[hcount % 2]
                hcount += 1
                P.dma("sp", Kt, self.KTd[hh, :, c0:c0 + SEQ], writes=[Tk])
                P.dma("sp", Qt, self.QTd[hh, :, c0:c0 + SEQ], writes=[Tk])
                P.dma("sp", Vt, self.Vd[c0:c0 + SEQ, hh * 128:(hh + 1) * 128].rearrange("(b p) e -> p b e", p=128), writes=[Tk])
                b31 = self.vecs[:, orb + hh * 32 + 31:orb + hh * 32 + 32]
                for qt in range(4):
                    nkb = 4 * qt + 4
                    items = [(kb, c) for kb in range(nkb) for c in range(2)]
                    pend = []

                    def do_pv(item):
                        kb, c, ei, n_lo = item
                        E, TE_ = Eb[ei], TE[ei]
                        N = TL - n_lo
                        ob, sb = accb[c], accb[2 + c]
                        self.mm(self.ps[ob][:, n_lo:TL], Vt[:, kb, :], E[:, :N], kb == 0, kb == nkb - 1, [Tk, TE_], [self.Tps[ob]])
                        self.mm(self.ps[sb][:, n_lo:TL], self.ones_bf[:, :], E[:, :N], kb == 0, kb == nkb - 1, [self.Tconst, TE_], [self.Tps[sb]])

                    for (kb, c) in items:
                        n_lo = max(0, kb * 128 - qt * TL)
                        N = TL - n_lo
                        sbk = scb[it % 4]
                        ei = it % NE
                        it += 1
                        pb, Tpb = self.ps[sbk], self.Tps[sbk]
                        q0 = qt * TL + n_lo
                        self.mm(pb[:, :N], Kt[64 * c:64 * c + 64, kb * 128:(kb + 1) * 128], Qt[64 * c:64 * c + 64, q0:q0 + N], True, True, [Tk], [Tpb])
                        dblk0 = (q0 // 128) - kb
                        if dblk0 == 0:
                            w_ = min(N, 256)
                            self.tt("dve", pb[:, 0:w_], pb[:, 0:w_], tab[:, hh, 0:w_], ALU.add, [Tpb, self.Tt5], [Tpb])
                        elif dblk0 == 1:
                            self.tt("dve", pb[:, 0:128], pb[:, 0:128], tab[:, hh, 128:256], ALU.add, [Tpb, self.Tt5], [Tpb])
                        self.act(Eb[ei][:, :N], pb[:, :N], AF.Exp, [Tpb, self.Tconst], [TE[ei]], bias=b31)
                        pend.append((kb, c, ei, n_lo))
                        if len(pend) > 2:
                            do_pv(pend.pop(0))
                    while pend:
                        do_pv(pend.pop(0))
                    f = fin[qt % 2]
                    Tf = Tfin[qt % 2]
                    r1, r2, o1, o2 = f
                    self.recip(r1, self.ps[accb[2]][:, :], [self.Tps[accb[2]]], [Tf])
                    self.recip(r2, self.ps[accb[3]][:, :], [self.Tps[accb[3]]], [Tf])
                    self.tt("dve", o1, self.ps[accb[0]][:, :], r1, ALU.mult, [self.Tps[accb[0]], Tf], [Tf])
                    self.tt("dve", o2, self.ps[accb[1]][:, :], r2, ALU.mult, [self.Tps[accb[1]], Tf], [Tf])
                    self.stt("dve", o1, o2, neglam, o1, ALU.mult, ALU.add, [Tf, self.Tlam], [Tf])
                    self.act(sqs, o1, AF.Square, [Tf], [Tsqs])
                    sbk = scb[it % 4]
                    it += 1
                    pb, Tpb = self.ps[sbk], self.Tps[sbk]
                    self.mm(pb[:, :], self.ones_bf[:, :], sqs, True, True, [Tsqs, self.Tconst], [Tpb])
                    self.act(rss, pb[:, :], AF.Ln, [Tpb], [Trss], scale=1.0 / 128, bias=EPS)
                    self.act(rss, rss, AF.Exp, [Trss], [Trss], scale=-0.5)
                    self.stt("dve", ON[:, hh, qt * TL:(qt + 1) * TL], o1, gsub, rss, ALU.mult, ALU.mult, [Tf, Trss, self.Tlam], [TON[hh][qt]])
            for qt in range(4):
                t = s * 4 + qt
                P.dma("sp", xt, self.xtile_view(xin, t), writes=[Txt])
                for oc in range(8):
                    pb, Tpb = self.bank()
                    for hh in range(8):
                        self.mm(pb[:, :], wo[:, hh, oc * 128:(oc + 1) * 128], ON[:, hh, qt * TL:(qt + 1) * TL], hh == 0, hh == 7, [Two, TON[hh][qt]], [Tpb])
                    self.tt("dve", xt[:, oc, :], xt[:, oc, :], pb[:, :], ALU.add, [Txt, Tpb], [Txt])
                P.dma("sp", self.xtile_view(xout, t), xt, reads=[Txt])


def make_inputs(inp, core, vp, vecs, ssm, t5):
    x = np.asarray(inp["x"], np.float32)[2 * core:2 * core + 2].reshape(NT, D)
    p = np.asarray(inp["p"], np.float32)[:, 2 * core:2 * core + 2].reshape(DEPTH, NT, 256)
    m = {"xT": np.ascontiguousarray(x.T), "pT": np.ascontiguousarray(p.transpose(0, 2, 1)), "vecs": vecs,
         "t5oh": t5[0], "t5mask0": t5[1]}
    for l in range(N_A):
        m["ssmF%d" % l] = ssm[l][0]
        m["ssmS%d" % l] = ssm[l][1]
    for nm in ("ssm_w_in", "ssm_w_glu", "kv_w", "attn_w_q", "attn_w_o", "ffn_w_up", "ffn_w_down", "ple_w_gate", "ple_w_proj"):
        m[nm] = np.ascontiguousarray(inp[nm], dtype=np.float32)
    return m


def kernel(**inputs):
    inp = {k: np.asarray(v) for k, v in inputs.items()}
    vp = vec_layout(inp)
    vecs = vp.build()
    ssm = [ssm_layouts(inp, l) for l in range(N_A)]
    t5 = t5_consts()
    nc = Builder(vp).build()
    ncores = 8
    in_maps = [make_inputs(inp, c, vp, vecs, ssm, t5) for c in range(ncores)]
    res = run_bass_kernel_spmd(nc, in_maps, core_ids=list(range(ncores)))
    outs = []
    for c in range(ncores):
        yT = np.asarray(res.results[c]["yT"])
        outs.append(yT.T.reshape(2, SEQ, D))
    return np.ascontiguousarray(np.concatenate(outs, axis=0).astype(np.float32))
```

```python
import math
from concourse.bass_utils import run_bass_kernel_spmd
import numpy as np
import concourse.bass as bass
import concourse.mybir as mybir
from contextlib import ExitStack

F32 = mybir.dt.float32
BF16 = mybir.dt.bfloat16
AF = mybir.ActivationFunctionType
ALU = mybir.AluOpType
AX = mybir.AxisListType

ENGS = ("pe", "act", "dve", "pool", "sp")
SAME_ENGINE_SYNC = ("act", "dve", "pool")
NDMA_SEMS = 12


class T:
    __slots__ = ("name", "w", "r")

    def __init__(self, name):
        self.name = name
        self.w = None
        self.r = []


class Op:
    __slots__ = ("eng", "fn", "deps", "is_dma", "signal", "sigval", "sem", "idx", "noinst")

    def __init__(self, eng, fn, is_dma):
        self.eng = eng
        self.fn = fn
        self.deps = []
        self.is_dma = is_dma
        self.signal = False
        self.sigval = 0
        self.sem = None
        self.idx = 0
        self.noinst = False


class Prog:
    def __init__(self, nc):
        self.nc = nc
        self.ops = {e: [] for e in ENGS}
        self.dma_slots = {e: [None] * NDMA_SEMS for e in ENGS}
        self.dma_count = {e: 0 for e in ENGS}
        self.dma_slot_vals = {e: [0] * NDMA_SEMS for e in ENGS}
        self.all_ops = []
        self.last_barrier = None

    def _track(self, op, reads, writes):
        deps = op.deps
        for t in reads:
            if t.w is not None:
                deps.append(t.w)
        for t in writes:
            if t.w is not None:
                deps.append(t.w)
            deps.extend(t.r)
        for t in reads:
            t.r.append(op)
        for t in writes:
            t.w = op
            t.r = []

    def add(self, eng, fn, reads=(), writes=()):
        op = Op(eng, fn, False)
        self._track(op, reads, writes)
        op.idx = len(self.all_ops)
        self.all_ops.append(op)
        self.ops[eng].append(op)
        return op

    def wait_only(self, eng, reads=()):
        op = Op(eng, lambda e: None, False)
        op.noinst = True
        for t in reads:
            if t.w is not None:
                op.deps.append(t.w)
        op.idx = len(self.all_ops)
        self.all_ops.append(op)
        self.ops[eng].append(op)
        return op

    def dma(self, eng, out, in_, reads=(), writes=(), **kw):
        def fn(e):
            return e.dma_start(out=out, in_=in_, **kw)
        op = Op(eng, fn, True)
        self._track(op, reads, writes)
        n = self.dma_count[eng]
        slot = n % NDMA_SEMS
        self.dma_count[eng] = n + 1
        prev = self.dma_slots[eng][slot]
        if prev is not None:
            op.deps.append(prev)
        self.dma_slots[eng][slot] = op
        self.dma_slot_vals[eng][slot] += 16
        op.sem = (eng, slot)
        op.sigval = self.dma_slot_vals[eng][slot]
        op.idx = len(self.all_ops)
        self.all_ops.append(op)
        self.ops[eng].append(op)
        return op

    def barrier(self):
        lasts = []
        for e in ENGS:
            for o in reversed(self.ops[e]):
                if o.noinst or o.is_dma:
                    continue
                lasts.append(o)
                break
        outstanding = [op for e in ENGS for op in self.dma_slots[e] if op is not None]
        for e in ENGS:
            op = Op(e, lambda eng: None, False)
            op.noinst = True
            op.deps = list(lasts) + list(outstanding)
            op.idx = len(self.all_ops)
            self.all_ops.append(op)
            self.ops[e].append(op)

    def emit(self, final_waits=()):
        nc = self.nc
        for op in self.all_ops:
            for d in op.deps:
                if d.is_dma:
                    continue
                if d.eng == op.eng and d.eng not in SAME_ENGINE_SYNC:
                    continue
                d.signal = True
        for e in ENGS:
            c = 0
            for op in self.ops[e]:
                if not op.is_dma:
                    if op.signal:
                        c += 1
                    op.sigval = c
        self.stats = {e: len(self.ops[e]) for e in ENGS}
        with ExitStack() as st:
            esem = {e: st.enter_context(nc.semaphore("s_" + e)) for e in ENGS}
            dsem = {}
            for e in ENGS:
                if self.dma_count[e] > 0:
                    for s in range(min(NDMA_SEMS, self.dma_count[e])):
                        dsem[(e, s)] = st.enter_context(nc.semaphore("d_%s_%d" % (e, s)))
            block = st.enter_context(nc.Block())
            ops = self.ops

            def run(ename, eng):
                waited = {}
                nwait = 0
                for op in ops[ename]:
                    need = {}
                    for d in op.deps:
                        if d.is_dma:
                            key = ("d",) + d.sem
                            v = d.sigval
                        else:
                            if d.eng == ename and ename not in SAME_ENGINE_SYNC:
                                continue
                            key = ("e", d.eng)
                            v = d.sigval
                        if waited.get(key, 0) >= v:
                            continue
                        if need.get(key, 0) < v:
                            need[key] = v
                    for key, v in need.items():
                        sem = dsem[key[1:]] if key[0] == "d" else esem[key[1]]
                        eng.wait_ge(sem, v)
                        waited[key] = v
                        nwait += 1
                    ins = op.fn(eng)
                    if ins is None:
                        continue
                    if op.is_dma:
                        ins.then_inc(dsem[op.sem], 16)
                    elif op.signal:
                        ins.then_inc(esem[ename], 1)
                self.stats["wait_" + ename] = nwait

            @block.tensor
            def _(eng):
                run("pe", eng)

            @block.scalar
            def _(eng):
                run("act", eng)

            @block.vector
            def _(eng):
                run("dve", eng)

            @block.gpsimd
            def _(eng):
                run("pool", eng)

            @block.sync
            def _(eng):
                run("sp", eng)


NT = 4096
SEQ = 2048
TL = 512
NTILE = NT // TL
D = 1024
DFF = 2816
NFC = 22
EPS = 1e-6
TWO_PI = 2.0 * math.pi
MAGIC = 12582912.0
NEG = -30000.0
N_A = 2
DEPTH = 4


def colv(v):
    v = np.asarray(v, np.float32)
    return np.ascontiguousarray(v.reshape(-1, 128).T)


class VecPack:
    def __init__(self):
        self.cols = {}
        self.parts = []
        self.n = 0

    def add(self, name, arr):
        arr = np.ascontiguousarray(arr, dtype=np.float32)
        assert arr.shape[0] == 128
        self.cols[name] = (self.n, arr.shape[1])
        self.parts.append(arr)
        self.n += arr.shape[1]

    def build(self):
        return np.ascontiguousarray(np.concatenate(self.parts, axis=1))


def t5_bucket_np(n):
    n = np.maximum(n, 0)
    nf = np.maximum(n, 16).astype(np.float32)
    large = 16 + (np.log(nf / np.float32(16)) / np.float32(math.log(128 / 16)) * np.float32(16)).astype(np.int32)
    large = np.minimum(large, 31)
    return np.where(n < 16, n, large)


def vec_layout(inp):
    vp = VecPack()
    for l in range(N_A):
        vp.add("ssm_norm%d" % l, colv(inp["ssm_norm"][l]))
        vp.add("ssm_d%d" % l, colv(inp["ssm_d"][l]))
    vp.add("kv_norm", colv(inp["kv_norm"]))
    for j in range(2):
        vp.add("attn_norm%d" % j, colv(inp["attn_norm"][j]))
    for l in range(DEPTH):
        vp.add("ffn_norm%d" % l, colv(inp["ffn_norm"][l]))
        vp.add("ple_norm%d" % l, colv(inp["ple_norm"][l]))
        cw = np.concatenate([colv(inp["ffn_conv_w"][l][j]) for j in range(3)], axis=1)
        vp.add("conv_w%d" % l, cw)
        vp.add("conv_b%d" % l, colv(inp["ffn_conv_b"][l]))
    vp.add("k_norm", np.tile(np.asarray(inp["k_norm"], np.float32), 2)[:, None])
    for j in range(2):
        vp.add("q_norm%d" % j, np.tile(np.asarray(inp["q_norm"][j], np.float32), 2)[:, None])
        vp.add("subln%d" % j, np.asarray(inp["subln"][j], np.float32)[:, None])
        for nm in ("lambda_q1", "lambda_k1", "lambda_q2", "lambda_k2"):
            vp.add("%s%d" % (nm, j), np.broadcast_to(np.asarray(inp[nm][j], np.float32)[None, :], (128, 64)))
    vp.add("rbT", np.broadcast_to(np.asarray(inp["rel_bias"], np.float32).T.reshape(1, 256), (128, 256)))
    sgn = np.concatenate([-np.ones(64, np.float32), np.ones(64, np.float32)])[:, None]
    vp.add("sgn", sgn)
    vp.add("nsgn", -sgn)
    mask8 = (np.arange(128)[:, None] // 16 == np.arange(8)[None, :]).astype(np.float32)
    vp.add("mask8", mask8)
    return vp


def ssm_layouts(inp, l):
    lre = np.asarray(inp["ssm_lambda_re"][l], np.float32)
    lim = np.asarray(inp["ssm_lambda_im"][l], np.float32)
    ldt = np.asarray(inp["ssm_log_dt"][l], np.float32)
    bre = np.asarray(inp["ssm_b_re"][l], np.float32)
    bim = np.asarray(inp["ssm_b_im"][l], np.float32)
    cre = np.asarray(inp["ssm_c_re"][l], np.float32)
    cim = np.asarray(inp["ssm_c_im"][l], np.float32)
    def F_gm(a):
        a4 = a.reshape(8, 8, 64)
        a4 = np.repeat(a4[:, :, None, :], 16, axis=2)
        return a4.transpose(1, 2, 0, 3).reshape(128, 8, 64)
    def F_b(b):
        b4 = b.reshape(8, 8, 64, 16)
        return b4.transpose(1, 3, 0, 2).reshape(128, 8, 64)
    ldtF = np.broadcast_to(ldt[:, None], (64, 64))
    ssmF = np.stack([F_gm(lre), F_gm(lim), F_b(bre), F_b(bim), F_gm(ldtF)], axis=1)
    ssmF = np.ascontiguousarray(ssmF.reshape(128, 5 * 512), dtype=np.float32)
    def S_gm(a):
        return np.concatenate([a.T, a.T], axis=0)
    ccS = np.concatenate([cre.transpose(2, 0, 1), cim.transpose(2, 0, 1)], axis=0)
    ccS2 = np.concatenate([cim.transpose(2, 0, 1), cre.transpose(2, 0, 1)], axis=0)
    bSre = np.concatenate([bre.transpose(1, 0, 2), bre.transpose(1, 0, 2)], axis=0)
    bSim = np.concatenate([bim.transpose(1, 0, 2), bim.transpose(1, 0, 2)], axis=0)
    ssmS = np.concatenate([S_gm(lre), S_gm(lim), S_gm(ldtF), ccS.reshape(128, 1024), ccS2.reshape(128, 1024),
                           bSre.reshape(128, 1024), bSim.reshape(128, 1024)], axis=1)
    return ssmF, np.ascontiguousarray(ssmS, dtype=np.float32)


def t5_consts():
    k = np.arange(128)[:, None]
    q = np.arange(128)[None, :]
    oh = np.zeros((2, 128, 128, 32), np.float32)
    for dd in range(2):
        n = q - k + 128 * dd
        b = t5_bucket_np(n)
        oh[dd] = (b[:, :, None] == np.arange(32)[None, None, :]).astype(np.float32)
    mask0 = np.where(q >= k, 0.0, NEG).astype(np.float32)
    return np.ascontiguousarray(oh.reshape(2, 128, 4096)), mask0


class Arena:
    def __init__(self, ap, n):
        self.ap = ap
        self.n = n
        self.off = 0

    def f32(self, n):
        a = self.ap[:, self.off:self.off + n]
        self.off += n
        assert self.off <= self.n, ("arena overflow", self.off, self.n)
        return a

    def bf16(self, n):
        m = (n + 1) // 2
        a = self.ap[:, self.off:self.off + m].bitcast(BF16)
        self.off += m
        assert self.off <= self.n, ("arena overflow", self.off, self.n)
        return a


class Builder:
    def __init__(self, vp, stages=None, dump=None):
        self.vp = vp
        self.nc = bass.Bass("TRN2", target_bir_lowering=False)
        self.stages = stages
        self.bank_rr = 0

    def mm(self, out, lhsT, rhs, start, stop, reads, writes):
        self.P.add("pe", lambda e: e.matmul(out, lhsT, rhs, start=start, stop=stop), reads, writes)

    def act(self, out, in_, func, reads, writes, scale=1.0, bias=0.0):
        self.P.add("act", lambda e: e.activation(out=out, in_=in_, func=func, scale=scale, bias=bias), reads, writes)

    def stt(self, eng, out, in0, scalar, in1, op0, op1, reads, writes):
        self.P.add(eng, lambda e: e.scalar_tensor_tensor(out=out, in0=in0, scalar=scalar, in1=in1, op0=op0, op1=op1), reads, writes)

    def tt(self, eng, out, in0, in1, op, reads, writes):
        self.P.add(eng, lambda e: e.tensor_tensor(out=out, in0=in0, in1=in1, op=op), reads, writes)

    def ts(self, eng, out, in0, s1, s2, op0, op1, reads, writes):
        if s2 is None:
            self.P.add(eng, lambda e: e.tensor_scalar(out=out, in0=in0, scalar1=s1, scalar2=None, op0=op0), reads, writes)
        else:
            self.P.add(eng, lambda e: e.tensor_scalar(out=out, in0=in0, scalar1=s1, scalar2=s2, op0=op0, op1=op1), reads, writes)

    def cp(self, eng, out, in_, reads, writes):
        self.P.add(eng, lambda e: e.tensor_copy(out=out, in_=in_), reads, writes)

    def memset(self, eng, ap, val, writes):
        self.P.add(eng, lambda e: e.memset(ap, val), (), writes)

    def reduce(self, out, in_, reads, writes):
        self.P.add("dve", lambda e: e.tensor_reduce(out=out, in_=in_, axis=AX.X, op=ALU.add), reads, writes)

    def recip(self, out, in_, reads, writes):
        self.P.add("dve", lambda e: e.reciprocal(out=out, in_=in_), reads, writes)

    def vcol(self, name, i=0, n=1):
        o, w = self.vp.cols[name]
        return self.vecs[:, o + i:o + i + n]

    def bank(self):
        b = self.bank_rr % 8
        self.bank_rr += 1
        return self.ps[b], self.Tps[b]

    def load_w(self, dst3, src2, ncols, T_, c0=0, step=512):
        sv = src2.rearrange("(kc p) n -> p kc n", p=128)
        for a in range(0, ncols, step):
            b = min(ncols, a + step)
            self.P.dma("pool", dst3[:, :, a:b], sv[:, :, c0 + a:c0 + b], writes=[T_])

    def rmsnorm(self, xt, Txt, gname, h, Th, sq, Tsq, rstd, Trstd, N):
        Tsql = Tsq if isinstance(Tsq, list) else [Tsq]
        self.act(sq.rearrange("p c n -> p (c n)"), xt.rearrange("p c n -> p (c n)"), AF.Square, [Txt], Tsql)
        pb, Tpb = self.bank()
        for c in range(8):
            self.mm(pb[:, :N], self.ones_bf[:, :], sq[:, c, :], c == 0, c == 7, Tsql + [self.Tconst], [Tpb])
        self.act(rstd, pb[:, :N], AF.Ln, [Tpb], [Trstd], scale=1.0 / D, bias=EPS)
        self.act(rstd, rstd, AF.Exp, [Trstd], [Trstd], scale=-0.5)
        for c in range(8):
            self.stt("dve", h[:, c, :], xt[:, c, :], self.vcol(gname, c), rstd, ALU.mult, ALU.mult, [Txt, Trstd], [Th])

    def headnorm(self, pb, Tpb, gcol, out, Tout, sq, Tsq, rstd, Trstd, N, dim, ones_mat):
        self.act(sq, pb[:, :N], AF.Square, [Tpb], [Tsq])
        p2, Tp2 = self.bank()
        self.mm(p2[:, :N], ones_mat, sq, True, True, [Tsq, self.Tconst], [Tp2])
        self.act(rstd, p2[:, :N], AF.Ln, [Tp2], [Trstd], scale=1.0 / dim, bias=EPS)
        self.act(rstd, rstd, AF.Exp, [Trstd], [Trstd], scale=-0.5)
        self.stt("dve", out, pb[:, :N], gcol, rstd, ALU.mult, ALU.mult, [Tpb, Trstd], [Tout])

    def build(self, upto=None):
        nc = self.nc
        st = ExitStack()
        with st:
            def din(name, shape, dt=F32):
                return nc.dram_tensor(name, list(shape), dt, kind="ExternalInput").ap()
            self.xT = din("xT", [D, NT])
            self.pT = din("pT", [DEPTH, 256, NT])
            self.vecs_d = din("vecs", [128, self.vp.n])
            self.ssmF_d = [din("ssmF%d" % l, [128, 2560]) for l in range(N_A)]
            self.ssmS_d = [din("ssmS%d" % l, [128, 4288]) for l in range(N_A)]
            self.oh_d = din("t5oh", [2, 128, 4096])
            self.mask0_d = din("t5mask0", [128, 128])
            self.w = {}
            for nm, shp in (("ssm_w_in", [2, D, D]), ("ssm_w_glu", [2, D, 2 * D]), ("kv_w", [D, 2 * D]),
                            ("attn_w_q", [2, D, D]), ("attn_w_o", [2, D, D]), ("ffn_w_up", [4, D, 2 * DFF]),
                            ("ffn_w_down", [4, DFF, D]), ("ple_w_gate", [4, D, D]), ("ple_w_proj", [4, 256, D])):
                self.w[nm] = din(nm, shp)
            self.yT = nc.dram_tensor("yT", [D, NT], F32, kind="ExternalOutput").ap()
            self.XT = nc.dram_tensor("XTs", [D, NT], F32).ap()
            self.KTd = nc.dram_tensor("KTs", [8, 128, NT], BF16).ap()
            self.QTd = nc.dram_tensor("QTs", [8, 128, NT], BF16).ap()
            self.Vd = nc.dram_tensor("Vs", [NT, D], BF16).ap()

            NARENA = 52800
            arena_t = st.enter_context(nc.sbuf_tensor("arena", [128, NARENA], F32))
            self.A = Arena(arena_t[:, :], NARENA)
            self.ps = [st.enter_context(nc.psum_tensor("ps%d" % i, [128, 512], F32)) for i in range(8)]
            self.Tps = [T("ps%d" % i) for i in range(8)]
            self.P = Prog(nc)
            P = self.P
            A = self.A
            self.Tconst = T("const")
            self.vecs = A.f32(self.vp.n)
            P.dma("sp", self.vecs, self.vecs_d, writes=[self.Tconst])
            self.ones_bf = A.bf16(128)
            self.memset("pool", self.ones_bf, 1.0, [self.Tconst])
            self.blk_bf = A.bf16(128)
            self.memset("pool", self.blk_bf, 0.0, [self.Tconst])
            self.memset("pool", self.blk_bf[0:64, 0:64], 1.0, [self.Tconst])
            self.memset("pool", self.blk_bf[64:128, 64:128], 1.0, [self.Tconst])
            self.t5tab = A.f32(8 * 2 * 128)
            self.Tt5 = T("t5tab")
            self.lamcols = A.f32(8)
            self.Tlam = T("lam")
            self.persist_off = A.off

            stages = []
            for l in range(DEPTH):
                if l < N_A:
                    stages.append(("A0_%d" % l, lambda xi, xo, l=l: self.stage_A(l, xi, xo)))
                else:
                    if l == N_A:
                        stages.append(("T5", lambda xi, xo: self.stage_T5()))
                        stages.append(("KV", lambda xi, xo: self.stage_KV(xi)))
                    stages.append(("B_%d" % l, lambda xi, xo, l=l: self.stage_B(l, xi, xo)))
                stages.append(("FFN_%d" % l, lambda xi, xo, l=l: self.stage_FFN(l, xi, xo)))
                stages.append(("PLE_%d" % l, lambda xi, xo, l=l: self.stage_PLE(l, xi, xo)))
            if upto is not None:
                stages = stages[:upto]
            mod = [i for i, (n, f) in enumerate(stages) if not (n.startswith("T5") or n.startswith("KV"))]
            first, last = mod[0], mod[-1]
            for i, (n, f) in enumerate(stages):
                xi = self.xT if i <= first else self.XT
                xo = self.yT if i == last else self.XT
                if i > last:
                    xi = self.yT
                P.barrier()
                A.off = self.persist_off
                f(xi, xo)
            P.barrier()
            P.emit()
            print("ops", P.stats)
        return nc

    def xtile_view(self, xd, t):
        return xd.rearrange("(c p) n -> p c n", p=128)[:, :, t * TL:(t + 1) * TL]

    def stage_A(self, l, xin, xout):
        P, A = self.P, self.A
        coef = A.f32(64 * 11 * 3).rearrange("p (g k j) -> p g k j", k=11, j=3)
        U = A.bf16(8 * NT).rearrange("p (c n) -> p c n", n=NT)
        abF = A.f32(1024)
        bbF = A.f32(1024)
        pwS = A.f32(2 * 9 * 64).rearrange("p (t k g) -> p t k g", t=2, k=9)
        ccS = A.f32(1024).rearrange("p (g c) -> p g c", c=16)
        ccS2 = A.f32(1024).rearrange("p (g c) -> p g c", c=16)
        bbS = A.f32(1024).rearrange("p (g c) -> p g c", c=16)
        Ttab = T("ssmtab")
        Tcoef = T("coef")
        mark = A.off
        sF = A.f32(2560)
        TF = T("sF")
        P.dma("sp", sF, self.ssmF_d[l], writes=[TF])
        lre, lim, bre, bim, ldt = [sF[:, i * 512:(i + 1) * 512] for i in range(5)]
        tmpF = [A.f32(512) for _ in range(8)]
        Tt = [T("tF%d" % i) for i in range(8)]

        def trig(theta, Tth, lrdt, Tlr, out_r, out_i, To, tA, TA, tB, TB):
            self.ts("dve", tA, theta, 1.0 / TWO_PI, MAGIC, ALU.mult, ALU.add, [Tth], [TA])
            self.ts("dve", tA, tA, -MAGIC, None, ALU.add, None, [TA], [TA])
            self.stt("dve", tA, tA, -TWO_PI, theta, ALU.mult, ALU.add, [TA, Tth], [TA])
            self.ts("dve", tA, tA, 3.1415925, -3.1415925, ALU.min, ALU.max, [TA], [TA])
            self.act(out_i, tA, AF.Sin, [TA], [To])
            self.act(tB, tA, AF.Abs, [TA], [TB])
            self.act(out_r, tB, AF.Sin, [TB], [To], scale=-1.0, bias=math.pi / 2)
            self.act(tB, lrdt, AF.Exp, [Tlr], [TB])
            self.tt("dve", out_r, out_r, tB, ALU.mult, [To, TB], [To])
            self.tt("dve", out_i, out_i, tB, ALU.mult, [To, TB], [To])

        def fcoef(lr_, li_, Tl, a_r, a_i, Ta_, t5, T5_, t6, T6_, t7, T7_, t8, T8_, t9, T9_):
            self.tt("dve", t5, lr_, lr_, ALU.mult, [Tl], [T5_])
            self.tt("dve", t6, li_, li_, ALU.mult, [Tl], [T6_])
            self.tt("dve", t5, t5, t6, ALU.add, [T5_, T6_], [T5_])
            self.recip(t5, t5, [T5_], [T5_])
            self.ts("dve", t6, a_r, -1.0, None, ALU.add, None, [Ta_], [T6_])
            self.tt("dve", t7, t6, lr_, ALU.mult, [T6_, Tl], [T7_])
            self.tt("dve", t8, a_i, li_, ALU.mult, [Ta_, Tl], [T8_])
            self.tt("dve", t7, t7, t8, ALU.add, [T7_, T8_], [T7_])
            self.tt("dve", t7, t7, t5, ALU.mult, [T7_, T5_], [T7_])
            self.tt("dve", t8, a_i, lr_, ALU.mult, [Ta_, Tl], [T8_])
            self.tt("dve", t9, t6, li_, ALU.mult, [T6_, Tl], [T9_])
            self.tt("dve", t8, t8, t9, ALU.subtract, [T8_, T9_], [T8_])
            self.tt("dve", t8, t8, t5, ALU.mult, [T8_, T5_], [T8_])

        dt, lrdt, th, t5, t6, t7, t8, t9 = tmpF
        Tdt, Tlrdt, Tth, T5_, T6_, T7_, T8_, T9_ = Tt
        abr, abi = abF[:, 0:512], abF[:, 512:1024]
        bbr, bbi = bbF[:, 0:512], bbF[:, 512:1024]
        self.act(dt, ldt, AF.Exp, [TF], [Tdt])
        self.tt("dve", lrdt, lre, dt, ALU.mult, [TF, Tdt], [Tlrdt])
        self.tt("dve", th, lim, dt, ALU.mult, [TF, Tdt], [Tth])
        trig(th, Tth, lrdt, Tlrdt, abr, abi, Ttab, t5, T5_, t6, T6_)
        fcoef(lre, lim, TF, abr, abi, Ttab, t5, T5_, t6, T6_, t7, T7_, t8, T8_, t9, T9_)
        self.tt("dve", t5, t7, bre, ALU.mult, [T7_, TF], [T5_])
        self.tt("dve", t6, t8, bim, ALU.mult, [T8_, TF], [T6_])
        self.tt("dve", bbr, t5, t6, ALU.subtract, [T5_, T6_], [Ttab])
        self.tt("dve", t5, t7, bim, ALU.mult, [T7_, TF], [T5_])
        self.tt("dve", t6, t8, bre, ALU.mult, [T8_, TF], [T6_])
        self.tt("dve", bbi, t5, t6, ALU.add, [T5_, T6_], [Ttab])
        sS = A.f32(4288)
        TS = T("sS")
        P.dma("sp", sS, self.ssmS_d[l], writes=[TS])
        lreS, limS, ldtS = sS[:, 0:64], sS[:, 64:128], sS[:, 128:192]
        cc_in = sS[:, 192:1216].rearrange("p (g c) -> p g c", c=16)
        cc2_in = sS[:, 1216:2240].rearrange("p (g c) -> p g c", c=16)
        bSre = sS[:, 2240:3264].rearrange("p (g c) -> p g c", c=16)
        bSim = sS[:, 3264:4288].rearrange("p (g c) -> p g c", c=16)
        self.cp("dve", ccS, cc_in, [TS], [Ttab])
        self.cp("dve", ccS2, cc2_in, [TS], [Ttab])
        s_ = [A.f32(64) for _ in range(10)]
        Ts_ = [T("tS%d" % i) for i in range(10)]
        dtS, lrdtS, thS, u5, u6, u7, u8, u9, ar, ai = s_
        TdtS, TlrS, TthS, Tu5, Tu6, Tu7, Tu8, Tu9, Ta, _ = Ts_
        self.act(dtS, ldtS, AF.Exp, [TS], [TdtS])
        self.tt("dve", lrdtS, lreS, dtS, ALU.mult, [TS, TdtS], [TlrS])
        self.tt("dve", thS, limS, dtS, ALU.mult, [TS, TdtS], [TthS])
        trig(thS, TthS, lrdtS, TlrS, ar, ai, Ta, u5, Tu5, u6, Tu6)
        self.memset("dve", pwS[:, 0, 0, :], 1.0, [Ttab])
        self.memset("dve", pwS[:, 1, 0, :], 0.0, [Ttab])
        self.cp("dve", pwS[:, 0, 1, :], ar, [Ta], [Ttab])
        self.cp("dve", pwS[:, 1, 1, :], ai, [Ta], [Ttab])
        for k in range(1, 8):
            pr, pi = pwS[:, 0, k, :], pwS[:, 1, k, :]
            nr, ni = pwS[:, 0, k + 1, :], pwS[:, 1, k + 1, :]
            self.tt("dve", u5, pr, ar, ALU.mult, [Ttab, Ta], [Tu5])
            self.tt("dve", u6, pi, ai, ALU.mult, [Ttab, Ta], [Tu6])
            self.tt("dve", nr, u5, u6, ALU.subtract, [Tu5, Tu6], [Ttab])
            self.tt("dve", u5, pr, ai, ALU.mult, [Ttab, Ta], [Tu5])
            self.tt("dve", u6, pi, ar, ALU.mult, [Ttab, Ta], [Tu6])
            self.tt("dve", ni, u5, u6, ALU.add, [Tu5, Tu6], [Ttab])
        fcoef(lreS, limS, TS, ar, ai, Ta, u5, Tu5, u6, Tu6, u7, Tu7, u8, Tu8, u9, Tu9)
        fr_b = u7.unsqueeze(2).broadcast_to([128, 64, 16])
        fi_b = u8.unsqueeze(2).broadcast_to([128, 64, 16])
        w1 = A.f32(1024).rearrange("p (g c) -> p g c", c=16)
        w2 = A.f32(1024).rearrange("p (g c) -> p g c", c=16)
        w3 = A.f32(1024).rearrange("p (g c) -> p g c", c=16)
        Tw1, Tw2, Tw3 = T("w1"), T("w2"), T("w3")
        self.tt("dve", w1, bSre, fr_b, ALU.mult, [TS, Tu7], [Tw1])
        self.tt("dve", w2, bSim, fi_b, ALU.mult, [TS, Tu8], [Tw2])
        self.tt("dve", w3, w1, w2, ALU.subtract, [Tw1, Tw2], [Tw3])
        self.cp("dve", bbS[0:64], w3[0:64], [Tw3], [Ttab])
        self.tt("dve", w1, bSim, fr_b, ALU.mult, [TS, Tu7], [Tw1])
        self.tt("dve", w2, bSre, fi_b, ALU.mult, [TS, Tu8], [Tw2])
        self.tt("dve", w3, w1, w2, ALU.add, [Tw1, Tw2], [Tw3])
        self.cp("dve", bbS[64:128], w3[64:128], [Tw3], [Ttab])
        for k in range(11):
            self.cp("dve", coef[:, :, k, 0], ar, [Ta], [Tcoef])
            self.ts("dve", coef[:, :, k, 1], ai, self.vcol("sgn"), None, ALU.mult, None, [Ta, self.Tconst], [Tcoef])
            self.ts("dve", coef[:, :, k, 2], ai, self.vcol("nsgn"), None, ALU.mult, None, [Ta, self.Tconst], [Tcoef])
            if k < 10:
                self.tt("dve", u5, ar, ar, ALU.mult, [Ta], [Tu5])
                self.tt("dve", u6, ai, ai, ALU.mult, [Ta], [Tu6])
                self.tt("dve", u9, ar, ai, ALU.mult, [Ta], [Tu9])
                self.tt("dve", ar, u5, u6, ALU.subtract, [Tu5, Tu6], [Ta])
                self.ts("dve", ai, u9, 2.0, None, ALU.mult, None, [Tu9], [Ta])

        if getattr(self, "substop", 9) <= 0:
            return
        P.barrier()
        A.off = mark
        win = A.bf16(8 * D).rearrange("p (k n) -> p k n", n=D)
        Tw = T("win")
        self.load_w(win, self.w["ssm_w_in"][l], D, Tw)
        xb = [A.f32(8 * TL).rearrange("p (c n) -> p c n", n=TL) for _ in range(2)]
        Txb = [T("xb0"), T("xb1")]
        sq = A.bf16(8 * TL).rearrange("p (c n) -> p c n", n=TL)
        Tsq = T("sq")
        h = A.bf16(8 * TL).rearrange("p (c n) -> p c n", n=TL)
        Th = T("h")
        rstd = A.f32(TL)
        Trs = T("rstd")
        TU = [[T("U%d_%d" % (c, t)) for t in range(NTILE)] for c in range(8)]
        for t in range(NTILE):
            xt, Txt = xb[t % 2], Txb[t % 2]
            P.dma("sp", xt, self.xtile_view(xin, t), writes=[Txt])
            self.rmsnorm(xt, Txt, "ssm_norm%d" % l, h, Th, sq, Tsq, rstd, Trs, TL)
            for oc in range(8):
                pb, Tpb = self.bank()
                for kc in range(8):
                    self.mm(pb[:, :], win[:, kc, oc * 128:(oc + 1) * 128], h[:, kc, :], kc == 0, kc == 7, [Tw, Th], [Tpb])
                self.act(U[:, oc, t * TL:(t + 1) * TL], pb[:, :], AF.Copy, [Tpb], [TU[oc][t]])

        if getattr(self, "substop", 9) <= 1:
            return
        P.barrier()
        A.off = mark
        NG = 2
        NU = 8 // NG
        BtK = [A.bf16(8 * NG * 128).rearrange("p (s g m) -> p s g m", s=8, g=NG) for _ in range(2)]
        BtKs = [A.bf16(8 * NG * 128).rearrange("p (s g m) -> p s g m", s=8, g=NG) for _ in range(2)]
        TBt = [T("BtK0"), T("BtK1")]
        CAp = A.bf16(9 * 1152).rearrange("p (k x) -> p k x", x=1152)
        TCA = T("CAp")
        bbSp = A.bf16(1152)
        TbSp = T("bbSp")
        Klag = A.bf16(8 * 128).rearrange("p (k x) -> p k x", x=128)
        TK = T("Klag")
        Xall = [A.f32(NG * 512).rearrange("p (g n) -> p g n", n=512) for _ in range(2)]
        Xsall = [A.f32(NG * 512).rearrange("p (g n) -> p g n", n=512) for _ in range(2)]
        TXg = [[T("X%d_%d" % (b_, i)) for i in range(NG)] for b_ in range(2)]
        TXsg = [[T("Xs%d_%d" % (b_, i)) for i in range(NG)] for b_ in range(2)]
        Xp = A.bf16(8 * 512).rearrange("p (g n) -> p g n", n=512)
        TXp = [T("Xp%d" % i) for i in range(8)]
        Ysb = A.f32(NT)
        TY = T("Ysb")
        bk = [A.f32(128) for _ in range(2)]
        bks = A.f32(128)
        Tbk = [T("bk0"), T("bk1")]
        Tbks = T("bks")
        q5, q6 = A.f32(64), A.f32(64)
        Tq5, Tq6 = T("q5"), T("q6")
        tA = A.f32(128).rearrange("p (g c) -> p g c", c=16)
        tB = A.f32(128).rearrange("p (g c) -> p g c", c=16)
        TtA, TtB = T("tA"), T("tB")
        self.memset("pool", CAp.rearrange("p k x -> p (k x)"), 0.0, [TCA])
        self.memset("pool", bbSp, 0.0, [TbSp])
        self.memset("pool", Xp.rearrange("p g n -> p (g n)"), 0.0, TXp)
        abr3 = abr.rearrange("p (c m) -> p c m", m=64)
        abi3 = abi.rearrange("p (c m) -> p c m", m=64)
        bbr3 = bbr.rearrange("p (c m) -> p c m", m=64)
        bbi3 = bbi.rearrange("p (c m) -> p c m", m=64)
        nch = getattr(self, "nch", 8)

        def Ucv(ch):
            return U[:, ch, :].rearrange("p (q s) -> p s q", s=8)

        def chunk_tables(ch):
            for lag in range(9):
                prb = pwS[:, 0, lag, ch * 8:(ch + 1) * 8].unsqueeze(2).broadcast_to([128, 8, 16])
                pib = pwS[:, 1, lag, ch * 8:(ch + 1) * 8].unsqueeze(2).broadcast_to([128, 8, 16])
                self.tt("dve", tA, ccS[:, ch * 8:(ch + 1) * 8, :], prb, ALU.mult, [Ttab], [TtA])
                self.tt("dve", tB, ccS2[:, ch * 8:(ch + 1) * 8, :], pib, ALU.mult, [Ttab], [TtB])
                dv = CAp[:, lag, :].rearrange("p (g x) -> p g x", x=144)[:, :, 0:16]
                self.stt("dve", dv, tA, self.vcol("nsgn"), tB, ALU.mult, ALU.subtract, [TtA, TtB, self.Tconst], [TCA])
            self.cp("dve", bbSp.rearrange("p (g x) -> p g x", x=144)[:, :, 0:16], bbS[:, ch * 8:(ch + 1) * 8, :], [Ttab], [TbSp])
            for lag in range(8):
                pb, Tpb = self.bank()
                for g8 in range(8):
                    self.mm(pb[:, 0:128], bbSp[:, g8 * 128:(g8 + 1) * 128], CAp[:, lag, g8 * 128:(g8 + 1) * 128], g8 == 0, g8 == 7, [TbSp, TCA], [Tpb])
                self.act(Klag[:, lag, :], pb[:, 0:128], AF.Copy, [Tpb], [TK])

        def unit_tables(ch, un, bi):
            cur = 0
            self.cp("dve", bk[0][:, 0:64], bbr3[:, ch, :], [Ttab], [Tbk[0]])
            self.cp("dve", bk[0][:, 64:128], bbi3[:, ch, :], [Ttab], [Tbk[0]])
            for k in range(8):
                s_i = 7 - k
                b_, Tb_ = bk[cur], Tbk[cur]
                self.cp("dve", bks[:, 0:64], b_[:, 64:128], [Tb_], [Tbks])
                self.cp("dve", bks[:, 64:128], b_[:, 0:64], [Tb_], [Tbks])
                for gi in range(NG):
                    g8 = un * NG + gi
                    self.act(BtK[bi][:, s_i, gi, :], b_, AF.Copy, [Tb_, self.Tconst], [TBt[bi]], scale=self.vcol("mask8", g8))
                    self.ts("dve", BtKs[bi][:, s_i, gi, :], bks, self.vcol("mask8", g8), None, ALU.mult, None, [Tbks, self.Tconst], [TBt[bi]])
                if k < 7:
                    n_, Tn_ = bk[1 - cur], Tbk[1 - cur]
                    self.tt("dve", q5, b_[:, 0:64], abr3[:, ch, :], ALU.mult, [Tb_, Ttab], [Tq5])
                    self.tt("dve", q6, b_[:, 64:128], abi3[:, ch, :], ALU.mult, [Tb_, Ttab], [Tq6])
                    self.tt("dve", n_[:, 0:64], q5, q6, ALU.subtract, [Tq5, Tq6], [Tn_])
                    self.tt("dve", q5, b_[:, 0:64], abi3[:, ch, :], ALU.mult, [Tb_, Ttab], [Tq5])
                    self.tt("dve", q6, b_[:, 64:128], abr3[:, ch, :], ALU.mult, [Tb_, Ttab], [Tq6])
                    self.tt("dve", n_[:, 64:128], q5, q6, ALU.add, [Tq5, Tq6], [Tn_])
                    cur = 1 - cur

        def unit_V(ch, un, bi):
            Uc = Ucv(ch)
            for gi in range(NG):
                for (bt, dst, Td) in ((BtK[bi], Xall[bi], TXg[bi][gi]), (BtKs[bi], Xsall[bi], TXsg[bi][gi])):
                    pb, Tpb = self.bank()
                    for s_i in range(8):
                        self.mm(pb[:, :], bt[:, s_i, gi, :], Uc[:, s_i, :], s_i == 0, s_i == 7, [TBt[bi]] + TU[ch], [Tpb])
                    self.act(dst[:, gi, :], pb[:, :], AF.Copy, [Tpb], [Td])

        def unit_scan(ch, un, bi):
            def lvl(gi, tsl, ssl, q, k):
                g = ch * 8 + un * NG + gi
                p1 = coef[:, g, k + 3, 0:1]
                p2 = coef[:, g, k + 3, 1:2]
                p2s = coef[:, g, k + 3, 2:3]
                Xv = Xall[bi][:, gi, :].rearrange("p (s m q) -> p s m q", s=2, q=q)
                Xsv = Xsall[bi][:, gi, :].rearrange("p (s m q) -> p s m q", s=2, q=q)
                tX, sX = Xv[:, :, tsl[0], tsl[1]], Xv[:, :, ssl[0], ssl[1]]
                tXs, sXs = Xsv[:, :, tsl[0], tsl[1]], Xsv[:, :, ssl[0], ssl[1]]
                TX, TXs = TXg[bi][gi], TXsg[bi][gi]
                self.stt("dve", tX, sX, p1, tX, ALU.mult, ALU.add, [TX, Tcoef], [TX])
                self.stt("dve", tX, sXs, p2, tX, ALU.mult, ALU.add, [TX, TXs, Tcoef], [TX])
                self.stt("dve", tXs, sXs, p1, tXs, ALU.mult, ALU.add, [TXs, Tcoef], [TXs])
                self.stt("dve", tXs, sX, p2s, tXs, ALU.mult, ALU.add, [TX, TXs, Tcoef], [TXs])
            for k in range(8):
                blk, d = 2 << k, 1 << k
                for gi in range(NG):
                    lvl(gi, (slice(None), blk - 1), (slice(None), d - 1), blk, k)
            for k in range(6, -1, -1):
                blk, d = 2 << k, 1 << k
                M = 256 // blk
                for gi in range(NG):
                    lvl(gi, (slice(1, M), d - 1), (slice(0, M - 1), blk - 1), blk, k)
            for gi in range(NG):
                g8 = un * NG + gi
                self.act(Xp[:, g8, :].rearrange("p (s j) -> p s j", s=2)[:, :, 1:256],
                         Xall[bi][:, gi, :].rearrange("p (s j) -> p s j", s=2)[:, :, 0:255], AF.Copy, [TXg[bi][gi]], [TXp[g8]])

        def chunk_out(ch):
            Uc = Ucv(ch)
            Yv = Ysb.rearrange("p (q s) -> p s q", s=8)
            for r in range(8):
                pb, Tpb = self.bank()
                for g8 in range(8):
                    self.mm(pb[:, :], CAp[:, r + 1, g8 * 128:(g8 + 1) * 128], Xp[:, g8, :], g8 == 0, False, [TCA, TXp[g8]], [Tpb])
                for s_i in range(r + 1):
                    self.mm(pb[:, :], Klag[:, r - s_i, :], Uc[:, s_i, :], False, s_i == r, [TK] + TU[ch], [Tpb])
                self.act(Yv[:, r, :], pb[:, :], AF.Copy, [Tpb], [TY])
            Tuc = TU[ch]
            self.stt("dve", Ysb, U[:, ch, :], self.vcol("ssm_d%d" % l, ch), Ysb, ALU.mult, ALU.add, [TY] + Tuc, [TY])
            self.act(U[:, ch, :], Ysb, AF.Gelu, [TY], Tuc)

        units = [(ch, un) for ch in range(nch) for un in range(NU)]
        if units:
            chunk_tables(0)
            unit_tables(0, 0, 0)
            unit_V(0, 0, 0)
        for i, (ch, un) in enumerate(units):
            bi = i % 2
            if i + 1 < len(units):
                ch2, un2 = units[i + 1]
                unit_tables(ch2, un2, 1 - bi)
                unit_V(ch2, un2, 1 - bi)
            unit_scan(ch, un, bi)
            if un == NU - 1:
                chunk_out(ch)
                if ch + 1 < nch:
                    chunk_tables(ch + 1)

        if getattr(self, "substop", 9) <= 2:
            return
        P.barrier()
        A.off = mark
        wg = A.bf16(8 * 2 * D).rearrange("p (k n) -> p k n", n=2 * D)
        Twg = T("wglu")
        self.load_w(wg, self.w["ssm_w_glu"][l], 2 * D, Twg)
        xb = [A.f32(8 * TL).rearrange("p (c n) -> p c n", n=TL) for _ in range(2)]
        Txb = [T("xb0"), T("xb1")]
        sg = [A.f32(TL) for _ in range(2)]
        Tsg = [T("sg0"), T("sg1")]
        for t in range(NTILE):
            xt, Txt = xb[t % 2], Txb[t % 2]
            P.dma("sp", xt, self.xtile_view(xin, t), writes=[Txt])
            for oc in range(8):
                pv, Tpv = self.bank()
                pg, Tpg = self.bank()
                for kc in range(8):
                    self.mm(pv[:, :], wg[:, kc, oc * 128:(oc + 1) * 128], U[:, kc, t * TL:(t + 1) * TL], kc == 0, kc == 7, [Twg, TU[kc][t]], [Tpv])
                for kc in range(8):
                    self.mm(pg[:, :], wg[:, kc, D + oc * 128:D + (oc + 1) * 128], U[:, kc, t * TL:(t + 1) * TL], kc == 0, kc == 7, [Twg, TU[kc][t]], [Tpg])
                s_, Ts2 = sg[oc % 2], Tsg[oc % 2]
                self.act(s_, pg[:, :], AF.Sigmoid, [Tpg], [Ts2])
                self.tt("dve", s_, pv[:, :], s_, ALU.mult, [Tpv, Ts2], [Ts2])
                self.tt("dve", xt[:, oc, :], xt[:, oc, :], s_, ALU.add, [Txt, Ts2], [Txt])
            P.dma("sp", self.xtile_view(xout, t), xt, reads=[Txt])

    def stage_FFN(self, l, xin, xout):
        P, A = self.P, self.A
        wup = A.bf16(8 * 2 * DFF).rearrange("p (k n) -> p k n", n=2 * DFF)
        Twu = [T("wup%d" % j) for j in range(11)]
        wsrc = self.w["ffn_w_up"][l].rearrange("(kc p) n -> p kc n", p=128)
        for j in range(11):
            P.dma("pool", wup[:, :, j * 256:(j + 1) * 256], wsrc[:, :, j * 256:(j + 1) * 256], writes=[Twu[j]])
            P.dma("pool", wup[:, :, DFF + j * 256:DFF + (j + 1) * 256], wsrc[:, :, DFF + j * 256:DFF + (j + 1) * 256], writes=[Twu[j]])
        dsrc = self.w["ffn_w_down"][l].rearrange("(kc p) n -> p kc n", p=128)
        NR = 3
        wdr = [A.bf16(NFC * 128).rearrange("p (k n) -> p k n", n=128) for _ in range(NR)]
        Twd = [T("wdr%d" % i) for i in range(NR)]
        xb = [A.f32(8 * TL).rearrange("p (c n) -> p c n", n=TL) for _ in range(2)]
        Txb = [T("xb0"), T("xb1")]
        h = A.bf16(8 * TL).rearrange("p (c n) -> p c n", n=TL)
        Th = T("h")
        rstd = A.f32(TL)
        Trs = T("rstd")
        abuf = A.bf16(NFC * TL).rearrange("p (k n) -> p k n", n=TL)
        Ta = [T("a%d" % k) for k in range(NFC)]
        ub = [A.f32(TL + 2) for _ in range(4)]
        Tub = [T("ub%d" % i) for i in range(4)]
        tball = A.f32(4 * TL)
        tb = [tball[:, i * TL:(i + 1) * TL] for i in range(4)]
        Ttb = [T("tb%d" % i) for i in range(4)]
        sqh = tball.bitcast(BF16).rearrange("p (c n) -> p c n", n=TL)
        halo = A.f32(44 * 2).rearrange("p (v c) -> p v c", c=2)
        Thalo = [T("halo%d" % v) for v in range(44)]
        cw = lambda tap, vc: self.vcol("conv_w%d" % l, tap * 44 + vc)
        cb = lambda vc: self.vcol("conv_b%d" % l, vc)
        cnt = 0
        dcnt = 0
        for t in range(NTILE):
            xt, Txt = xb[t % 2], Txb[t % 2]
            P.dma("sp", xt, self.xtile_view(xin, t), writes=[Txt])
            self.rmsnorm(xt, Txt, "ffn_norm%d" % l, h, Th, sqh, list(Ttb), rstd, Trs, TL)
            for k in range(NFC):
                res = []
                for part in range(2):
                    vc = part * NFC + k
                    col0 = part * DFF + k * 128
                    u, Tu = ub[cnt % 4], Tub[cnt % 4]
                    tt_, Ttt = tb[cnt % 4], Ttb[cnt % 4]
                    cnt += 1
                    pb, Tpb = self.bank()
                    for kc in range(8):
                        self.mm(pb[:, :], wup[:, kc, col0:col0 + 128], h[:, kc, :], kc == 0, kc == 7, [Twu[k // 2], Th], [Tpb])
                    if t % 4 == 0:
                        self.memset("dve", u[:, 0:2], 0.0, [Tu])
                    else:
                        self.act(u[:, 0:2], halo[:, vc, :], AF.Copy, [Thalo[vc]], [Tu])
                    self.act(u[:, 2:TL + 2], pb[:, :], AF.Copy, [Tpb], [Tu])
                    self.act(halo[:, vc, :], u[:, TL:TL + 2], AF.Copy, [Tu], [Thalo[vc]])
                    self.act(tt_, u[:, 2:TL + 2], AF.Identity, [Tu, self.Tconst], [Ttt], scale=cw(2, vc), bias=cb(vc))
                    self.stt("dve", tt_, u[:, 1:TL + 1], cw(1, vc), tt_, ALU.mult, ALU.add, [Tu, Ttt], [Ttt])
                    self.stt("dve", tt_, u[:, 0:TL], cw(0, vc), tt_, ALU.mult, ALU.add, [Tu, Ttt], [Ttt])
                    res.append((tt_, Ttt))
                (tg, Ttg), (tv, Ttv) = res
                self.act(tg, tg, AF.Gelu, [Ttg], [Ttg])
                self.tt("dve", abuf[:, k, :], tg, tv, ALU.mult, [Ttg, Ttv], [Ta[k]])
            for oc in range(8):
                wd, Tw_ = wdr[dcnt % NR], Twd[dcnt % NR]
                dcnt += 1
                P.dma("pool", wd, dsrc[:, :, oc * 128:(oc + 1) * 128], writes=[Tw_])
                pb, Tpb = self.bank()
                for k in range(NFC):
                    self.mm(pb[:, :], wd[:, k, :], abuf[:, k, :], k == 0, k == NFC - 1, [Tw_, Ta[k]], [Tpb])
                self.tt("dve", xt[:, oc, :], xt[:, oc, :], pb[:, :], ALU.add, [Txt, Tpb], [Txt])
            P.dma("sp", self.xtile_view(xout, t), xt, reads=[Txt])

    def stage_PLE(self, l, xin, xout):
        P, A = self.P, self.A
        wgt = A.bf16(8 * D).rearrange("p (k n) -> p k n", n=D)
        wpj = A.bf16(2 * D).rearrange("p (k n) -> p k n", n=D)
        Tw = T("wple")
        self.load_w(wgt, self.w["ple_w_gate"][l], D, Tw)
        self.load_w(wpj, self.w["ple_w_proj"][l], D, Tw)
        xb = [A.f32(8 * TL).rearrange("p (c n) -> p c n", n=TL) for _ in range(2)]
        Txb = [T("xb0"), T("xb1")]
        pb_ = [A.bf16(2 * TL).rearrange("p (c n) -> p c n", n=TL) for _ in range(2)]
        Tpp = [T("pp0"), T("pp1")]
        sq = A.bf16(8 * TL).rearrange("p (c n) -> p c n", n=TL)
        Tsq = T("sq")
        h = A.bf16(8 * TL).rearrange("p (c n) -> p c n", n=TL)
        Th = T("h")
        rstd = A.f32(TL)
        Trs = T("rstd")
        sg = [A.f32(TL) for _ in range(2)]
        Tsg = [T("sg0"), T("sg1")]
        psrc = self.pT[l].rearrange("(c p) n -> p c n", p=128)
        for t in range(NTILE):
            xt, Txt = xb[t % 2], Txb[t % 2]
            pp, Tp = pb_[t % 2], Tpp[t % 2]
            P.dma("sp", xt, self.xtile_view(xin, t), writes=[Txt])
            P.dma("pool", pp, psrc[:, :, t * TL:(t + 1) * TL], writes=[Tp])
            self.rmsnorm(xt, Txt, "ple_norm%d" % l, h, Th, sq, Tsq, rstd, Trs, TL)
            for oc in range(8):
                pg, Tpg = self.bank()
                pv, Tpv = self.bank()
                for kc in range(8):
                    self.mm(pg[:, :], wgt[:, kc, oc * 128:(oc + 1) * 128], h[:, kc, :], kc == 0, kc == 7, [Tw, Th], [Tpg])
                for kc in range(2):
                    self.mm(pv[:, :], wpj[:, kc, oc * 128:(oc + 1) * 128], pp[:, kc, :], kc == 0, kc == 1, [Tw, Tp], [Tpv])
                s_, Ts2 = sg[oc % 2], Tsg[oc % 2]
                self.act(s_, pg[:, :], AF.Sigmoid, [Tpg], [Ts2])
                self.tt("dve", s_, pv[:, :], s_, ALU.mult, [Tpv, Ts2], [Ts2])
                self.tt("dve", xt[:, oc, :], xt[:, oc, :], s_, ALU.add, [Txt, Ts2], [Txt])
            P.dma("sp", self.xtile_view(xout, t), xt, reads=[Txt])

    def stage_T5(self):
        P, A = self.P, self.A
        oh = [A.f32(4096) for _ in range(2)]
        Toh = T("oh")
        for dd in range(2):
            P.dma("sp", oh[dd], self.oh_d[dd], writes=[Toh])
        m0 = A.f32(128)
        P.dma("sp", m0, self.mask0_d, writes=[Toh])
        tmp = A.f32(4096)
        Ttmp = T("t5tmp")
        tab = self.t5tab.rearrange("p (h d q) -> p h d q", d=2, q=128)
        o, _ = self.vp.cols["rbT"]
        for hh in range(8):
            rb = self.vecs[:, o + hh * 32:o + (hh + 1) * 32]
            rbb = rb.unsqueeze(1).broadcast_to([128, 128, 32])
            for dd in range(2):
                self.tt("dve", tmp.rearrange("p (q b) -> p q b", b=32), oh[dd].rearrange("p (q b) -> p q b", b=32), rbb, ALU.mult,
                        [Toh, self.Tconst], [Ttmp])
                self.reduce(tab[:, hh, dd, :], tmp.rearrange("p (q b) -> p q b", b=32), [Ttmp], [self.Tt5])
                self.ts("dve", tab[:, hh, dd, :], tab[:, hh, dd, :], self.vecs[:, o + hh * 32 + 31:o + hh * 32 + 32], None, ALU.subtract, None,
                        [self.Tt5, self.Tconst], [self.Tt5])
            self.tt("dve", tab[:, hh, 0, :], tab[:, hh, 0, :], m0, ALU.add, [self.Tt5, Toh], [self.Tt5])

    def stage_KV(self, xin):
        P, A = self.P, self.A
        wkv = A.bf16(8 * 2 * D).rearrange("p (k n) -> p k n", n=2 * D)
        Tw = T("wkv")
        self.load_w(wkv, self.w["kv_w"], 2 * D, Tw)
        xb = [A.f32(8 * TL).rearrange("p (c n) -> p c n", n=TL) for _ in range(2)]
        Txb = [T("xb0"), T("xb1")]
        sq = A.bf16(8 * TL).rearrange("p (c n) -> p c n", n=TL)
        Tsq = T("sq")
        h = A.bf16(8 * TL).rearrange("p (c n) -> p c n", n=TL)
        Th = T("h")
        rstd = A.f32(TL)
        Trs = T("rstd")
        sq2 = [A.bf16(TL) for _ in range(2)]
        Tsq2 = [T("sq2a"), T("sq2b")]
        rs2 = [A.f32(TL) for _ in range(2)]
        Trs2 = [T("rs2a"), T("rs2b")]
        kb_ = [A.bf16(8 * TL).rearrange("p (c n) -> p c n", n=TL) for _ in range(2)]
        Tkb = [T("kb0"), T("kb1")]
        vb_ = [A.bf16(4 * D).rearrange("p (b n) -> p b n", n=D) for _ in range(2)]
        Tvb = [T("vb0"), T("vb1")]
        for t in range(NTILE):
            xt, Txt = xb[t % 2], Txb[t % 2]
            kb, Tk = kb_[t % 2], Tkb[t % 2]
            vb, Tv = vb_[t % 2], Tvb[t % 2]
            P.dma("sp", xt, self.xtile_view(xin, t), writes=[Txt])
            self.rmsnorm(xt, Txt, "kv_norm", h, Th, sq, Tsq, rstd, Trs, TL)
            for hh in range(8):
                pb, Tpb = self.bank()
                for kc in range(8):
                    self.mm(pb[:, :], wkv[:, kc, hh * 128:(hh + 1) * 128], h[:, kc, :], kc == 0, kc == 7, [Tw, Th], [Tpb])
                self.headnorm(pb, Tpb, self.vcol("k_norm"), kb[:, hh, :], Tk, sq2[hh % 2], Tsq2[hh % 2], rs2[hh % 2], Trs2[hh % 2], TL, 64, self.blk_bf[:, :])
            P.dma("sp", self.KTd.rearrange("h p n -> p h n")[:, :, t * TL:(t + 1) * TL], kb, reads=[Tk])
            for tb in range(4):
                for half in range(2):
                    pb, Tpb = self.bank()
                    for kc in range(8):
                        self.mm(pb[:, :], h[:, kc, tb * 128:(tb + 1) * 128], wkv[:, kc, D + half * 512:D + (half + 1) * 512], kc == 0, kc == 7, [Tw, Th], [Tpb])
                    self.act(vb[:, tb, half * 512:(half + 1) * 512], pb[:, :], AF.Copy, [Tpb], [Tv])
            P.dma("sp", self.Vd[t * TL:(t + 1) * TL, :].rearrange("(b p) n -> p b n", p=128), vb, reads=[Tv])

    def stage_B(self, l, xin, xout):
        P, A = self.P, self.A
        j = l - N_A
        lam_init = 0.8 - 0.6 * math.exp(-0.3 * l)
        mark = A.off
        lt = A.f32(64)
        Tlt = T("lt")
        la = A.f32(4)
        Tla = T("la")
        for i, (a, b) in enumerate((("lambda_q1", "lambda_k1"), ("lambda_q2", "lambda_k2"))):
            oa, _ = self.vp.cols["%s%d" % (a, j)]
            ob, _ = self.vp.cols["%s%d" % (b, j)]
            self.tt("dve", lt, self.vecs[:, oa:oa + 64], self.vecs[:, ob:ob + 64], ALU.mult, [self.Tconst], [Tlt])
            self.reduce(la[:, i:i + 1], lt, [Tlt], [Tla])
        self.act(la[:, 0:2], la[:, 0:2], AF.Exp, [Tla], [Tla])
        neglam = self.lamcols[:, 0:1]
        gq = self.lamcols[:, 1:2]
        gsub = self.lamcols[:, 2:3]
        self.tt("dve", la[:, 2:3], la[:, 1:2], la[:, 0:1], ALU.subtract, [Tla], [Tla])
        self.ts("dve", neglam, la[:, 2:3], -lam_init, None, ALU.add, None, [Tla], [self.Tlam])
        self.ts("dve", gq, self.vcol("q_norm%d" % j), 0.125, None, ALU.mult, None, [self.Tconst], [self.Tlam])
        self.ts("dve", gsub, self.vcol("subln%d" % j), 1.0 - lam_init, None, ALU.mult, None, [self.Tconst], [self.Tlam])

        wq = A.bf16(8 * D).rearrange("p (k n) -> p k n", n=D)
        Tw = T("wq")
        self.load_w(wq, self.w["attn_w_q"][j], D, Tw)
        xb = [A.f32(8 * TL).rearrange("p (c n) -> p c n", n=TL) for _ in range(2)]
        Txb = [T("xb0"), T("xb1")]
        sq = A.bf16(8 * TL).rearrange("p (c n) -> p c n", n=TL)
        Tsq = T("sq")
        h = A.bf16(8 * TL).rearrange("p (c n) -> p c n", n=TL)
        Th = T("h")
        rstd = A.f32(TL)
        Trs = T("rstd")
        sq2 = [A.bf16(TL) for _ in range(2)]
        Tsq2 = [T("sq2a"), T("sq2b")]
        rs2 = [A.f32(TL) for _ in range(2)]
        Trs2 = [T("rs2a"), T("rs2b")]
        qb_ = [A.bf16(8 * TL).rearrange("p (c n) -> p c n", n=TL) for _ in range(2)]
        Tqb = [T("qb0"), T("qb1")]
        for t in range(NTILE):
            xt, Txt = xb[t % 2], Txb[t % 2]
            qb, Tq = qb_[t % 2], Tqb[t % 2]
            P.dma("sp", xt, self.xtile_view(xin, t), writes=[Txt])
            self.rmsnorm(xt, Txt, "attn_norm%d" % j, h, Th, sq, Tsq, rstd, Trs, TL)
            for hh in range(8):
                pb, Tpb = self.bank()
                for kc in range(8):
                    self.mm(pb[:, :], wq[:, kc, hh * 128:(hh + 1) * 128], h[:, kc, :], kc == 0, kc == 7, [Tw, Th], [Tpb])
                self.headnorm(pb, Tpb, gq, qb[:, hh, :], Tq, sq2[hh % 2], Tsq2[hh % 2], rs2[hh % 2], Trs2[hh % 2], TL, 64, self.blk_bf[:, :])
            P.dma("sp", self.QTd.rearrange("h p n -> p h n")[:, :, t * TL:(t + 1) * TL], qb, reads=[Tq, self.Tlam])

        P.barrier()
        A.off = mark
        wo = A.bf16(8 * D).rearrange("p (k n) -> p k n", n=D)
        Two = T("wo")
        self.load_w(wo, self.w["attn_w_o"][j], D, Two)
        ON = A.bf16(8 * SEQ).rearrange("p (h n) -> p h n", n=SEQ)
        TON = [[T("on%d_%d" % (hh, qt)) for qt in range(4)] for hh in range(8)]
        kq = [[A.bf16(SEQ), A.bf16(SEQ), A.bf16(16 * 128).rearrange("p (b e) -> p b e", e=128)] for _ in range(2)]
        Tkq = [T("kq0"), T("kq1")]
        NE = 4
        Eb = [A.bf16(TL) for _ in range(NE)]
        TE = [T("E%d" % i) for i in range(NE)]
        fin = [[A.f32(TL) for _ in range(4)] for _ in range(2)]
        Tfin = [T("fin0"), T("fin1")]
        sqs = A.bf16(TL)
        Tsqs = T("sqs")
        rss = A.f32(TL)
        Trss = T("rss")
        xt = A.f32(8 * TL).rearrange("p (c n) -> p c n", n=TL)
        Txt = T("xt")
        tab = self.t5tab.rearrange("p (h d q) -> p h (d q)", d=2, q=128)
        orb, _ = self.vp.cols["rbT"]
        accb = [0, 1, 2, 3]
        scb = [4, 5, 6, 7]
        it = 0
        hcount = 0
        for s in range(2):
            c0 = s * SEQ
            for hh in range(8):
                Kt, Qt, Vt = kq[hcount % 2]
                Tk = Tkq[hcount % 2]
                hcount += 1
                P.dma("sp", Kt, self.KTd[hh, :, c0:c0 + SEQ], writes=[Tk])
                P.dma("sp", Qt, self.QTd[hh, :, c0:c0 + SEQ], writes=[Tk])
                P.dma("sp", Vt, self.Vd[c0:c0 + SEQ, hh * 128:(hh + 1) * 128].rearrange("(b p) e -> p b e", p=128), writes=[Tk])
                b31 = self.vecs[:, orb + hh * 32 + 31:orb + hh * 32 + 32]
                for qt in range(4):
                    nkb = 4 * qt + 4
                    items = [(kb, c) for kb in range(nkb) for c in range(2)]
                    pend = []

                    def do_pv(item):
                        kb, c, ei, n_lo = item
                        E, TE_ = Eb[ei], TE[ei]
                        N = TL - n_lo
                        ob, sb = accb[c], accb[2 + c]
                        self.mm(self.ps[ob][:, n_lo:TL], Vt[:, kb, :], E[:, :N], kb == 0, kb == nkb - 1, [Tk, TE_], [self.Tps[ob]])
                        self.mm(self.ps[sb][:, n_lo:TL], self.ones_bf[:, :], E[:, :N], kb == 0, kb == nkb - 1, [self.Tconst, TE_], [self.Tps[sb]])

                    for (kb, c) in items:
                        n_lo = max(0, kb * 128 - qt * TL)
                        N = TL - n_lo
                        sbk = scb[it % 4]
                        ei = it % NE
                        it += 1
                        pb, Tpb = self.ps[sbk], self.Tps[sbk]
                        q0 = qt * TL + n_lo
                        self.mm(pb[:, :N], Kt[64 * c:64 * c + 64, kb * 128:(kb + 1) * 128], Qt[64 * c:64 * c + 64, q0:q0 + N], True, True, [Tk], [Tpb])
                        dblk0 = (q0 // 128) - kb
                        if dblk0 == 0:
                            w_ = min(N, 256)
                            self.tt("dve", pb[:, 0:w_], pb[:, 0:w_], tab[:, hh, 0:w_], ALU.add, [Tpb, self.Tt5], [Tpb])
                        elif dblk0 == 1:
                            self.tt("dve", pb[:, 0:128], pb[:, 0:128], tab[:, hh, 128:256], ALU.add, [Tpb, self.Tt5], [Tpb])
                        self.act(Eb[ei][:, :N], pb[:, :N], AF.Exp, [Tpb, self.Tconst], [TE[ei]], bias=b31)
                        pend.append((kb, c, ei, n_lo))
                        if len(pend) > 2:
                            do_pv(pend.pop(0))
                    while pend:
                        do_pv(pend.pop(0))
                    f = fin[qt % 2]
                    Tf = Tfin[qt % 2]
                    r1, r2, o1, o2 = f
                    self.recip(r1, self.ps[accb[2]][:, :], [self.Tps[accb[2]]], [Tf])
                    self.recip(r2, self.ps[accb[3]][:, :], [self.Tps[accb[3]]], [Tf])
                    self.tt("dve", o1, self.ps[accb[0]][:, :], r1, ALU.mult, [self.Tps[accb[0]], Tf], [Tf])
                    self.tt("dve", o2, self.ps[accb[1]][:, :], r2, ALU.mult, [self.Tps[accb[1]], Tf], [Tf])
                    self.stt("dve", o1, o2, neglam, o1, ALU.mult, ALU.add, [Tf, self.Tlam], [Tf])
                    self.act(sqs, o1, AF.Square, [Tf], [Tsqs])
                    sbk = scb[it % 4]
                    it += 1
                    pb, Tpb = self.ps[sbk], self.Tps[sbk]
                    self.mm(pb[:, :], self.ones_bf[:, :], sqs, True, True, [Tsqs, self.Tconst], [Tpb])
                    self.act(rss, pb[:, :], AF.Ln, [Tpb], [Trss], scale=1.0 / 128, bias=EPS)
                    self.act(rss, rss, AF.Exp, [Trss], [Trss], scale=-0.5)
                    self.stt("dve", ON[:, hh, qt * TL:(qt + 1) * TL], o1, gsub, rss, ALU.mult, ALU.mult, [Tf, Trss, self.Tlam], [TON[hh][qt]])
            for qt in range(4):
                t = s * 4 + qt
                P.dma("sp", xt, self.xtile_view(xin, t), writes=[Txt])
                for oc in range(8):
                    pb, Tpb = self.bank()
                    for hh in range(8):
                        self.mm(pb[:, :], wo[:, hh, oc * 128:(oc + 1) * 128], ON[:, hh, qt * TL:(qt + 1) * TL], hh == 0, hh == 7, [Two, TON[hh][qt]], [Tpb])
                    self.tt("dve", xt[:, oc, :], xt[:, oc, :], pb[:, :], ALU.add, [Txt, Tpb], [Txt])
                P.dma("sp", self.xtile_view(xout, t), xt, reads=[Txt])


def make_inputs(inp, core, vp, vecs, ssm, t5):
    x = np.asarray(inp["x"], np.float32)[2 * core:2 * core + 2].reshape(NT, D)
    p = np.asarray(inp["p"], np.float32)[:, 2 * core:2 * core + 2].reshape(DEPTH, NT, 256)
    m = {"xT": np.ascontiguousarray(x.T), "pT": np.ascontiguousarray(p.transpose(0, 2, 1)), "vecs": vecs,
         "t5oh": t5[0], "t5mask0": t5[1]}
    for l in range(N_A):
        m["ssmF%d" % l] = ssm[l][0]
        m["ssmS%d" % l] = ssm[l][1]
    for nm in ("ssm_w_in", "ssm_w_glu", "kv_w", "attn_w_q", "attn_w_o", "ffn_w_up", "ffn_w_down", "ple_w_gate", "ple_w_proj"):
        m[nm] = np.ascontiguousarray(inp[nm], dtype=np.float32)
    return m


def kernel(**inputs):
    inp = {k: np.asarray(v) for k, v in inputs.items()}
    vp = vec_layout(inp)
    vecs = vp.build()
    ssm = [ssm_layouts(inp, l) for l in range(N_A)]
    t5 = t5_consts()
    nc = Builder(vp).build()
    ncores = 8
    in_maps = [make_inputs(inp, c, vp, vecs, ssm, t5) for c in range(ncores)]
    res = run_bass_kernel_spmd(nc, in_maps, core_ids=list(range(ncores)))
    outs = []
    for c in range(ncores):
        yT = np.asarray(res.results[c]["yT"])
        outs.append(yT.T.reshape(2, SEQ, D))
    return np.ascontiguousarray(np.concatenate(outs, axis=0).astype(np.float32))
```
